# Optimizing a Trainium2 kernel written in Bass

```python
import math
import jax, jax.numpy as jnp
from jax import lax
import numpy as np

D_MODEL = 1024
BATCH = 8
SEQ = 2048
DEPTH = 2
DEC_BATCH = 128
DEC_SEQ = 8
PAST_LEN = 16384
PAGE_SIZE = 128

N_MIXERS = 2
N_POOL_LAYERS = (DEPTH + 1) // 2
N_SSM_LAYERS = DEPTH // 2
POOL_WINDOWS = (2, 4, 8, 16)
POOL_GROUPS = len(POOL_WINDOWS)
POOL_GC = D_MODEL // POOL_GROUPS
POOL_MAX = 16
POOL_STATE = POOL_MAX - 1
SSM_GC = 16
SSM_GROUPS = D_MODEL // SSM_GC
SSM_P = 64
FFN_HIDDEN = 2816
FFN_UP = 2 * FFN_HIDDEN
CONV_W = 3
EPS = 1e-6

kernel_name = "hybrid_pool_s5_convffn_step"


def rmsnorm(x, g):
    xf = x.astype(jnp.float32)
    return xf * lax.rsqrt(jnp.mean(xf * xf, axis=-1, keepdims=True) + EPS) * g.astype(jnp.float32)


def ada_mod(c, w, b):
    m = jax.nn.silu(c.astype(jnp.float32)) @ w.astype(jnp.float32) + b.astype(jnp.float32)
    return jnp.split(m, 6, axis=-1)


def pre_mod(x, g, shift, scale):
    return (rmsnorm(x, g) * (1.0 + scale[:, None, :]) + shift[:, None, :]).astype(x.dtype)


def gated_residual(x, m, g, gate):
    return x + (gate[:, None, :] * rmsnorm(m, g)).astype(x.dtype)


def pool_mix(h_ext, n_prev, pos0, w_pool, pool_scale):
    Bsz, Le, D = h_ext.shape
    L = Le - n_prev
    hf = h_ext.astype(jnp.float32)
    cs = jnp.cumsum(jnp.pad(hf, ((0, 0), (POOL_MAX, 0), (0, 0))), axis=1)
    pos = pos0 + jnp.arange(n_prev, Le)
    parts = []
    for g, w in enumerate(POOL_WINDOWS):
        sl = slice(g * POOL_GC, (g + 1) * POOL_GC)
        hi = cs[:, POOL_MAX + n_prev:POOL_MAX + Le, sl]
        lo = cs[:, POOL_MAX + n_prev - w:POOL_MAX + Le - w, sl]
        cnt = jnp.minimum(pos + 1, w).astype(jnp.float32)[None, :, None]
        parts.append((hi - lo) / cnt - hf[:, n_prev:, sl])
    pooled = jnp.stack(parts, axis=2)
    y = jnp.einsum('blgc,gcd->blgd', pooled, w_pool.astype(jnp.float32)).reshape(Bsz, L, D)
    return y * pool_scale.astype(jnp.float32)


def _cmul_combine(e1, e2):
    a1r, a1i, b1r, b1i = e1
    a2r, a2i, b2r, b2i = e2
    ar = a2r * a1r - a2i * a1i
    ai = a2r * a1i + a2i * a1r
    br = a2r * b1r - a2i * b1i + b2r
    bi = a2r * b1i + a2i * b1r + b2i
    return (ar, ai, br, bi)


def s5_mix(u, x0_re, x0_im, A_re, A_im, log_dt, B_re, B_im, C_re, C_im, D_skip, w_glu_a, w_glu_b):
    f32 = jnp.float32
    Bsz, L, D = u.shape
    uf = u.astype(f32).reshape(Bsz, L, SSM_GROUPS, SSM_GC)
    A_re = A_re.astype(f32); A_im = A_im.astype(f32)
    dt = jnp.exp(log_dt.astype(f32))[:, None]
    mag = jnp.exp(A_re * dt); ang = A_im * dt
    lb_re = mag * jnp.cos(ang); lb_im = mag * jnp.sin(ang)
    n_re = lb_re - 1.0; n_im = lb_im
    den = A_re * A_re + A_im * A_im
    f_re = (n_re * A_re + n_im * A_im) / den
    f_im = (n_im * A_re - n_re * A_im) / den
    B_re = B_re.astype(f32); B_im = B_im.astype(f32)
    Bb_re = f_re[..., None] * B_re - f_im[..., None] * B_im
    Bb_im = f_re[..., None] * B_im + f_im[..., None] * B_re
    bu_re = jnp.einsum('blgc,gpc->blgp', uf, Bb_re)
    bu_im = jnp.einsum('blgc,gpc->blgp', uf, Bb_im)
    x0_re = x0_re.astype(f32); x0_im = x0_im.astype(f32)
    bu_re = bu_re.at[:, 0].add(lb_re * x0_re - lb_im * x0_im)
    bu_im = bu_im.at[:, 0].add(lb_re * x0_im + lb_im * x0_re)
    a_re = jnp.broadcast_to(lb_re, bu_re.shape)
    a_im = jnp.broadcast_to(lb_im, bu_im.shape)
    _, _, s_re, s_im = lax.associative_scan(_cmul_combine, (a_re, a_im, bu_re, bu_im), axis=1)
    y = (jnp.einsum('blgp,gcp->blgc', s_re, C_re.astype(f32))
         - jnp.einsum('blgp,gcp->blgc', s_im, C_im.astype(f32))
         + D_skip.astype(f32).reshape(SSM_GROUPS, SSM_GC) * uf).reshape(Bsz, L, D)
    g = jax.nn.gelu(y, approximate=False)
    out = (g @ w_glu_a.astype(f32)) * jax.nn.sigmoid(g @ w_glu_b.astype(f32))
    return out, s_re[:, -1], s_im[:, -1]


def conv_ffn(h, conv_prev, w_up, conv_w, conv_b, w_down):
    L = h.shape[1]
    up = h @ w_up
    ext = jnp.concatenate([conv_prev.astype(up.dtype), up], axis=1)
    conv = conv_b
    for k in range(CONV_W):
        conv = conv + ext[:, k:k + L] * conv_w[k]
    gate, val = jnp.split(conv, 2, axis=-1)
    out = (jax.nn.gelu(gate, approximate=False) * val) @ w_down
    return out, ext[:, -(CONV_W - 1):]


def setup_inputs(seed: int = 0) -> dict:
    key = jax.random.key(seed)
    ks = jax.random.split(key, 32)
    nrm = jax.random.normal
    f32 = jnp.float32
    D = D_MODEL
    d = {}
    d["x_prompt"] = nrm(ks[0], (BATCH, SEQ, D), f32)
    d["x_sample"] = nrm(ks[1], (DEC_BATCH, DEC_SEQ, D), f32)
    d["c_prompt"] = nrm(ks[2], (BATCH, D), f32)
    d["c_sample"] = nrm(ks[3], (DEC_BATCH, D), f32)
    d["state_pool"] = nrm(ks[4], (N_POOL_LAYERS, DEC_BATCH, POOL_STATE, D), f32)
    d["state_ssm_re"] = 0.1 * nrm(ks[5], (N_SSM_LAYERS, DEC_BATCH, SSM_GROUPS, SSM_P), f32)
    d["state_ssm_im"] = 0.1 * nrm(ks[6], (N_SSM_LAYERS, DEC_BATCH, SSM_GROUPS, SSM_P), f32)
    d["state_ffn_conv"] = 0.5 * nrm(ks[7], (DEPTH, DEC_BATCH, CONV_W - 1, FFN_UP), f32)
    d["ada_w"] = nrm(ks[8], (DEPTH, D, 6 * D), f32) * (0.5 * D ** -0.5)
    d["ada_b"] = 0.02 * nrm(ks[9], (DEPTH, 6 * D), f32)
    d["mix_pre_g"] = 1.0 + 0.02 * nrm(ks[10], (DEPTH, D), f32)
    d["mix_post_g"] = 1.0 + 0.02 * nrm(ks[11], (DEPTH, D), f32)
    d["ffn_pre_g"] = 1.0 + 0.02 * nrm(ks[12], (DEPTH, D), f32)
    d["ffn_post_g"] = 1.0 + 0.02 * nrm(ks[13], (DEPTH, D), f32)
    d["pool_w"] = nrm(ks[14], (N_POOL_LAYERS, POOL_GROUPS, POOL_GC, POOL_GC), f32) * POOL_GC ** -0.5
    d["pool_scale"] = 1.0 + 0.02 * nrm(ks[15], (N_POOL_LAYERS, D), f32)
    d["ssm_A_re"] = -0.5 + 0.01 * nrm(ks[16], (N_SSM_LAYERS, SSM_GROUPS, SSM_P), f32)
    d["ssm_A_im"] = math.pi * jnp.arange(SSM_P, dtype=f32) + 0.01 * nrm(ks[17], (N_SSM_LAYERS, SSM_GROUPS, SSM_P), f32)
    d["ssm_log_dt"] = jax.random.uniform(ks[18], (N_SSM_LAYERS, SSM_GROUPS), f32, math.log(1e-3), math.log(1e-1))
    d["ssm_B_re"] = nrm(ks[19], (N_SSM_LAYERS, SSM_GROUPS, SSM_P, SSM_GC), f32) * (2 * SSM_GC) ** -0.5
    d["ssm_B_im"] = nrm(ks[20], (N_SSM_LAYERS, SSM_GROUPS, SSM_P, SSM_GC), f32) * (2 * SSM_GC) ** -0.5
    d["ssm_C_re"] = nrm(ks[21], (N_SSM_LAYERS, SSM_GROUPS, SSM_GC, SSM_P), f32) * SSM_P ** -0.5
    d["ssm_C_im"] = nrm(ks[22], (N_SSM_LAYERS, SSM_GROUPS, SSM_GC, SSM_P), f32) * SSM_P ** -0.5
    d["ssm_D"] = nrm(ks[23], (N_SSM_LAYERS, D), f32)
    d["ssm_glu_a"] = nrm(ks[24], (N_SSM_LAYERS, D, D), f32) * D ** -0.5
    d["ssm_glu_b"] = nrm(ks[25], (N_SSM_LAYERS, D, D), f32) * D ** -0.5
    d["ffn_w_up"] = nrm(ks[26], (DEPTH, D, FFN_UP), f32) * D ** -0.5
    d["ffn_conv_w"] = nrm(ks[27], (DEPTH, CONV_W, FFN_UP), f32) * CONV_W ** -0.5
    d["ffn_conv_b"] = 0.02 * nrm(ks[28], (DEPTH, FFN_UP), f32)
    d["ffn_w_down"] = nrm(ks[29], (DEPTH, FFN_HIDDEN, D), f32) * FFN_HIDDEN ** -0.5
    return d


def reference(x_prompt, x_sample, c_prompt, c_sample, state_pool, state_ssm_re, state_ssm_im,
              state_ffn_conv, ada_w, ada_b, mix_pre_g, mix_post_g, ffn_pre_g, ffn_post_g,
              pool_w, pool_scale, ssm_A_re, ssm_A_im, ssm_log_dt, ssm_B_re, ssm_B_im,
              ssm_C_re, ssm_C_im, ssm_D, ssm_glu_a, ssm_glu_b,
              ffn_w_up, ffn_conv_w, ffn_conv_b, ffn_w_down):
    yp, ys = x_prompt, x_sample
    pool_p, pool_s, sre_p, sim_p, sre_s, sim_s, conv_p, conv_s = [], [], [], [], [], [], [], []
    for l in range(DEPTH):
        sh1p, sc1p, g1p, sh2p, sc2p, g2p = ada_mod(c_prompt, ada_w[l], ada_b[l])
        sh1s, sc1s, g1s, sh2s, sc2s, g2s = ada_mod(c_sample, ada_w[l], ada_b[l])
        hp = pre_mod(yp, mix_pre_g[l], sh1p, sc1p)
        hs = pre_mod(ys, mix_pre_g[l], sh1s, sc1s)
        j = l // N_MIXERS
        if l % N_MIXERS == 0:
            mp = pool_mix(hp, 0, 0, pool_w[j], pool_scale[j])
            hs_ext = jnp.concatenate([state_pool[j].astype(hs.dtype), hs], axis=1)
            ms = pool_mix(hs_ext, POOL_STATE, PAST_LEN - POOL_STATE, pool_w[j], pool_scale[j])
            pool_p.append(hp[:, -POOL_STATE:])
            pool_s.append(hs_ext[:, -POOL_STATE:])
        else:
            z0 = jnp.zeros((hp.shape[0], SSM_GROUPS, SSM_P), jnp.float32)
            mp, lr_p, li_p = s5_mix(hp, z0, z0, ssm_A_re[j], ssm_A_im[j], ssm_log_dt[j], ssm_B_re[j],
                                    ssm_B_im[j], ssm_C_re[j], ssm_C_im[j], ssm_D[j], ssm_glu_a[j], ssm_glu_b[j])
            ms, lr_s, li_s = s5_mix(hs, state_ssm_re[j], state_ssm_im[j], ssm_A_re[j], ssm_A_im[j], ssm_log_dt[j],
                                    ssm_B_re[j], ssm_B_im[j], ssm_C_re[j], ssm_C_im[j], ssm_D[j],
                                    ssm_glu_a[j], ssm_glu_b[j])
            sre_p.append(lr_p.astype(state_ssm_re.dtype)); sim_p.append(li_p.astype(state_ssm_im.dtype))
            sre_s.append(lr_s.astype(state_ssm_re.dtype)); sim_s.append(li_s.astype(state_ssm_im.dtype))
        yp = gated_residual(yp, mp, mix_post_g[l], g1p)
        ys = gated_residual(ys, ms, mix_post_g[l], g1s)
        fp = pre_mod(yp, ffn_pre_g[l], sh2p, sc2p)
        fs = pre_mod(ys, ffn_pre_g[l], sh2s, sc2s)
        zc = jnp.zeros((fp.shape[0], CONV_W - 1, FFN_UP), fp.dtype)
        op, cp = conv_ffn(fp, zc, ffn_w_up[l], ffn_conv_w[l], ffn_conv_b[l], ffn_w_down[l])
        os_, cs_ = conv_ffn(fs, state_ffn_conv[l], ffn_w_up[l], ffn_conv_w[l], ffn_conv_b[l], ffn_w_down[l])
        conv_p.append(cp); conv_s.append(cs_)
        yp = gated_residual(yp, op, ffn_post_g[l], g2p)
        ys = gated_residual(ys, os_, ffn_post_g[l], g2s)
    new_pool_prompt = jnp.stack(pool_p, axis=0)
    new_pool_sample = jnp.stack(pool_s, axis=0)
    new_ssm_re_prompt = jnp.stack(sre_p, axis=0)
    new_ssm_im_prompt = jnp.stack(sim_p, axis=0)
    new_ssm_re_sample = jnp.stack(sre_s, axis=0)
    new_ssm_im_sample = jnp.stack(sim_s, axis=0)
    new_conv_prompt = jnp.stack(conv_p, axis=0)
    new_conv_sample = jnp.stack(conv_s, axis=0)
    return (yp, ys, new_pool_prompt, new_pool_sample, new_ssm_re_prompt, new_ssm_im_prompt,
            new_ssm_re_sample, new_ssm_im_sample, new_conv_prompt, new_conv_sample)
```

```python
import contextlib
import numpy as np
import concourse.bass as bass
import concourse.mybir as mybir
from concourse.bass_utils import run_bass_kernel_spmd

F32 = mybir.dt.float32
BF16 = mybir.dt.bfloat16
AF = mybir.ActivationFunctionType
ALU = mybir.AluOpType
AX = mybir.AxisListType

NCORES = 8
D = 1024
NDT = 8
FH = 2816
FUP = 5632
NFT = 44
NGT = 22
SEQ = 2048
LB = 256
NPB = SEQ // LB
NSB = 16
LS = 8
HP = 15
HC = 2
EPS = 1e-6
ENGS = ("pe", "act", "dve", "pool", "sp")


class Buf:
    __slots__ = ("name", "lastw", "readers", "const")

    def __init__(self, name):
        self.name = name
        self.lastw = None
        self.readers = []
        self.const = False


class Op:
    __slots__ = ("eng", "fn", "deps", "signal", "val", "is_dma", "key", "dcount", "wait_all")

    def __init__(self, eng, fn, deps):
        self.eng = eng
        self.fn = fn
        self.deps = deps
        self.signal = False
        self.val = 0
        self.is_dma = False
        self.key = None
        self.dcount = 0
        self.wait_all = False


class Prog:
    def __init__(self):
        self.ops = {e: [] for e in ENGS}
        self.dcnt = {}
        self.nbuf = 0
        self.stores = []
        self.enabled = True

    def buf(self, name=None):
        self.nbuf += 1
        return Buf(name or f"b{self.nbuf}")

    def bufs(self, n, name="b"):
        return [self.buf(f"{name}{i}") for i in range(n)]

    def _mk(self, eng, fn, r, w):
        if not self.enabled:
            return None
        deps = set()
        for b in r:
            if b.lastw is not None:
                deps.add(b.lastw)
        for b in w:
            if b.lastw is not None:
                deps.add(b.lastw)
            deps.update(b.readers)
        o = Op(eng, fn, deps)
        for b in r:
            if not b.const:
                b.readers.append(o)
        for b in w:
            b.lastw = o
            b.readers = []
        self.ops[eng].append(o)
        return o

    def op(self, eng, fn, r=(), w=()):
        return self._mk(eng, fn, r, w)

    def dma(self, eng, fn, r=(), w=(), key=None, wait_all=False, store=False):
        o = self._mk(eng, fn, r, w)
        if o is None:
            return None
        o.is_dma = True
        o.key = key
        o.wait_all = wait_all
        if wait_all:
            o.deps = set(d for d in o.deps if not (d.is_dma and d.key == key))
        self.dcnt[key] = self.dcnt.get(key, 0) + 1
        o.dcount = self.dcnt[key]
        if store:
            self.stores.append(o)
        return o

    def full_barrier(self):
        deps = set()
        for e in ENGS:
            comp = [o for o in self.ops[e] if o.fn is not None and not o.is_dma]
            if comp:
                deps.add(comp[-1])
            last_by_key = {}
            for o in self.ops[e]:
                if o.is_dma:
                    last_by_key[o.key] = o
            deps.update(last_by_key.values())
        for e in ENGS:
            self.ops[e].append(Op(e, None, set(deps)))

    def barrier_wait(self, eng, ops):
        o = Op(eng, None, set(ops))
        self.ops[eng].append(o)
        return o

    def emit(self, nc, stack):
        for e in ENGS:
            for o in self.ops[e]:
                for d in o.deps:
                    d.signal = True
        sems = {}
        for e in ENGS:
            sems[e] = stack.enter_context(nc.semaphore("sem_" + e))
            c = 0
            for o in self.ops[e]:
                if o.is_dma or o.fn is None:
                    continue
                if o.signal:
                    c += 1
                    o.val = c
        dsem = {}
        for k in self.dcnt:
            dsem[k] = stack.enter_context(nc.semaphore("dsem_" + str(k)))

        def ev(d):
            if d.is_dma:
                if d.wait_all:
                    return dsem[d.key], 16 * self.dcnt[d.key], ("d", d.key)
                return dsem[d.key], 16 * d.dcount, ("d", d.key)
            return sems[d.eng], d.val, ("e", d.eng)

        block = stack.enter_context(nc.Block())
        prog = self

        def run(engname, eobj):
            known = {}
            for o in prog.ops[engname]:
                waits = {}
                for d in o.deps:
                    if (not d.is_dma) and d.eng == "pe" and engname == "pe" and not o.is_dma:
                        continue
                    s, v, kk = ev(d)
                    if known.get(kk, 0) >= v:
                        continue
                    if kk not in waits or waits[kk][1] < v:
                        waits[kk] = (s, v)
                for kk, (s, v) in waits.items():
                    known[kk] = v
                wl = list(waits.values())
                attach = None
                if o.fn is not None and wl and engname in ("act", "dve", "pool") and not o.is_dma:
                    attach = wl.pop()
                for s, v in wl:
                    eobj.wait_ge(s, v)
                if o.fn is None:
                    continue
                ins = o.fn(eobj)
                if attach is not None:
                    ins._wait_ge(attach[0], attach[1])
                if o.is_dma:
                    ins.then_inc(dsem[o.key], 16)
                elif o.signal:
                    ins.then_inc(sems[engname], 1)

        @block.tensor
        def _(e):
            run("pe", e)

        @block.scalar
        def _(e):
            run("act", e)

        @block.vector
        def _(e):
            run("dve", e)

        @block.gpsimd
        def _(e):
            run("pool", e)

        @block.sync
        def _(e):
            run("sp", e)


def seg3(ap2, nseg):
    return ap2.rearrange("p (s c) -> p s c", s=nseg)


def seg4(ap3, nseg):
    return ap3.rearrange("p k (s c) -> p k s c", s=nseg)


def build_nc(cfg=None):
    cfg = cfg or {}
    n_pblocks = cfg.get("n_pblocks", NPB)
    do_sample = cfg.get("do_sample", True)
    skip_s5 = cfg.get("skip_s5", False)
    nlayers = cfg.get("nlayers", 2)
    stage = cfg.get("stage", 99)

    nc = bass.Bass("TRN2", target_bir_lowering=False)

    def din(name, shape):
        return nc.dram_tensor(name, list(shape), F32, kind="ExternalInput").ap()

    def dout(name, shape):
        return nc.dram_tensor(name, list(shape), F32, kind="ExternalOutput").ap()

    xp = din("xp", [SEQ, D]); xs = din("xs", [128, D])
    cp = din("cp", [128, D]); cs = din("cs", [128, D])
    spool = din("spool", [NSB, HP, D])
    sre = din("sre", [NSB, 64, 64]); sim = din("sim", [NSB, 64, 64])
    sconv = din("sconv", [2, NSB, HC, FUP])
    ada_w = din("ada_w", [2, D, 6 * D]); ada_b = din("ada_b", [2, 6 * D])
    mix_pre_g = din("mix_pre_g", [2, D]); mix_post_g = din("mix_post_g", [2, D])
    ffn_pre_g = din("ffn_pre_g", [2, D]); ffn_post_g = din("ffn_post_g", [2, D])
    pool_w = din("pool_w", [1, 4, 256, 256]); pool_scale = din("pool_scale", [1, D])
    ssm_A_re = din("ssm_A_re", [1, 64, 64]); ssm_A_im = din("ssm_A_im", [1, 64, 64])
    ssm_log_dt = din("ssm_log_dt", [1, 64])
    ssm_B_re = din("ssm_B_re", [1, 64, 64, 16]); ssm_B_im = din("ssm_B_im", [1, 64, 64, 16])
    ssm_C_re = din("ssm_C_re", [1, 64, 16, 64]); ssm_C_im = din("ssm_C_im", [1, 64, 16, 64])
    ssm_D = din("ssm_D", [1, D])
    ssm_glu_a = din("ssm_glu_a", [1, D, D]); ssm_glu_b = din("ssm_glu_b", [1, D, D])
    ffn_w_up = din("ffn_w_up", [2, D, FUP]); ffn_conv_w = din("ffn_conv_w", [2, 3, FUP])
    ffn_conv_b = din("ffn_conv_b", [2, FUP]); ffn_w_down = din("ffn_w_down", [2, FH, D])

    yp = dout("yp", [SEQ, D]); ys = dout("ys", [128, D])
    npp = dout("npp", [HP, D]); nps = dout("nps", [NSB, HP, D])
    nrp = dout("nrp", [64, 64]); nip = dout("nip", [64, 64])
    nrs = dout("nrs", [NSB, 64, 64]); nis = dout("nis", [NSB, 64, 64])
    ncp = dout("ncp", [2, HC, FUP]); ncs = dout("ncs", [2, NSB, HC, FUP])

    NSS = NGT // 2
    wu_scr = nc.dram_tensor("wu_scr", [2, NSS, 128, NDT * 512], BF16).ap()
    wd_scr = nc.dram_tensor("wd_scr", [2, NSS, 128, 2 * D], BF16).ap()
    wg_scr = nc.dram_tensor("wg_scr", [2, 128, NDT * D], BF16).ap()
    s5ops = nc.dram_tensor("s5ops", [8, 128, 4 * 8 * 128], BF16).ap()
    s5tab_p = nc.dram_tensor("s5tab_p", [3, 128, 64 * 33], F32).ap()
    s5tab_s = nc.dram_tensor("s5tab_s", [3, 128, 64 * 32], F32).ap()

    P = Prog()
    st = contextlib.ExitStack()
    with st:
        def sb(name, shape, dt=F32):
            return st.enter_context(nc.sbuf_tensor(name, list(shape), dt))

        identf = sb("identf", [128, 128]); identb = sb("identb", [128, 128], BF16)
        epsc = sb("epsc", [128, 1])
        fmp = sb("fmp", [128, 512])
        fmp1 = sb("fmp1", [128, 96])
        Wp = sb("Wp", [128, 4, 2, 256], BF16)
        modp = sb("modp", [128, 2, 4, NDT])
        mods = sb("mods", [128, 2, 4, NDT, NSB])
        Gp = sb("Gp", [128, 2, 2, D]); Gs = sb("Gs", [128, 2, 2, D])
        inv15 = sb("inv15", [128, 4, HP])
        stT = sb("stT", [128, 2, NFT, NSB * HC])
        cstage = sb("cstage", [128, 1, NFT, NSB * HC])
        b_ident, b_fmp, b_Wp, b_modp, b_mods, b_Gp, b_Gs, b_inv15, b_stT = P.bufs(9, "c")
        b_cstage = P.buf("cstage")

        FM_ADAB = 0; FM_MPG = 96; FM_FPG = 112; FM_CW = 128; FM_CB = 392

        psall = st.enter_context(nc.psum_tensor("psall", [128, 8, 512], F32))
        banks = [psall[:, i, :] for i in range(8)]
        b_bank = P.bufs(8, "bank")

        def bank_bf(i):
            return banks[i][:].bitcast(BF16)

        pro = contextlib.ExitStack()
        with pro:
            def psb(name, shape, dt=F32):
                return pro.enter_context(nc.sbuf_tensor(name, list(shape), dt))

            P.op("pool", lambda e: e.memset(identf[:], 0.0), w=[b_ident])
            P.op("pool", lambda e: e.affine_select(out=identf[:], in_=identf[:], pattern=[[-1, 128]],
                                                   compare_op=ALU.not_equal, fill=1.0, base=0,
                                                   channel_multiplier=1), r=[b_ident], w=[b_ident])
            P.op("dve", lambda e: e.tensor_copy(out=identb[:], in_=identf[:]), r=[b_ident], w=[b_ident])
            P.op("pool", lambda e: e.memset(epsc[:], EPS), w=[b_ident])

            P.enabled = stage >= 1
            b_wscr = P.buf("wscr")
            NCV = 3
            cvf = psb("cvf", [128, NCV, NDT * 512]); cvb = psb("cvb", [128, NCV, NDT * 512], BF16)
            b_cvf = P.bufs(NCV, "cvf"); b_cvb = P.bufs(NCV, "cvb")
            cvctr = [0]
            cast_rr = ["dve", "act"]

            def cast_op(eng, dst, src, r, w):
                if eng == "act":
                    P.op("act", lambda e: e.activation(out=dst, in_=src, func=AF.Copy), r=r, w=w)
                else:
                    P.op(eng, lambda e: e.tensor_copy(out=dst, in_=src), r=r, w=w)

            def convert(loads, nelem, dst_ap):
                k = cvctr[0]
                cvctr[0] += 1
                sl = k % NCV
                for li, (vf, dap) in enumerate(loads):
                    qn = "sp"
                    P.dma(qn, lambda e, vf=vf, dap=dap, sl=sl: e.dma_start(out=vf(cvf[:, sl, :]), in_=dap),
                          w=[b_cvf[sl]], key=f"cvf{sl}")
                e1 = cast_rr[(2 * k) % len(cast_rr)]
                e2 = cast_rr[(2 * k + 1) % len(cast_rr)]
                h = nelem // 2
                cast_op(e1, cvb[:, sl, 0:h], cvf[:, sl, 0:h], [b_cvf[sl]], [b_cvb[sl]])
                cast_op(e2, cvb[:, sl, h:nelem], cvf[:, sl, h:nelem], [b_cvf[sl]], [b_cvb[sl]])
                P.dma("pool", lambda e, sl=sl: e.dma_start(out=dst_ap, in_=cvb[:, sl, 0:nelem]),
                      r=[b_cvb[sl]], w=[b_wscr], key=f"cvb{sl}")

            cv_jobs = []

            def convert_ffn(l):
                for ss in range(NSS):
                    loads = []
                    for h in range(2):
                        dap = ffn_w_up[l, :, h * FH + ss * 256: h * FH + (ss + 1) * 256].rearrange("(kt p) n -> p kt n", p=128)
                        loads.append((lambda v, h=h: v.rearrange("p (kt h c) -> p kt h c", kt=NDT, h=2)[:, :, h, :], dap))
                    cv_jobs.append((loads, NDT * 512, wu_scr[l, ss]))
                    dap = ffn_w_down[l, ss * 256:(ss + 1) * 256, :].rearrange("(j p) d -> p j d", p=128)
                    cv_jobs.append(([(lambda v: v[:, 0:2 * D].rearrange("p (j d) -> p j d", j=2), dap)], 2 * D, wd_scr[l, ss]))

            def convert_glu():
                for ab, wsrc in enumerate((ssm_glu_a, ssm_glu_b)):
                    for k0 in range(0, NDT, 4):
                        dap = wsrc[0, k0 * 128:(k0 + 4) * 128, :].rearrange("(kt p) n -> p kt n", p=128)
                        cv_jobs.append(([(lambda v: v.rearrange("p (kt n) -> p kt n", kt=4), dap)], 4 * D,
                                        wg_scr[ab, :, k0 * D:(k0 + 4) * D]))

            convert_ffn(0)
            if not skip_s5:
                convert_glu()
            if nlayers > 1:
                convert_ffn(1)

            def pump_convert(k):
                for _ in range(k):
                    if cv_jobs:
                        convert(*cv_jobs.pop(0))

            P.enabled = stage >= 1
            stg = psb("stg", [128, 4, 128])
            b_stg = P.buf("stg")
            P.op("dve", lambda e: e.memset(stg[:], 0.0), w=[b_stg])
            srcs = [
                (0, 0, 96, ada_b.rearrange("l (t p) -> (l t) p", p=128)),
                (0, 96, 16, mix_pre_g.rearrange("l (t p) -> (l t) p", p=128)),
                (0, 112, 16, ffn_pre_g.rearrange("l (t p) -> (l t) p", p=128)),
            ]
            cwv = ffn_conv_w.rearrange("l k (t p) -> (l k t) p", p=128)
            srcs += [(1, 0, 128, cwv[0:128]), (2, 0, 128, cwv[128:256]), (3, 0, 8, cwv[256:264]),
                     (3, 8, 88, ffn_conv_b.rearrange("l (t p) -> (l t) p", p=128))]
            sub = cfg.get("sub", 99)
            for (ti, r0, n, sap) in srcs[:sub]:
                P.dma("sp", lambda e, ti=ti, r0=r0, n=n, sap=sap: e.dma_start(out=stg[r0:r0 + n, ti, :], in_=sap),
                      r=[b_stg], w=[b_stg], key="cst", wait_all=True)
            for ti in range(4 if sub >= 20 else 0):
                P.op("pe", lambda e, ti=ti: e.transpose(out=banks[7][:, ti * 128:(ti + 1) * 128], in_=stg[:, ti, :],
                                                        identity=identf[:]), r=[b_stg, b_ident], w=[b_bank[7]])
            P.op("act", lambda e: e.activation(out=fmp[:], in_=banks[7][:], func=AF.Copy), r=[b_bank[7]], w=[b_fmp])
            P.op("dve", lambda e: e.tensor_scalar(out=fmp1[:], in0=fmp[:, 0:96], scalar1=1.0, scalar2=None, op0=ALU.add),
                 r=[b_fmp], w=[b_fmp])

            P.enabled = stage >= 2
            bc = psb("bc", [128, 9, D])
            b_bc = P.buf("bc")
            bsrc = [mix_post_g[0:1, :], ffn_post_g[0:1, :], mix_post_g[1:2, :], ffn_post_g[1:2, :],
                    ada_b[0:1, 2 * D:3 * D], ada_b[0:1, 5 * D:6 * D], ada_b[1:2, 2 * D:3 * D], ada_b[1:2, 5 * D:6 * D],
                    pool_scale[0:1, :]]
            for i, s in enumerate(bsrc):
                P.dma("sp", lambda e, i=i, s=s: e.dma_start(out=bc[:, i, :], in_=s.partition_broadcast(128)),
                      w=[b_bc], key="cst", wait_all=True)
            for i in range(4):
                P.op("dve", lambda e, i=i: e.tensor_tensor(out=bc[:, 4 + i, :], in0=bc[:, 4 + i, :], in1=bc[:, i, :], op=ALU.mult),
                     r=[b_bc], w=[b_bc])

            P.enabled = stage >= 3
            wpf = psb("wpf", [128, 4, 2, 256])
            b_wpf = P.buf("wpf")
            for g in range(4):
                P.dma("sp", lambda e, g=g: e.dma_start(out=wpf[:, g], in_=pool_w[0, g].rearrange("(kt p) n -> p kt n", p=128)),
                      w=[b_wpf], key="cst", wait_all=True)
            for g in range(4):
                for kt in range(2):
                    P.op("dve", lambda e, g=g, kt=kt: e.tensor_tensor(out=Wp[:, g, kt, :], in0=wpf[:, g, kt, :],
                                                                     in1=bc[:, 8, g * 256:(g + 1) * 256], op=ALU.mult),
                         r=[b_wpf, b_bc], w=[b_Wp])

            P.enabled = stage >= 3
            for gi, w_ in enumerate((2, 4, 8, 16)):
                P.op("pool", lambda e, gi=gi, w_=w_: e.memset(inv15[:, gi, :], 1.0 / w_), w=[b_inv15])
                for t in range(min(w_ - 1, HP)):
                    P.op("pool", lambda e, gi=gi, t=t: e.memset(inv15[:, gi, t:t + 1], 1.0 / (t + 1)), w=[b_inv15])

            P.enabled = stage >= 4
            ctile = psb("ctile", [128, 2, D]); csil = psb("csil", [128, 2, D], BF16)
            cTp = psb("cTp", [128, NDT, 128], BF16); cTs = psb("cTs", [128, NDT, 128], BF16)
            cT17 = psb("cT17", [128, NDT, 17], BF16)
            b_ct, b_cT = P.buf("ct"), P.buf("cT")
            P.dma("sp", lambda e: e.dma_start(out=ctile[:, 0, :], in_=cp[:, :]), w=[b_ct], key="cst", wait_all=True)
            P.dma("sp", lambda e: e.dma_start(out=ctile[:, 1, :], in_=cs[:, :]), w=[b_ct], key="cst", wait_all=True)
            P.op("act", lambda e: e.activation(out=csil[:], in_=ctile[:], func=AF.Silu), r=[b_ct], w=[b_ct])
            for which, dstT in ((0, cTp), (1, cTs)):
                for dt in range(NDT):
                    P.op("pe", lambda e, which=which, dt=dt: e.transpose(
                        out=bank_bf(6)[:, dt * 128:(dt + 1) * 128], in_=csil[:, which, dt * 128:(dt + 1) * 128],
                        identity=identb[:]), r=[b_ct, b_ident], w=[b_bank[6]])
                P.op("act", lambda e, dstT=dstT: e.activation(out=dstT[:], in_=bank_bf(6).rearrange("p (k c) -> p k c", k=NDT),
                                                              func=AF.Copy), r=[b_bank[6]], w=[b_cT])
            P.op("dve", lambda e: e.tensor_copy(out=cT17[:, :, 0:1], in_=cTp[:, :, 0:1]), r=[b_cT], w=[b_cT])
            P.op("dve", lambda e: e.tensor_copy(out=cT17[:, :, 1:17], in_=cTs[:, :, 0:128:8]), r=[b_cT], w=[b_cT])

            P.enabled = stage >= 5
            tmpf = psb("tmpf", [128, 2, 17]); b_tmpf = P.bufs(2, "tmpf")
            for l in range(2):
                for v in range(6):
                    for hf in range(2):
                        k = cvctr[0]
                        cvctr[0] += 1
                        sl = k % NCV
                        c0 = v * D + hf * 512
                        P.dma("sp", lambda e, l=l, c0=c0, sl=sl: e.dma_start(
                            out=cvf[:, sl, :].rearrange("p (kt n) -> p kt n", kt=NDT),
                            in_=ada_w[l, :, c0:c0 + 512].rearrange("(kt p) n -> p kt n", p=128)),
                            w=[b_cvf[sl]], key=f"cvf{sl}")
                        for part, eng in enumerate(("dve", "act", "dve", "act")):
                            cast_op(eng, cvb[:, sl, part * 1024:(part + 1) * 1024], cvf[:, sl, part * 1024:(part + 1) * 1024],
                                    [b_cvf[sl]], [b_cvb[sl]])
                        wv_ = cvb[:, sl, :].rearrange("p (kt n) -> p kt n", kt=NDT)
                        if v in (0, 1, 3, 4):
                            mslot = {0: 1, 1: 0, 3: 3, 4: 2}[v]
                            is_scale = v in (1, 4)
                            gbase = FM_MPG if v == 1 else FM_FPG
                            for d4 in range(4):
                                dt = hf * 4 + d4
                                bk = 4 + (dt % 2)
                                for kt in range(NDT):
                                    P.op("pe", lambda e, kt=kt, d4=d4, bk=bk, wv_=wv_: e.matmul(
                                        banks[bk][:, 0:17], wv_[:, kt, d4 * 128:(d4 + 1) * 128], cT17[:, kt, :],
                                        start=(kt == 0), stop=(kt == NDT - 1)), r=[b_cvb[sl], b_cT], w=[b_bank[bk]])
                                bcol = FM_ADAB + l * 48 + v * 8 + dt
                                ts = dt % 2
                                if is_scale:
                                    P.op("act", lambda e, bk=bk, bcol=bcol, ts=ts: e.activation(
                                        out=tmpf[:, ts, :], in_=banks[bk][:, 0:17], func=AF.Identity,
                                        bias=fmp1[:, bcol:bcol + 1]), r=[b_bank[bk], b_fmp], w=[b_tmpf[ts]])
                                    gcol = gbase + l * 8 + dt
                                    P.op("dve", lambda e, l=l, mslot=mslot, dt=dt, ts=ts, gcol=gcol: e.tensor_scalar(
                                        out=modp[:, l, mslot, dt:dt + 1], in0=tmpf[:, ts, 0:1], scalar1=fmp[:, gcol:gcol + 1],
                                        scalar2=None, op0=ALU.mult), r=[b_tmpf[ts], b_fmp], w=[b_modp])
                                    P.op("dve", lambda e, l=l, mslot=mslot, dt=dt, ts=ts, gcol=gcol: e.tensor_scalar(
                                        out=mods[:, l, mslot, dt, :], in0=tmpf[:, ts, 1:17],
                                        scalar1=fmp[:, gcol:gcol + 1], scalar2=None, op0=ALU.mult),
                                        r=[b_tmpf[ts], b_fmp], w=[b_mods])
                                else:
                                    P.op("act", lambda e, l=l, mslot=mslot, dt=dt, bk=bk, bcol=bcol: e.activation(
                                        out=modp[:, l, mslot, dt:dt + 1], in_=banks[bk][:, 0:1], func=AF.Identity,
                                        bias=fmp[:, bcol:bcol + 1]), r=[b_bank[bk], b_fmp], w=[b_modp])
                                    P.op("act", lambda e, l=l, mslot=mslot, dt=dt, bk=bk, bcol=bcol: e.activation(
                                        out=mods[:, l, mslot, dt, :], in_=banks[bk][:, 1:17], func=AF.Identity,
                                        bias=fmp[:, bcol:bcol + 1]), r=[b_bank[bk], b_fmp], w=[b_mods])
                        else:
                            which = 0 if v == 2 else 1
                            gi = l * 2 + which
                            for grp, (cT_, Gt, bG) in enumerate(((cTp, Gp, b_Gp), (cTs, Gs, b_Gs))):
                                bk = 6 + grp
                                for kt in range(NDT):
                                    P.op("pe", lambda e, kt=kt, bk=bk, cT_=cT_, wv_=wv_: e.matmul(
                                        banks[bk][:], cT_[:, kt, :], wv_[:, kt, :],
                                        start=(kt == 0), stop=(kt == NDT - 1)), r=[b_cvb[sl], b_cT], w=[b_bank[bk]])
                                P.op("dve", lambda e, l=l, which=which, hf=hf, bk=bk, Gt=Gt, gi=gi: e.tensor_tensor(
                                    out=Gt[:, l, which, hf * 512:(hf + 1) * 512], in0=banks[bk][:],
                                    in1=bc[:, gi, hf * 512:(hf + 1) * 512], op=ALU.mult),
                                    r=[b_bank[bk], b_bc], w=[bG])
                                P.op("dve", lambda e, l=l, which=which, hf=hf, Gt=Gt, gi=gi: e.tensor_tensor(
                                    out=Gt[:, l, which, hf * 512:(hf + 1) * 512],
                                    in0=Gt[:, l, which, hf * 512:(hf + 1) * 512],
                                    in1=bc[:, 4 + gi, hf * 512:(hf + 1) * 512], op=ALU.add),
                                    r=[bG, b_bc], w=[bG])
                        if stage >= 7:
                            pump_convert(2)

            P.enabled = stage >= 6
            if do_sample:
                cst_tok = cvf[0:32].rearrange("p a n -> p (a n)")[:, 0:FUP]
                for l in range(2):
                    P.dma("sp", lambda e, l=l: e.dma_start(out=cst_tok, in_=sconv[l].rearrange("b k f -> (b k) f")),
                          w=[b_cvf[0], b_cvf[1]], key="cst_tok")
                    for f0 in range(0, NFT, 16):
                        nf = min(16, NFT - f0)
                        for j in range(nf):
                            ft = f0 + j
                            P.op("pe", lambda e, ft=ft, j=j: e.transpose(
                                out=banks[6][:, j * 32:(j + 1) * 32], in_=cst_tok[:, ft * 128:(ft + 1) * 128],
                                identity=identf[0:32, 0:32]), r=[b_cvf[0], b_cvf[1], b_ident], w=[b_bank[6]])
                        P.op("act", lambda e, l=l, f0=f0, nf=nf: e.activation(
                            out=stT[:, l, f0:f0 + nf, :], in_=banks[6][:, 0:nf * 32].rearrange("p (f c) -> p f c", f=nf),
                            func=AF.Copy), r=[b_bank[6]], w=[b_stT])

            P.enabled = stage >= 7
            pump_convert(len(cv_jobs))

        P.enabled = stage >= 8
        P.full_barrier()


        if not skip_s5 and nlayers > 1:
            pro2 = contextlib.ExitStack()
            with pro2:
                def p2(name, shape, dt=F32):
                    return pro2.enter_context(nc.sbuf_tensor(name, list(shape), dt))
                TWO_PI = float(2 * np.pi)
                C1 = 6.28125
                C2 = 0.0019353071795864769
                MAGIC = 12582912.0
                bq = {n: P.buf("q_" + n) for n in "Ald AT dtv marg ang kang nq trig magk L f W Bst Bsw Cld CT CA CB Dld Dcol alpha phi tab colv Eb stage perm".split()}
                Ald = p2("Ald", [64, 2, 128]); AT = p2("AT", [128, 2, 64]); dtv = p2("dtv", [128, 64])
                marg = p2("marg", [128, 64]); ang = p2("ang", [128, 64])
                for i, src in enumerate((ssm_A_re, ssm_A_im)):
                    for hh in range(2):
                        P.dma("sp", lambda e, i=i, hh=hh, src=src: e.dma_start(out=Ald[:, i, hh * 64:(hh + 1) * 64], in_=src[0]),
                              w=[bq["Ald"]], key="q_c", wait_all=True)
                P.dma("sp", lambda e: e.dma_start(out=dtv[:], in_=ssm_log_dt[0:1, :].partition_broadcast(128)),
                      w=[bq["dtv"]], key="q_c", wait_all=True)
                for i in range(2):
                    P.op("pe", lambda e, i=i: e.transpose(out=banks[7][:, i * 64:(i + 1) * 64], in_=Ald[:, i, :],
                                                          identity=identf[0:64, 0:64]), r=[bq["Ald"], b_ident], w=[b_bank[7]])
                P.op("act", lambda e: e.activation(out=AT[:].rearrange("p a g -> p (a g)"), in_=banks[7][:, 0:128], func=AF.Copy),
                     r=[b_bank[7]], w=[bq["AT"]])
                P.op("act", lambda e: e.activation(out=dtv[:], in_=dtv[:], func=AF.Exp), r=[bq["dtv"]], w=[bq["dtv"]])
                P.op("dve", lambda e: e.tensor_tensor(out=marg[:], in0=AT[:, 0, :], in1=dtv[:], op=ALU.mult),
                     r=[bq["AT"], bq["dtv"]], w=[bq["marg"]])
                P.op("dve", lambda e: e.tensor_tensor(out=ang[:], in0=AT[:, 1, :], in1=dtv[:], op=ALU.mult),
                     r=[bq["AT"], bq["dtv"]], w=[bq["ang"]])

                def reduce_2pi(x, nq_, bx, bn):
                    P.op("dve", lambda e: e.tensor_scalar(out=nq_, in0=x, scalar1=1.0 / TWO_PI, scalar2=MAGIC, op0=ALU.mult, op1=ALU.add),
                         r=[bx], w=[bn])
                    P.op("dve", lambda e: e.tensor_scalar(out=nq_, in0=nq_, scalar1=MAGIC, scalar2=None, op0=ALU.subtract),
                         r=[bn], w=[bn])
                    P.op("dve", lambda e: e.scalar_tensor_tensor(out=x, in0=nq_, scalar=-C1, in1=x, op0=ALU.mult, op1=ALU.add),
                         r=[bn, bx], w=[bx])
                    P.op("dve", lambda e: e.scalar_tensor_tensor(out=x, in0=nq_, scalar=-C2, in1=x, op0=ALU.mult, op1=ALU.add),
                         r=[bn, bx], w=[bx])

                kang = p2("kang", [128, 2, 9, 64]); nq = p2("nq", [128, 2 * 33 * 64]); trig = p2("trig", [128, 2, 9, 64])
                magk = p2("magk", [128, 9, 64]); Lt = p2("Lt", [128, 2, 9, 64])
                for k in range(9):
                    P.op("dve", lambda e, k=k: e.tensor_scalar(out=kang[:, 0, k, :], in0=ang[:], scalar1=float(k), scalar2=None, op0=ALU.mult),
                         r=[bq["ang"]], w=[bq["kang"]])
                    P.op("dve", lambda e, k=k: e.tensor_scalar(out=kang[:, 1, k, :], in0=ang[:], scalar1=float(k), scalar2=float(np.pi / 2),
                                                               op0=ALU.mult, op1=ALU.add), r=[bq["ang"]], w=[bq["kang"]])
                    P.op("act", lambda e, k=k: e.activation(out=magk[:, k, :], in_=marg[:], func=AF.Exp, scale=float(k)),
                         r=[bq["marg"]], w=[bq["magk"]])
                kflat = kang[:].rearrange("p a k g -> p (a k g)")
                reduce_2pi(kflat, nq[:, 0:2 * 9 * 64], bq["kang"], bq["nq"])
                P.op("act", lambda e: e.activation(out=trig[:].rearrange("p a k g -> p (a k g)"), in_=kflat, func=AF.Sin),
                     r=[bq["kang"]], w=[bq["trig"]])
                P.op("dve", lambda e: e.tensor_tensor(out=Lt[:, 0], in0=magk[:], in1=trig[:, 1], op=ALU.mult),
                     r=[bq["magk"], bq["trig"]], w=[bq["L"]])
                P.op("dve", lambda e: e.tensor_tensor(out=Lt[:, 1], in0=magk[:], in1=trig[:, 0], op=ALU.mult),
                     r=[bq["magk"], bq["trig"]], w=[bq["L"]])
                ft = p2("ft", [128, 8, 64])
                P.op("dve", lambda e: e.tensor_scalar(out=ft[:, 0, :], in0=Lt[:, 0, 1, :], scalar1=-1.0, scalar2=None, op0=ALU.add),
                     r=[bq["L"]], w=[bq["f"]])
                P.op("dve", lambda e: e.tensor_tensor(out=ft[:, 1, :], in0=AT[:, 0, :], in1=AT[:, 0, :], op=ALU.mult), r=[bq["AT"], bq["f"]], w=[bq["f"]])
                P.op("dve", lambda e: e.tensor_tensor(out=ft[:, 2, :], in0=AT[:, 1, :], in1=AT[:, 1, :], op=ALU.mult), r=[bq["AT"], bq["f"]], w=[bq["f"]])
                P.op("dve", lambda e: e.tensor_tensor(out=ft[:, 1, :], in0=ft[:, 1, :], in1=ft[:, 2, :], op=ALU.add), r=[bq["f"]], w=[bq["f"]])
                P.op("dve", lambda e: e.reciprocal(out=ft[:, 1, :], in_=ft[:, 1, :]), r=[bq["f"]], w=[bq["f"]])
                P.op("dve", lambda e: e.tensor_tensor(out=ft[:, 2, :], in0=ft[:, 0, :], in1=AT[:, 0, :], op=ALU.mult), r=[bq["f"], bq["AT"]], w=[bq["f"]])
                P.op("dve", lambda e: e.tensor_tensor(out=ft[:, 3, :], in0=Lt[:, 1, 1, :], in1=AT[:, 1, :], op=ALU.mult), r=[bq["L"], bq["AT"], bq["f"]], w=[bq["f"]])
                P.op("dve", lambda e: e.tensor_tensor(out=ft[:, 2, :], in0=ft[:, 2, :], in1=ft[:, 3, :], op=ALU.add), r=[bq["f"]], w=[bq["f"]])
                P.op("dve", lambda e: e.tensor_tensor(out=ft[:, 4, :], in0=ft[:, 2, :], in1=ft[:, 1, :], op=ALU.mult), r=[bq["f"]], w=[bq["f"]])
                P.op("dve", lambda e: e.tensor_tensor(out=ft[:, 2, :], in0=Lt[:, 1, 1, :], in1=AT[:, 0, :], op=ALU.mult), r=[bq["L"], bq["AT"], bq["f"]], w=[bq["f"]])
                P.op("dve", lambda e: e.tensor_tensor(out=ft[:, 3, :], in0=ft[:, 0, :], in1=AT[:, 1, :], op=ALU.mult), r=[bq["f"], bq["AT"]], w=[bq["f"]])
                P.op("dve", lambda e: e.tensor_tensor(out=ft[:, 2, :], in0=ft[:, 2, :], in1=ft[:, 3, :], op=ALU.subtract), r=[bq["f"]], w=[bq["f"]])
                P.op("dve", lambda e: e.tensor_tensor(out=ft[:, 5, :], in0=ft[:, 2, :], in1=ft[:, 1, :], op=ALU.mult), r=[bq["f"]], w=[bq["f"]])
                Wt = p2("Wt", [128, 2, 8, 64]); Wtmp = nq[:, 3584:4096].rearrange("p (k g) -> p k g", k=8)
                fre_b = ft[:, 4, :].unsqueeze(1).to_broadcast([128, 8, 64])
                fim_b = ft[:, 5, :].unsqueeze(1).to_broadcast([128, 8, 64])
                P.op("dve", lambda e: e.tensor_tensor(out=Wt[:, 0], in0=Lt[:, 0, 0:8, :], in1=fre_b, op=ALU.mult), r=[bq["L"], bq["f"]], w=[bq["W"]])
                P.op("dve", lambda e: e.tensor_tensor(out=Wtmp[:], in0=Lt[:, 1, 0:8, :], in1=fim_b, op=ALU.mult), r=[bq["L"], bq["f"]], w=[bq["nq"]])
                P.op("dve", lambda e: e.tensor_tensor(out=Wt[:, 0], in0=Wt[:, 0], in1=Wtmp[:], op=ALU.subtract), r=[bq["W"], bq["nq"]], w=[bq["W"]])
                P.op("dve", lambda e: e.tensor_tensor(out=Wt[:, 1], in0=Lt[:, 0, 0:8, :], in1=fim_b, op=ALU.mult), r=[bq["L"], bq["f"]], w=[bq["W"]])
                P.op("dve", lambda e: e.tensor_tensor(out=Wtmp[:], in0=Lt[:, 1, 0:8, :], in1=fre_b, op=ALU.mult), r=[bq["L"], bq["f"], bq["W"]], w=[bq["nq"]])
                P.op("dve", lambda e: e.tensor_tensor(out=Wt[:, 1], in0=Wt[:, 1], in1=Wtmp[:], op=ALU.add), r=[bq["W"], bq["nq"]], w=[bq["W"]])

                Bst = p2("Bst", [128, 64, 16]); Bsw = p2("Bsw", [128, 64, 16])
                for (dst, bname, top, bot) in ((Bst, "Bst", ssm_B_re, ssm_B_im), (Bsw, "Bsw", ssm_B_im, ssm_B_re)):
                    for hh, src in enumerate((top, bot)):
                        for g0 in range(0, 64, 16):
                            P.dma("sp", lambda e, dst=dst, hh=hh, src=src, g0=g0: e.dma_start(
                                out=dst[hh * 64:(hh + 1) * 64, g0:g0 + 16, :], in_=src[0, g0:g0 + 16].rearrange("g p c -> p g c")),
                                w=[bq[bname]], key="q_c", wait_all=True)
                P.op("dve", lambda e: e.tensor_scalar(out=Bsw[0:64], in0=Bsw[0:64], scalar1=-1.0, scalar2=None, op0=ALU.mult),
                     r=[bq["Bsw"]], w=[bq["Bsw"]])
                Cld = p2("Cld", [128, 2, 8, 128]); CA = p2("CA", [128, 64, 16]); CB = p2("CB", [128, 64, 16])
                for v, (left, right) in enumerate(((ssm_C_re, ssm_C_im), (ssm_C_im, ssm_C_re))):
                    for hh, src in enumerate((left, right)):
                        P.dma("sp", lambda e, v=v, hh=hh, src=src: e.dma_start(
                            out=Cld[:, v, :, hh * 64:(hh + 1) * 64], in_=src[0].rearrange("(t g) c p -> (g c) t p", g=8)),
                            w=[bq["Cld"]], key="q_c", wait_all=True)
                for v, (dst, bname) in enumerate(((CA, "CA"), (CB, "CB"))):
                    for t4 in range(2):
                        for tt in range(4):
                            t = t4 * 4 + tt
                            P.op("pe", lambda e, v=v, t=t, tt=tt: e.transpose(out=banks[6][:, tt * 128:(tt + 1) * 128], in_=Cld[:, v, t, :],
                                                                              identity=identf[:]), r=[bq["Cld"], b_ident], w=[b_bank[6]])
                        P.op("act", lambda e, dst=dst, t4=t4: e.activation(
                            out=dst[:, t4 * 32:(t4 + 1) * 32, :].rearrange("p g c -> p (g c)"), in_=banks[6][:], func=AF.Copy),
                            r=[b_bank[6]], w=[bq[bname]])
                P.op("dve", lambda e: e.tensor_scalar(out=CA[64:128], in0=CA[64:128], scalar1=-1.0, scalar2=None, op0=ALU.mult), r=[bq["CA"]], w=[bq["CA"]])
                P.op("dve", lambda e: e.tensor_scalar(out=CB[:], in0=CB[:], scalar1=-1.0, scalar2=None, op0=ALU.mult), r=[bq["CB"]], w=[bq["CB"]])
                Dld = p2("Dld", [64, 8, 16]); Dcol = p2("Dcol", [128, 64])
                P.dma("sp", lambda e: e.dma_start(out=Dld[:, 0, :], in_=ssm_D[0].rearrange("(g c) -> g c", c=16)), w=[bq["Dld"]], key="q_c", wait_all=True)
                for r_ in range(1, 8):
                    P.op("dve", lambda e, r_=r_: e.tensor_copy(out=Dld[:, r_, :], in_=Dld[:, 0, :]), r=[bq["Dld"]], w=[bq["Dld"]])
                P.op("pe", lambda e: e.transpose(out=banks[7][:, 0:64], in_=Dld[:].rearrange("g r c -> g (r c)"), identity=identf[0:64, 0:64]),
                     r=[bq["Dld"], b_ident], w=[b_bank[7]])
                P.op("act", lambda e: e.activation(out=Dcol[:], in_=banks[7][:, 0:64], func=AF.Copy), r=[b_bank[7]], w=[bq["Dcol"]])

                alpha = p2("alpha", [128, 64]); colv = p2("colv", [128, 33])
                phi = p2("phi", [128, 2, 64, 33]); tab = p2("tab", [128, 3, 64, 33])
                P.op("dve", lambda e: e.tensor_scalar(out=alpha[:], in0=ang[:], scalar1=8.0, scalar2=None, op0=ALU.mult), r=[bq["ang"]], w=[bq["alpha"]])
                reduce_2pi(alpha[:], nq[:, 0:64], bq["alpha"], bq["nq"])
                for c_ in range(33):
                    P.op("pool", lambda e, c_=c_: e.memset(colv[:, c_:c_ + 1], float(c_)), w=[bq["colv"]])
                a_b = alpha[:].unsqueeze(2).to_broadcast([128, 64, 33])
                c_b = colv[:].unsqueeze(1).to_broadcast([128, 64, 33])
                P.op("dve", lambda e: e.tensor_tensor(out=phi[:, 0], in0=a_b, in1=c_b, op=ALU.mult), r=[bq["alpha"], bq["colv"]], w=[bq["phi"]])
                P.op("dve", lambda e: e.tensor_scalar(out=phi[:, 1], in0=phi[:, 0], scalar1=float(np.pi / 2), scalar2=None, op0=ALU.add),
                     r=[bq["phi"]], w=[bq["phi"]])
                pflat = phi[:].rearrange("p a g c -> p (a g c)")
                reduce_2pi(pflat, nq[:, 0:2 * 64 * 33], bq["phi"], bq["nq"])
                P.op("act", lambda e: e.activation(out=tab[:, 1], in_=phi[:, 0], func=AF.Sin), r=[bq["phi"]], w=[bq["tab"]])
                P.op("act", lambda e: e.activation(out=tab[:, 0], in_=phi[:, 1], func=AF.Sin), r=[bq["phi"]], w=[bq["tab"]])
                P.op("pool", lambda e: e.memset(tab[:, 2, :, 0:1], 0.0), w=[bq["tab"]])
                P.op("dve", lambda e: e.tensor_copy(out=tab[:, 2, :, 1:33], in_=magk[:, 8, :].unsqueeze(2).to_broadcast([128, 64, 32])),
                     r=[bq["magk"], bq["tab"]], w=[bq["tab"]])
                tabs_sa = p2("tabs_sa", [128, 2, 16, NSB, 2]); b_tabs_sa = P.bufs(2, "tabs_sa")
                qi = 0
                for kk in range(3):
                    P.dma("sp", lambda e, kk=kk: e.dma_start(out=s5tab_p[kk], in_=tab[:, kk].rearrange("p g c -> p (g c)")),
                          r=[bq["tab"]], w=[b_wscr], key="q_st")
                    for g0 in range(0, 64, 16):
                        sl_ = qi % 2
                        qi += 1
                        P.op("pool", lambda e, kk=kk, g0=g0, sl_=sl_: e.tensor_copy(
                            out=tabs_sa[:, sl_], in_=tab[:, kk, g0:g0 + 16, 0:2].unsqueeze(2).to_broadcast([128, 16, NSB, 2])),
                            r=[bq["tab"]], w=[b_tabs_sa[sl_]])
                        P.dma("sp", lambda e, kk=kk, g0=g0, sl_=sl_: e.dma_start(
                            out=s5tab_s[kk, :, g0 * 32:(g0 + 16) * 32], in_=tabs_sa[:, sl_].rearrange("p g b c -> p (g b c)")),
                            r=[b_tabs_sa[sl_]], w=[b_wscr], key="q_ts%d" % sl_)

                Eb = p2("Eb", [128, 2, 8, 15, 16]); Etmp = p2("Etmp", [128, 1, 8, 8, 16]); opstg = p2("opstg", [128, 2, 4, 8, 128], BF16)
                Ctmp = nq[:, 0:1024].rearrange("p (a g r c) -> p a g r c", a=1, g=8, r=8); Cp = nq[:, 1024:2048].rearrange("p (a g r c) -> p a g r c", a=1, g=8, r=8)
                Wrev = nq[:, 2048:3072].rearrange("p (a k g) -> p a k g", a=2, k=8)
                b_Eb, b_Etmp, b_Cp, b_Ctmp = P.bufs(2, "Eb"), P.bufs(2, "Etmp"), P.bufs(2, "Cp"), P.bufs(2, "Ctmp")
                b_stage = P.bufs(2, "stage"); b_Wrev = P.buf("Wrev")
                P.op("pool", lambda e: e.memset(Eb[:], 0.0), w=b_Eb)
                phif = phi[:].rearrange("p a g c -> p (a g c)")
                Ebb = phif[:, 0:1920].bitcast(BF16).rearrange("p (a g k c) -> p a g k c", a=2, g=8, k=15)
                CAb = phif[:, 1920:1920 + 512].bitcast(BF16).rearrange("p (g c) -> p g c", c=16)
                b_Ebb = P.bufs(2, "Ebb")
                P.op("act", lambda e: e.activation(out=CAb, in_=CA[:], func=AF.Copy), r=[bq["CA"], bq["phi"], bq["tab"]], w=[bq["CA"], bq["phi"]])
                for k_ in range(8):
                    P.op("pool", lambda e, k_=k_: e.tensor_copy(out=Wrev[:, :, k_, :], in_=Wt[:, :, 7 - k_, :]), r=[bq["W"], bq["nq"]], w=[b_Wrev])
                def sB1(bt):
                        gs = slice(bt * 8, (bt + 1) * 8)
                        sl = bt % 2
                        bst_b = Bst[:, gs, :].unsqueeze(2).to_broadcast([128, 8, 8, 16])
                        bsw_b = Bsw[:, gs, :].unsqueeze(2).to_broadcast([128, 8, 8, 16])
                        wr_b = Wrev[:, 0, :, gs].rearrange("p k g -> p g k").unsqueeze(3).to_broadcast([128, 8, 8, 16])
                        wi_b = Wrev[:, 1, :, gs].rearrange("p k g -> p g k").unsqueeze(3).to_broadcast([128, 8, 8, 16])
                        P.op("dve", lambda e, sl=sl, bst_b=bst_b, wr_b=wr_b: e.tensor_tensor(out=Eb[:, sl, :, 0:8, :], in0=bst_b, in1=wr_b, op=ALU.mult),
                             r=[bq["Bst"], b_Wrev], w=[b_Eb[sl]])
                        P.op("dve", lambda e, sl=sl, bsw_b=bsw_b, wi_b=wi_b: e.tensor_tensor(out=Etmp[:, 0], in0=bsw_b, in1=wi_b, op=ALU.mult),
                             r=[bq["Bsw"], b_Wrev], w=[b_Etmp[0]])
                        P.op("dve", lambda e, sl=sl: e.tensor_tensor(out=Eb[:, sl, :, 0:8, :], in0=Eb[:, sl, :, 0:8, :], in1=Etmp[:, 0], op=ALU.add),
                             r=[b_Eb[sl], b_Etmp[0]], w=[b_Eb[sl]])
                        P.op("act", lambda e, sl=sl: e.activation(out=Ebb[:, sl].rearrange("p g k c -> p (g k c)"),
                                                                  in_=Eb[:, sl].rearrange("p g k c -> p (g k c)"), func=AF.Copy),
                             r=[b_Eb[sl], bq["phi"]], w=[b_Ebb[sl]])

                def sB2(bt):
                        gs = slice(bt * 8, (bt + 1) * 8)
                        sl = bt % 2
                        for g4 in range(2):
                            bk = 4 + g4
                            for gg in range(4):
                                g8 = g4 * 4 + gg
                                P.op("pe", lambda e, g8=g8, gg=gg, bk=bk, sl=sl: e.transpose(
                                    out=banks[bk][:, gg * 128:(gg + 1) * 128], in_=Eb[:, sl, g8, 0:8, :].rearrange("p k c -> p (k c)"),
                                    identity=identf[:]), r=[b_Eb[sl], b_ident], w=[b_bank[bk]])
                            bv = banks[bk][:].rearrange("p (g c) -> p g c", g=4)
                            P.op("act", lambda e, g4=g4, bv=bv, sl=sl: e.activation(out=opstg[:, sl, 0, g4 * 4:(g4 + 1) * 4, :], in_=bv, func=AF.Copy),
                                 r=[b_bank[bk]], w=[b_stage[sl]])
                            P.op("act", lambda e, g4=g4, bv=bv, sl=sl: e.activation(out=opstg[:, sl, 1, g4 * 4:(g4 + 1) * 4, 0:64], in_=bv[:, :, 64:128], func=AF.Copy),
                                 r=[b_bank[bk]], w=[b_stage[sl]])
                            P.op("act", lambda e, g4=g4, bv=bv, sl=sl: e.activation(out=opstg[:, sl, 1, g4 * 4:(g4 + 1) * 4, 64:128], in_=bv[:, :, 0:64], func=AF.Copy, scale=-1.0),
                                 r=[b_bank[bk]], w=[b_stage[sl]])
                        for g8 in range(8):
                            g = bt * 8 + g8
                            bk2 = 6 + (g8 % 2)
                            for r_ in range(8):
                                P.op("pe", lambda e, g8=g8, r_=r_, g=g, bk2=bk2, sl=sl: e.matmul(
                                    banks[bk2][:, r_ * 16:(r_ + 1) * 16], Ebb[:, sl, g8, 7 - r_:15 - r_, :].rearrange("p k c -> p (k c)"),
                                    CAb[:, g, :], start=True, stop=True), r=[b_Ebb[sl], bq["CA"]], w=[b_bank[bk2]])
                            P.op("dve", lambda e, g8=g8, g=g, bk2=bk2, sl=sl: e.scalar_tensor_tensor(
                                out=opstg[:, sl, 2, g8, :], in0=identf[:], scalar=Dcol[:, g:g + 1], in1=banks[bk2][:, 0:128],
                                op0=ALU.mult, op1=ALU.add), r=[b_bank[bk2], bq["Dcol"], b_ident], w=[b_stage[sl]])
                        ca_b = CA[:, gs, :].unsqueeze(2).to_broadcast([128, 8, 8, 16])
                        cb_b = CB[:, gs, :].unsqueeze(2).to_broadcast([128, 8, 8, 16])
                        lr_b = Lt[:, 0, 1:9, gs].rearrange("p k g -> p g k").unsqueeze(3).to_broadcast([128, 8, 8, 16])
                        li_b = Lt[:, 1, 1:9, gs].rearrange("p k g -> p g k").unsqueeze(3).to_broadcast([128, 8, 8, 16])
                        P.op("pool", lambda e, sl=sl, ca_b=ca_b, lr_b=lr_b: e.tensor_tensor(out=Cp[:, 0], in0=ca_b, in1=lr_b, op=ALU.mult),
                             r=[bq["CA"], bq["L"], bq["nq"]], w=[b_Cp[0]])
                        P.op("pool", lambda e, sl=sl, cb_b=cb_b, li_b=li_b: e.tensor_tensor(out=Ctmp[:, 0], in0=cb_b, in1=li_b, op=ALU.mult),
                             r=[bq["CB"], bq["L"], bq["nq"]], w=[b_Ctmp[0]])
                        P.op("pool", lambda e, sl=sl: e.tensor_tensor(
                            out=opstg[:, sl, 3].rearrange("p g (r c) -> p g r c", r=8), in0=Cp[:, 0], in1=Ctmp[:, 0], op=ALU.add),
                            r=[b_Cp[0], b_Ctmp[0]], w=[b_stage[sl]])
                        P.dma("sp", lambda e, bt=bt, sl=sl: e.dma_start(out=s5ops[bt], in_=opstg[:, sl].rearrange("p k g c -> p (k g c)")),
                              r=[b_stage[sl]], w=[b_wscr], key=f"q_so{sl}")

                sB1(0)
                for bt in range(8):
                    if bt + 1 < 8:
                        sB1(bt + 1)
                    sB2(bt)
            P.full_barrier()
        P.enabled = stage >= 9

        xres = sb("xres", [128, 1, 2, D])
        b_xres = [[P.buf(f"xres{s}{q}") for q in range(2)] for s in range(1)]
        xn = sb("xn", [128, 2, D], BF16); b_xn = P.bufs(2, "xn")
        junk = sb("junk", [128, D], BF16); b_junk = P.buf("junk")
        ssb = sb("ssb", [128, 16]); b_ss = P.bufs(16, "ss")
        hT0 = sb("hT0", [128, 1, NDT, 368], BF16); b_hT0 = P.bufs(1, "hT0")
        hist0 = sb("hist0", [128, NDT, HP], BF16); b_hist0 = P.buf("hist0")
        h2T = sb("h2T", [128, 1, NDT, HC + LB], BF16); b_h2T = P.bufs(1, "h2T")
        hist2 = sb("hist2", [128, 2, NDT, HC], BF16); b_hist2 = P.bufs(2, "hist2")
        pooled = sb("pooled", [128, NDT, LB], BF16); b_pooled = P.buf("pooled")
        pt = sb("pt", [128, 2, 2, 368]); b_pt = P.bufs(2, "pt")
        tmp_tok = sb("tmp_tok", [128, 2, D]); b_tmp = P.bufs(2, "tmp")
        NS = 3
        wus = sb("wus", [128, NS, NDT, 2, 256], BF16); wds = sb("wds", [128, NS, 2, D], BF16)
        b_ws = P.bufs(NS, "ws")
        cg = sb("cg", [128, 2, LB]); cv = sb("cv", [128, 2, LB]); b_cg = P.bufs(2, "cg"); b_cv = P.bufs(2, "cv")
        gl = sb("gl", [128, 2, LB]); b_gl = P.bufs(2, "gl")
        gv = sb("gv", [128, 3, LB], BF16); b_gv = P.bufs(3, "gv")
        cso = sb("cso", [32, 512]); b_cso = P.buf("cso")

        P.op("dve", lambda e: e.memset(ssb[:], 0.0), w=b_ss)

        for b in (b_ident, b_fmp, b_Wp, b_modp, b_mods, b_Gp, b_Gs, b_inv15, b_stT):
            b.const = True

        ss_ctr = [0]

        def new_ss():
            i = ss_ctr[0] % 16
            ss_ctr[0] += 1
            return i

        def rms_stat(src_ap, r_bufs):
            i = new_ss()
            P.op("act", lambda e, i=i, src_ap=src_ap: e.activation(out=junk[:], in_=src_ap, func=AF.Square,
                                                                   accum_out=ssb[:, i:i + 1]),
                 r=list(r_bufs) + [b_ss[i]], w=[b_junk, b_ss[i]])
            P.op("act", lambda e, i=i: e.activation(out=ssb[:, i:i + 1], in_=ssb[:, i:i + 1], func=AF.Sqrt,
                                                    scale=1.0 / D, bias=epsc[:, 0:1]), r=[b_ss[i]], w=[b_ss[i]])
            P.op("dve", lambda e, i=i: e.reciprocal(out=ssb[:, i:i + 1], in_=ssb[:, i:i + 1]), r=[b_ss[i]], w=[b_ss[i]])
            return i

        def tok_cols(view3, nseg, q, H, L):
            if nseg == 1:
                return view3[:, 0, H + q * 128:H + (q + 1) * 128]
            return view3[:, :, H:H + L]

        def prenorm(blk, slot, q, l, which, dst_tile, dst_buf, H, W):
            nseg, L = blk["nseg"], blk["L"]
            xr = xres[:, slot, q, :]
            i = rms_stat(xr, [b_xres[slot][q]])
            xs_ = q % 2
            P.op("dve", lambda e, i=i, xs_=xs_, xr=xr: e.tensor_scalar(out=xn[:, xs_, :], in0=xr, scalar1=ssb[:, i:i + 1],
                                                                       scalar2=None, op0=ALU.mult),
                 r=[b_xres[slot][q], b_ss[i]], w=[b_xn[xs_]])
            bk = 6 + (q % 2)
            for dt in range(NDT):
                P.op("pe", lambda e, dt=dt, xs_=xs_, bk=bk: e.transpose(
                    out=bank_bf(bk)[:, dt * 128:(dt + 1) * 128], in_=xn[:, xs_, dt * 128:(dt + 1) * 128],
                    identity=identb[:]), r=[b_xn[xs_], b_ident], w=[b_bank[bk]])
            a_slot, b_slot = (0, 1) if which == 0 else (2, 3)
            if blk["kind"] == "p":
                for dt in range(NDT):
                    dstv = seg3(dst_tile[:, dt, 0:nseg * W], nseg)
                    P.op("act", lambda e, dt=dt, bk=bk, dstv=dstv, l=l, a_slot=a_slot, b_slot=b_slot: e.activation(
                        out=tok_cols(dstv, nseg, q, H, L), in_=bank_bf(bk)[:, dt * 128:(dt + 1) * 128], func=AF.Identity,
                        scale=modp[:, l, a_slot, dt:dt + 1], bias=modp[:, l, b_slot, dt:dt + 1]),
                        r=[b_bank[bk], b_modp], w=[dst_buf])
                return i
            else:
                am = mods[:, l, a_slot, :, :].unsqueeze(3).to_broadcast([128, NDT, NSB, LS])
                bm = mods[:, l, b_slot, :, :].unsqueeze(3).to_broadcast([128, NDT, NSB, LS])
                P.op("dve", lambda e, bk=bk, am=am: e.tensor_tensor(
                    out=tmp_tok[:, 0, :].rearrange("p (k s c) -> p k s c", k=NDT, s=NSB),
                    in0=bank_bf(bk).rearrange("p (k s c) -> p k s c", k=NDT, s=NSB), in1=am, op=ALU.mult),
                    r=[b_bank[bk], b_mods], w=[b_tmp[0]])
                dstv = seg4(dst_tile[:, :, 0:nseg * W], nseg)[:, :, :, H:H + L]
                P.op("dve", lambda e, bm=bm, dstv=dstv: e.tensor_tensor(
                    out=dstv, in0=tmp_tok[:, 0, :].rearrange("p (k s c) -> p k s c", k=NDT, s=NSB),
                    in1=bm, op=ALU.add), r=[b_tmp[0], b_mods], w=[dst_buf])
            return i

        def resid_update(blk, slot, q, acc_banks, l, which, srcs=None, final_out=False):
            full = None
            if srcs is None:
                srcs = [(banks[acc_banks[0]][:], b_bank[acc_banks[0]]), (banks[acc_banks[1]][:], b_bank[acc_banks[1]])]
                if acc_banks[1] == acc_banks[0] + 1:
                    full = (psall[:, acc_banks[0]:acc_banks[0] + 2, :], [b_bank[acc_banks[0]], b_bank[acc_banks[1]]])
            elif srcs == "mglu":
                full = (mglu[:, q, :].rearrange("p (a c) -> p a c", a=2), [b_mglu[q]])
            Gt = Gp if blk["kind"] == "p" else Gs
            bG = b_Gp if blk["kind"] == "p" else b_Gs
            i0 = None
            i = new_ss()
            if full is not None:
                fap, fbufs = full
                P.op("act", lambda e, fap=fap, i=i: e.activation(out=junk[:].rearrange("p (a c) -> p a c", a=2), in_=fap, func=AF.Square,
                                                                 accum_out=ssb[:, i:i + 1]),
                     r=list(fbufs) + [b_ss[i]], w=[b_junk, b_ss[i]])
            else:
                j = new_ss()
                for half, col in ((0, i), (1, j)):
                    sap, sbuf_ = srcs[half]
                    P.op("act", lambda e, sap=sap, col=col: e.activation(out=junk[:, 0:512], in_=sap, func=AF.Square,
                                                                         accum_out=ssb[:, col:col + 1]),
                         r=[sbuf_, b_ss[col]], w=[b_junk, b_ss[col]])
                P.op("dve", lambda e, i=i, j=j: e.tensor_tensor(out=ssb[:, i:i + 1], in0=ssb[:, i:i + 1], in1=ssb[:, j:j + 1],
                                                                op=ALU.add), r=[b_ss[i], b_ss[j]], w=[b_ss[i]])
            P.op("act", lambda e, i=i: e.activation(out=ssb[:, i:i + 1], in_=ssb[:, i:i + 1], func=AF.Sqrt,
                                                    scale=1.0 / D, bias=epsc[:, 0:1]), r=[b_ss[i]], w=[b_ss[i]])
            P.op("dve", lambda e, i=i: e.reciprocal(out=ssb[:, i:i + 1], in_=ssb[:, i:i + 1]), r=[b_ss[i]], w=[b_ss[i]])
            ts = q % 2
            if full is not None:
                fap, fbufs = full
                P.op("dve", lambda e, fap=fap, i=i, ts=ts, Gt=Gt: e.scalar_tensor_tensor(
                    out=tmp_tok[:, ts, :].rearrange("p (a c) -> p a c", a=2), in0=fap, scalar=ssb[:, i:i + 1],
                    in1=Gt[:, l, which, :].rearrange("p (a c) -> p a c", a=2), op0=ALU.mult, op1=ALU.mult),
                    r=list(fbufs) + [b_ss[i], bG], w=[b_tmp[ts]])
            for half in range(2 if full is None else 0):
                sap, sbuf_ = srcs[half]
                P.op("dve", lambda e, sap=sap, half=half, i=i, ts=ts, Gt=Gt: e.scalar_tensor_tensor(
                    out=tmp_tok[:, ts, half * 512:(half + 1) * 512], in0=sap, scalar=ssb[:, i:i + 1],
                    in1=Gt[:, l, which, half * 512:(half + 1) * 512], op0=ALU.mult, op1=ALU.mult),
                    r=[sbuf_, b_ss[i], bG], w=[b_tmp[ts]])
            if final_out:
                P.op("dve", lambda e, ts=ts: e.tensor_tensor(out=tmp_tok[:, ts, :], in0=xres[:, slot, q, :], in1=tmp_tok[:, ts, :],
                                                             op=ALU.add), r=[b_tmp[ts], b_xres[slot][q]], w=[b_tmp[ts]])
            else:
                P.op("dve", lambda e, ts=ts: e.tensor_tensor(out=xres[:, slot, q, :], in0=xres[:, slot, q, :], in1=tmp_tok[:, ts, :],
                                                             op=ALU.add), r=[b_tmp[ts], b_xres[slot][q]], w=[b_xres[slot][q]])

        wsteps = []
        wstate = {"issued": 0, "used": 0}

        def issue_weights(upto):
            while wstate["issued"] < min(upto, len(wsteps)):
                n = wstate["issued"]
                l, i = wsteps[n]
                s = n % NS
                P.dma("sp", lambda e, l=l, i=i, s=s: e.dma_start(out=wus[:, s].rearrange("p k h c -> p (k h c)"), in_=wu_scr[l, i]),
                      r=[b_wscr], w=[b_ws[s]], key=f"ws{s}")
                P.dma("sp", lambda e, l=l, i=i, s=s: e.dma_start(out=wds[:, s].rearrange("p j d -> p (j d)"), in_=wd_scr[l, i]),
                      r=[b_wscr], w=[b_ws[s]], key=f"ws{s}")
                wstate["issued"] += 1

        blocks = []
        if do_sample:
            blocks.append(dict(kind="s", nseg=NSB, L=LS, nt=1, pb=0))
        for pb in range(n_pblocks):
            blocks.append(dict(kind="p", nseg=1, L=LB, nt=2, pb=pb))
        for blk in blocks:
            for l in range(nlayers):
                for i in range(NSS):
                    wsteps.append((l, i))

        hT0_pp = [0]
        h2T_pp = [0]

        def ffn(blk, slot, l):
            nseg, L, nt = blk["nseg"], blk["L"], blk["nt"]
            W = HC + L
            ncols = nseg * W
            cur = 0
            h2 = h2T[:, cur]
            if blk["kind"] == "p":
                if blk["pb"] == 0:
                    P.op("pool", lambda e, h2=h2: e.memset(h2[:, :, 0:HC], 0.0), w=[b_h2T[cur]])
                else:
                    P.op("pool", lambda e, h2=h2: e.tensor_copy(out=h2[:, :, 0:HC], in_=hist2[:, l]),
                         r=[b_hist2[l]], w=[b_h2T[cur]])
            if blk["kind"] == "s":
                P.op("pool", lambda e, h2=h2: e.memset(h2[:, :, 0:ncols], 0.0), w=[b_h2T[cur]])
            for q in range(nt):
                prenorm(blk, slot, q, l, 1, h2, b_h2T[cur], HC, W)
            if blk["kind"] == "p":
                P.op("pool", lambda e, h2=h2: e.tensor_copy(out=hist2[:, l], in_=h2[:, :, LB:LB + HC]),
                     r=[b_h2T[cur]], w=[b_hist2[l]])
            last_p = blk["kind"] == "p" and blk["pb"] == n_pblocks - 1

            def up_mm(i, half, bk, s):
                j = i % 2
                for kt in range(NDT):
                    outp = banks[bk][:, 0:ncols]
                    rhs = h2[:, kt, 0:ncols]
                    P.op("pe", lambda e, kt=kt, half=half, s=s, outp=outp, rhs=rhs, j=j: e.matmul(
                        outp, wus[:, s, kt, half, j * 128:(j + 1) * 128], rhs, start=(kt == 0), stop=(kt == NDT - 1)),
                        r=[b_ws[s], b_h2T[cur]], w=[b_bank[bk]])

            def elementwise(i, bkg, bkv):
                ps_ = i % 2
                for half, bk, ct, bct in ((0, bkg, cg, b_cg), (1, bkv, cv, b_cv)):
                    ft = i + half * NGT
                    upv = seg3(banks[bk][:, 0:ncols], nseg)
                    if blk["kind"] == "s":
                        P.op("act", lambda e, upv=upv, ft=ft: e.activation(out=upv[:, :, 0:HC],
                                                                           in_=seg3(stT[:, l, ft, :], nseg), func=AF.Copy),
                             r=[b_stT, b_bank[bk]], w=[b_bank[bk]])
                    cvw = seg3(ct[:, ps_, 0:nseg * L], nseg)
                    wc = [fmp[:, FM_CW + (l * 3 + k) * NFT + ft:FM_CW + (l * 3 + k) * NFT + ft + 1] for k in range(3)]
                    bcol = fmp[:, FM_CB + l * NFT + ft:FM_CB + l * NFT + ft + 1]
                    P.op("act", lambda e, upv=upv, cvw=cvw, wc=wc, bcol=bcol: e.activation(
                        out=cvw, in_=upv[:, :, 2:W], func=AF.Identity, scale=wc[2], bias=bcol),
                        r=[b_bank[bk], b_fmp], w=[bct[ps_]])
                    P.op("dve", lambda e, upv=upv, cvw=cvw, wc=wc: e.scalar_tensor_tensor(
                        out=cvw, in0=upv[:, :, 1:W - 1], scalar=wc[1], in1=cvw, op0=ALU.mult, op1=ALU.add),
                        r=[b_bank[bk], b_fmp, bct[ps_]], w=[bct[ps_]])
                    P.op("dve", lambda e, upv=upv, cvw=cvw, wc=wc: e.scalar_tensor_tensor(
                        out=cvw, in0=upv[:, :, 0:W - 2], scalar=wc[0], in1=cvw, op0=ALU.mult, op1=ALU.add),
                        r=[b_bank[bk], b_fmp, bct[ps_]], w=[bct[ps_]])
                    if blk["kind"] == "s" or last_p:
                        dstc = seg3(cstage[:, 0, ft, 0:nseg * HC], nseg)
                        P.op("act", lambda e, upv=upv, dstc=dstc: e.activation(out=dstc, in_=upv[:, :, W - HC:W], func=AF.Copy),
                             r=[b_bank[bk]], w=[b_cstage])
                P.op("act", lambda e, ps_=ps_: e.activation(out=gl[:, ps_, 0:nseg * L], in_=cg[:, ps_, 0:nseg * L], func=AF.Gelu),
                     r=[b_cg[ps_]], w=[b_gl[ps_]])
                g3 = i % 3
                P.op("dve", lambda e, ps_=ps_, g3=g3: e.tensor_tensor(out=gv[:, g3, 0:nseg * L], in0=gl[:, ps_, 0:nseg * L],
                                                                       in1=cv[:, ps_, 0:nseg * L], op=ALU.mult),
                     r=[b_gl[ps_], b_cv[ps_]], w=[b_gv[g3]])

            def down_mm(i, s):
                g3 = i % 3
                j = i % 2
                for q in range(nt):
                    for half in range(2):
                        bk = 2 * q + half
                        P.op("pe", lambda e, q=q, half=half, bk=bk, s=s, g3=g3, j=j: e.matmul(
                            banks[bk][:], gv[:, g3, q * 128:(q + 1) * 128], wds[:, s, j, half * 512:(half + 1) * 512],
                            start=(i == 0), stop=(i == NGT - 1)), r=[b_gv[g3], b_ws[s]], w=[b_bank[bk]])

            base = wstate["used"]
            ahead = NS if blk["kind"] == "p" else NS - 1
            issue_weights(base + ahead)
            pend = None
            for i in range(NGT):
                n = base + i // 2
                s = n % NS
                bkg, bkv = (4, 5) if i % 2 == 0 else (6, 7)
                up_mm(i, 0, bkg, s)
                up_mm(i, 1, bkv, s)
                elementwise(i, bkg, bkv)
                if pend is not None:
                    down_mm(*pend)
                    issue_weights(base + i // 2 + ahead)
                pend = (i, s)
            down_mm(*pend)
            wstate["used"] = base + NSS
            issue_weights(wstate["used"] + ahead)
            for q in range(nt):
                resid_update(blk, slot, q, (2 * q, 2 * q + 1), l, 1, final_out=(l == nlayers - 1))

        def pool_mixer(blk, slot, l):
            nseg, L, nt = blk["nseg"], blk["L"], blk["nt"]
            W = HP + L
            ncols = nseg * W
            cur = 0
            hT = hT0[:, cur]
            if blk["kind"] == "p":
                if blk["pb"] == 0:
                    P.op("pool", lambda e, hT=hT: e.memset(hT[:, :, 0:HP], 0.0), w=[b_hT0[cur]])
                else:
                    P.op("pool", lambda e, hT=hT: e.tensor_copy(out=hT[:, :, 0:HP], in_=hist0[:]),
                         r=[b_hist0], w=[b_hT0[cur]])
            else:
                for hb in range(2):
                    P.dma("sp", lambda e, hb=hb: e.dma_start(out=tmp_tok[0:120, hb, :],
                                                             in_=spool[hb * 8:(hb + 1) * 8].rearrange("b k d -> (b k) d")),
                          w=[b_tmp[hb]], key=f"tmp{hb}")
                    P.op("dve", lambda e, hb=hb: e.tensor_copy(out=xn[0:120, hb, :], in_=tmp_tok[0:120, hb, :]),
                         r=[b_tmp[hb]], w=[b_xn[hb]])
                    bk = 6 + hb
                    for dt in range(NDT):
                        P.op("pe", lambda e, hb=hb, dt=dt, bk=bk: e.transpose(
                            out=bank_bf(bk)[:, dt * 128:dt * 128 + 120], in_=xn[0:120, hb, dt * 128:(dt + 1) * 128],
                            identity=identb[0:120, 0:120]), r=[b_xn[hb], b_ident], w=[b_bank[bk]])
                    dstv = seg4(hT[:, :, 0:ncols], nseg)[:, :, hb * 8:(hb + 1) * 8, 0:HP]
                    P.op("act", lambda e, bk=bk, dstv=dstv: e.activation(
                        out=dstv, in_=bank_bf(bk).rearrange("p (k c) -> p k c", k=NDT)[:, :, 0:120].rearrange(
                            "p k (b c) -> p k b c", b=8), func=AF.Copy), r=[b_bank[bk]], w=[b_hT0[cur]])
            for q in range(nt):
                i_ss = prenorm(blk, slot, q, l, 0, hT, b_hT0[cur], HP, W)
                if blk["kind"] == "s" or (blk["pb"] == n_pblocks - 1 and q == nt - 1):
                    new_pool_rows(blk, slot, q, l, i_ss)
            if blk["kind"] == "p":
                P.op("pool", lambda e, hT=hT: e.tensor_copy(out=hist0[:], in_=hT[:, :, LB:LB + HP]),
                     r=[b_hT0[cur]], w=[b_hist0])
            for gi, w_ in enumerate((2, 4, 8, 16)):
                src = seg4(hT[:, 2 * gi:2 * gi + 2, 0:ncols], nseg)
                ta = seg4(pt[:, 0, :, 0:ncols], nseg)
                tb = seg4(pt[:, 1, :, 0:ncols], nseg)
                P.op("dve", lambda e, src=src, ta=ta: e.tensor_tensor(out=ta[:, :, :, 1:W], in0=src[:, :, :, 1:W],
                                                                       in1=src[:, :, :, 0:W - 1], op=ALU.add),
                     r=[b_hT0[cur]], w=[b_pt[0]])
                curt, curb, oth, othb = ta, b_pt[0], tb, b_pt[1]
                lo = 1
                sh = 2
                while sh < w_:
                    P.op("dve", lambda e, curt=curt, oth=oth, lo=lo, sh=sh: e.tensor_tensor(
                        out=oth[:, :, :, lo + sh:W], in0=curt[:, :, :, lo + sh:W], in1=curt[:, :, :, lo:W - sh], op=ALU.add),
                        r=[curb], w=[othb])
                    curt, curb, oth, othb = oth, othb, curt, curb
                    lo += sh
                    sh *= 2
                pv = seg4(pooled[:, 2 * gi:2 * gi + 2, 0:nseg * L], nseg)
                P.op("dve", lambda e, curt=curt, src=src, pv=pv, w_=w_: e.scalar_tensor_tensor(
                    out=pv, in0=curt[:, :, :, HP:W], scalar=1.0 / w_, in1=src[:, :, :, HP:W], op0=ALU.mult, op1=ALU.subtract),
                    r=[curb, b_hT0[cur]], w=[b_pooled])
                if blk["kind"] == "p" and blk["pb"] == 0:
                    for d2 in range(2):
                        P.op("dve", lambda e, curt=curt, gi=gi, d2=d2: e.tensor_tensor(
                            out=pt[:, 0, d2, 0:HP], in0=curt[:, d2, 0, HP:2 * HP], in1=inv15[:, gi, :], op=ALU.mult),
                            r=[curb, b_inv15], w=[b_pt[0]] if curb is not b_pt[0] else [b_pt[0]])
                        P.op("dve", lambda e, gi=gi, d2=d2, src=src: e.tensor_tensor(
                            out=pooled[:, 2 * gi + d2, 0:HP], in0=pt[:, 0, d2, 0:HP], in1=src[:, d2, 0, HP:2 * HP], op=ALU.subtract),
                            r=[b_pt[0], b_hT0[cur]], w=[b_pooled])
            for q in range(nt):
                for gi in range(4):
                    bk = 2 * q + gi // 2
                    for kt in range(2):
                        pv = seg3(pooled[:, 2 * gi + kt, 0:nseg * L], nseg)
                        lhsT = tok_cols(pv, nseg, q, 0, L)
                        P.op("pe", lambda e, gi=gi, kt=kt, bk=bk, lhsT=lhsT: e.matmul(
                            banks[bk][:, (gi % 2) * 256:(gi % 2 + 1) * 256], lhsT, Wp[:, gi, kt, :],
                            start=(kt == 0), stop=(kt == 1)), r=[b_pooled, b_Wp], w=[b_bank[bk]])
                resid_update(blk, slot, q, (2 * q, 2 * q + 1), l, 0)


        def new_pool_rows(blk, slot, q, l, i):
            isp = blk["kind"] == "p"
            xr = xres[:, slot, q, :]
            P.op("dve", lambda e, xr=xr, i=i: e.tensor_scalar(out=tmp_tok[:, 0, :], in0=xr, scalar1=ssb[:, i:i + 1], scalar2=None, op0=ALU.mult),
                 r=[b_xres[slot][q], b_ss[i]], w=[b_tmp[0]])
            for dt in range(NDT):
                bk = 4 + dt // 4
                P.op("pe", lambda e, dt=dt, bk=bk: e.transpose(out=banks[bk][:, (dt % 4) * 128:(dt % 4 + 1) * 128],
                                                               in_=tmp_tok[:, 0, dt * 128:(dt + 1) * 128], identity=identf[:]),
                     r=[b_tmp[0], b_ident], w=[b_bank[bk]])
            hTf = tmp_tok[:, 1, :].rearrange("p (k c) -> p k c", k=NDT)
            if isp:
                for dt in range(NDT):
                    bk = 4 + dt // 4
                    P.op("act", lambda e, dt=dt, bk=bk: e.activation(
                        out=hTf[:, dt, :], in_=banks[bk][:, (dt % 4) * 128:(dt % 4 + 1) * 128], func=AF.Identity,
                        scale=modp[:, l, 0, dt:dt + 1], bias=modp[:, l, 1, dt:dt + 1]), r=[b_bank[bk], b_modp], w=[b_tmp[1]])
            else:
                am = mods[:, l, 0, :, :].unsqueeze(3).to_broadcast([128, NDT, NSB, LS])
                bm = mods[:, l, 1, :, :].unsqueeze(3).to_broadcast([128, NDT, NSB, LS])
                for h4 in range(2):
                    bk = 4 + h4
                    hv = tmp_tok[:, 1, h4 * 512:(h4 + 1) * 512].rearrange("p (k s c) -> p k s c", k=4, s=NSB)
                    P.op("dve", lambda e, bk=bk, hv=hv, am=am, h4=h4: e.tensor_tensor(
                        out=hv, in0=banks[bk][:].rearrange("p (k s c) -> p k s c", k=4, s=NSB), in1=am[:, h4 * 4:(h4 + 1) * 4], op=ALU.mult),
                        r=[b_bank[bk], b_mods], w=[b_tmp[1]])
                    P.op("dve", lambda e, hv=hv, bm=bm, h4=h4: e.tensor_tensor(out=hv, in0=hv, in1=bm[:, h4 * 4:(h4 + 1) * 4], op=ALU.add),
                         r=[b_tmp[1], b_mods], w=[b_tmp[1]])
            for dt in range(NDT):
                bk = 6 + dt // 4
                P.op("pe", lambda e, dt=dt, bk=bk: e.transpose(out=banks[bk][:, (dt % 4) * 128:(dt % 4 + 1) * 128],
                                                               in_=hTf[:, dt, :], identity=identf[:]),
                     r=[b_tmp[1], b_ident], w=[b_bank[bk]])
            for h4 in range(2):
                P.op("act", lambda e, h4=h4: e.activation(out=tmp_tok[:, 0, h4 * 512:(h4 + 1) * 512], in_=banks[6 + h4][:], func=AF.Copy),
                     r=[b_bank[6 + h4]], w=[b_tmp[0]])
            if isp:
                P.dma("sp", lambda e: e.dma_start(out=npp[:, :], in_=tmp_tok[128 - HP:128, 0, :]), r=[b_tmp[0]], key="tmp0", store=True)
            else:
                for b_ in range(NSB):
                    P.dma("sp", lambda e, b_=b_: e.dma_start(out=nps[b_, HP - LS:HP, :], in_=tmp_tok[b_ * LS:(b_ + 1) * LS, 0, :]),
                          r=[b_tmp[0]], key="tmp0", store=True)
                P.dma("sp", lambda e: e.dma_start(out=nps[:, 0:HP - LS, :], in_=spool[:, LS:HP, :]), key="npsd", store=True)

        def conv_out(blk, l):
            isp = blk["kind"] == "p"
            ncol = 2 if isp else NSB * HC
            dst = ncp[l] if isp else ncs[l].rearrange("b k f -> (b k) f")
            for f0 in range(0, NFT, 4):
                for j in range(4):
                    P.op("pe", lambda e, f0=f0, j=j: e.transpose(out=banks[4][0:ncol, j * 128:(j + 1) * 128],
                                                                 in_=cstage[:, 0, f0 + j, 0:ncol], identity=identf[:]),
                         r=[b_cstage, b_ident], w=[b_bank[4]])
                P.op("act", lambda e: e.activation(out=cso[0:ncol, :], in_=banks[4][0:ncol, :], func=AF.Copy), r=[b_bank[4]], w=[b_cso])
                P.dma("sp", lambda e, f0=f0, dst=dst: e.dma_start(out=dst[:, f0 * 128:(f0 + 4) * 128], in_=cso[0:ncol, :]),
                      r=[b_cso], key="cso", store=True)

        if not skip_s5 and nlayers > 1:
            hT1 = sb("hT1", [128, NDT, LB], BF16); b_hT1 = P.buf("hT1")
            opsb = sb("opsb", [128, 2, 4, 8, 128], BF16); tabs = sb("tabs", [128, 2, 3 * 8 * 33]); b_opsb = P.bufs(2, "opsb")
            h_cm = sb("h_cm", [32, 8, 128], BF16); b_hcm = P.buf("hcm")
            Ub = sb("Ub", [128, 2, 8, 32], BF16); b_U = P.bufs(2, "U")
            s5t = sb("s5t", [128, 4, 264]); b_s5t = P.bufs(4, "s5t")
            Xb = sb("Xb", [128, 264], BF16); b_Xb = P.buf("Xb")
            carry = sb("carry", [128, 64]); b_carry = P.buf("carry")
            x0b = sb("x0b", [128, 8, NSB]); b_x0b = P.buf("x0b")
            x0tok = sb("x0tok", [NSB, 8, 128]); b_x0tok = P.buf("x0tok")
            xfin = sb("xfin", [128, 64, NSB]); b_xfin = P.buf("xfin")
            gy = sb("gy", [128, 8 * 32], BF16); b_gy = P.buf("gy")
            gy_cm = sb("gy_cm", [32, 8, 128], BF16); b_gycm = P.buf("gycm")
            gyT = sb("gyT", [128, NDT, LB], BF16); b_gyT = P.buf("gyT")
            perm = sb("perm", [128, 128]); b_perm = P.buf("perm")
            mglu = sb("mglu", [128, 2, D]); b_mglu = P.bufs(2, "mglu")
            b_sg = b_pt
            finsb = mglu[0:64].rearrange("p q (b c) -> p (q b) c", c=128)
            P.op("dve", lambda e: e.tensor_copy(out=perm[:, 0:64], in_=identf[:, 64:128]), r=[b_ident], w=[b_perm])
            P.op("dve", lambda e: e.tensor_scalar(out=perm[:, 64:128], in0=identf[:, 0:64], scalar1=-1.0, scalar2=None, op0=ALU.mult),
                 r=[b_ident], w=[b_perm])
            P.op("pool", lambda e: e.memset(carry[:], 0.0), w=[b_carry])
            stTf = stT[:].rearrange("p l f c -> p (l f c)")
            xff = xfin[:].rearrange("p g b -> p (g b)")
            s5_opsv = [opsb[:, 0], opsb[:, 1], stTf[:, 0:2048].bitcast(BF16).rearrange("p (k g c) -> p k g c", k=4, g=8)]
            s5_tabv = [tabs[:, 0], tabs[:, 1], xff[:, 0:792]]
            s5_bops = [b_opsb[0], b_opsb[1], P.buf("opsb2")]
            s5_Uv = [Ub[:, 0], Ub[:, 1], xff[:, 792:920].bitcast(BF16).rearrange("p (g j) -> p g j", g=8)]
            s5_bU = [b_U[0], b_U[1], P.buf("U2")]
            s5_Xbv = [Xb[:], stTf[:, 2048:2180].bitcast(BF16)]
            s5_bXb = [b_Xb, P.buf("Xb1")]
            s5_bglu = P.bufs(2, "glu")

        def s5_alias_guard():
            P.op("act", lambda e: e.activation(out=epsc[:, 0:1], in_=epsc[:, 0:1], func=AF.Copy),
                 r=[b_xfin, b_stT], w=[b_xfin, s5_bops[2], s5_bU[2], s5_bXb[1]])
            P.op("dve", lambda e: e.tensor_copy(out=ssb[:, 15:16], in_=ssb[:, 15:16]), r=[b_ss[15]], w=[b_ss[15], s5_bglu[0], s5_bglu[1]])

        s5_pref = {"ops": set(), "glu": False}

        def s5_prefetch():
            n_ = 8 * 33
            for bt in (0, 1):
                ov, tv, bo = s5_opsv[bt % 3], s5_tabv[bt % 3], s5_bops[bt % 3]
                key = f"opsb{bt % 3}"
                P.dma("sp", lambda e, bt=bt, ov=ov: e.dma_start(out=ov.rearrange("p k g c -> p (k g c)"), in_=s5ops[bt]),
                      r=[b_wscr], w=[bo], key=key)
                tsrc = s5tab_p[:, :, bt * n_:(bt + 1) * n_].rearrange("k p c -> p k c")
                P.dma("sp", lambda e, tv=tv, tsrc=tsrc: e.dma_start(out=tv[:, 0:3 * n_].rearrange("p (k c) -> p k c", k=3), in_=tsrc),
                      r=[b_wscr], w=[bo], key=key)
                s5_pref["ops"].add(bt)
            for ab in range(2):
                gv_ = Gs[:].rearrange("p a b d -> p (a b d)")[:, ab * 2048:(ab + 1) * 2048].bitcast(BF16).rearrange("p (k c) -> p k c", k=NDT)
                P.dma("sp", lambda e, ab=ab, gv_=gv_: e.dma_start(
                    out=gv_, in_=wg_scr[ab].rearrange("p (kt n) -> p kt n", kt=NDT)[:, :, 0:512]),
                    r=[b_wscr], w=[s5_bglu[ab]], key="glu%d" % ab)
            s5_pref["glu"] = True

        def s5_mixer(blk, slot, l):
            nseg, L, nt = blk["nseg"], blk["L"], blk["nt"]
            isp = blk["kind"] == "p"
            ntok = nseg * L
            NCH = ntok // 8
            CW = 33 if isp else 32
            n = 8 * CW
            NR = 3 if isp else 2
            NX = 2 if isp else 1
            for q in range(nt):
                prenorm(blk, slot, q, l, 0, hT1, b_hT1, 0, L)
            if isp and blk["pb"] == 0:
                P.op("pool", lambda e: e.memset(carry[:], 0.0), r=[b_carry], w=[b_carry])

            def v3(ap):
                if isp:
                    return ap.rearrange("p (g c) -> p g c", g=8)
                return ap.rearrange("p (g b c) -> p g b c", g=8, b=NSB)

            def ops_v(bt):
                return s5_opsv[bt % NR], s5_tabv[bt % NR], s5_bops[bt % NR]

            def U_v(bt):
                return s5_Uv[bt % NR], s5_bU[bt % NR]

            def Xb_v(bt):
                return s5_Xbv[bt % NX], s5_bXb[bt % NX]

            def load_ops(bt):
                ov, tv, bo = ops_v(bt)
                key = f"opsb{bt % NR}"
                P.dma("sp", lambda e, bt=bt, ov=ov: e.dma_start(out=ov.rearrange("p k g c -> p (k g c)"), in_=s5ops[bt]),
                      r=[b_wscr], w=[bo], key=key)
                tsrc = (s5tab_p if isp else s5tab_s)[:, :, bt * n:(bt + 1) * n].rearrange("k p c -> p k c")
                P.dma("sp", lambda e, tv=tv, tsrc=tsrc: e.dma_start(out=tv[:, 0:3 * n].rearrange("p (k c) -> p k c", k=3), in_=tsrc),
                      r=[b_wscr], w=[bo], key=key)

            def s1a(bt):
                if isp and bt in s5_pref["ops"]:
                    s5_pref["ops"].discard(bt)
                else:
                    load_ops(bt)
                for r_ in range(8):
                    P.op("pe", lambda e, r_=r_, bt=bt: e.transpose(out=bank_bf(0)[0:NCH, r_ * 128:(r_ + 1) * 128],
                                                                   in_=hT1[:, bt, r_:ntok:8], identity=identb[:]),
                         r=[b_hT1, b_ident], w=[b_bank[0]])
                P.op("act", lambda e: e.activation(
                    out=h_cm[0:NCH].rearrange("p g (r c) -> p r g c", r=8),
                    in_=bank_bf(0)[0:NCH, :].rearrange("p (r g c) -> p r g c", r=8, g=8), func=AF.Copy),
                     r=[b_bank[0]], w=[b_hcm])
                for g8 in range(8):
                    P.op("pe", lambda e, g8=g8: e.transpose(out=bank_bf(1)[:, g8 * NCH:(g8 + 1) * NCH],
                                                            in_=h_cm[0:NCH, g8, :], identity=identb[0:NCH, 0:NCH]),
                         r=[b_hcm, b_ident], w=[b_bank[1]])
                Uv, bU = U_v(bt)
                P.op("act", lambda e, Uv=Uv: e.activation(out=Uv[:, :, 0:NCH], in_=bank_bf(1)[:, 0:8 * NCH].rearrange("p (g j) -> p g j", g=8),
                                                          func=AF.Copy), r=[b_bank[1]], w=[bU])

            def s1b(bt):
                ov, tv, bo = ops_v(bt)
                Uv, bU = U_v(bt)
                for kind, bk in ((0, 2), (1, 3)):
                    for g8 in range(8):
                        pv = v3(banks[bk][:, 0:n])
                        outp = pv[:, g8, 1:33] if isp else pv[:, g8, :, 1]
                        P.op("pe", lambda e, kind=kind, g8=g8, outp=outp, ov=ov, Uv=Uv: e.matmul(
                            outp, ov[:, kind, g8, :], Uv[:, g8, 0:NCH], start=True, stop=True),
                            r=[bo, bU], w=[b_bank[bk]])

            def s2(bt):
                ov, tv, bo = ops_v(bt)
                Xbv, bXb = Xb_v(bt)
                if not isp:
                    for hh, src in enumerate((sre, sim)):
                        P.dma("sp", lambda e, hh=hh, src=src, bt=bt: e.dma_start(out=x0tok[:, :, hh * 64:(hh + 1) * 64],
                                                                                 in_=src[:, bt * 8:(bt + 1) * 8, :]),
                              w=[b_x0tok], key="x0tok")
                    for g8 in range(8):
                        P.op("pe", lambda e, g8=g8: e.transpose(out=banks[7][:, g8 * NSB:(g8 + 1) * NSB], in_=x0tok[:, g8, :],
                                                                identity=identf[0:NSB, 0:NSB]), r=[b_x0tok, b_ident], w=[b_bank[7]])
                    P.op("act", lambda e: e.activation(out=x0b[:].rearrange("p g b -> p (g b)"), in_=banks[7][:, 0:8 * NSB], func=AF.Copy),
                         r=[b_bank[7]], w=[b_x0b])
                COS = tv[:, 0:n]; SIN = tv[:, n:2 * n]; M2 = tv[:, 2 * n:3 * n]
                t1 = s5t[:, 0, 0:n]; Sp = s5t[:, 1, 0:n]; V = s5t[:, 2, 0:n]; X = s5t[:, 3, 0:n]
                def nc_(ap):
                    return v3(ap)[:, :, 1:33] if isp else v3(ap)[:, :, :, 1]
                P.op("dve", lambda e, COS=COS, t1=t1: e.tensor_tensor(out=nc_(t1), in0=nc_(banks[2][:, 0:n]), in1=nc_(COS), op=ALU.mult),
                     r=[b_bank[2], bo], w=[b_s5t[0]])
                P.op("dve", lambda e, SIN=SIN, Sp=Sp: e.tensor_tensor(out=nc_(Sp), in0=nc_(banks[3][:, 0:n]), in1=nc_(SIN), op=ALU.mult),
                     r=[b_bank[3], bo], w=[b_s5t[1]])
                P.op("dve", lambda e, Sp=Sp, t1=t1: e.tensor_tensor(out=nc_(Sp), in0=nc_(Sp), in1=nc_(t1), op=ALU.add),
                     r=[b_s5t[0], b_s5t[1]], w=[b_s5t[1]])
                if isp:
                    P.op("dve", lambda e, Sp=Sp, bt=bt: e.tensor_copy(out=v3(Sp)[:, :, 0], in_=carry[:, bt * 8:(bt + 1) * 8]),
                         r=[b_carry, b_s5t[1]], w=[b_s5t[1]])
                else:
                    P.op("dve", lambda e, Sp=Sp: e.tensor_copy(out=v3(Sp)[:, :, :, 0], in_=x0b[:]),
                         r=[b_x0b, b_s5t[1]], w=[b_s5t[1]])
                P.op("dve", lambda e, M2=M2, Sp=Sp, V=V: e.tensor_tensor_scan(out=V, data0=M2, data1=Sp, initial=0.0,
                                                                             op0=ALU.mult, op1=ALU.add),
                     r=[b_s5t[1], bo], w=[b_s5t[2]])
                P.op("pe", lambda e, V=V: e.matmul(banks[4][:, 0:n], perm[:], V, start=True, stop=True),
                     r=[b_perm, b_s5t[2]], w=[b_bank[4]])
                P.op("dve", lambda e, COS=COS, t1=t1, V=V: e.tensor_tensor(out=t1, in0=V, in1=COS, op=ALU.mult),
                     r=[b_s5t[2], bo, b_s5t[0]], w=[b_s5t[0]])
                P.op("dve", lambda e, SIN=SIN, X=X: e.tensor_tensor(out=X, in0=banks[4][:, 0:n], in1=SIN, op=ALU.mult),
                     r=[b_bank[4], bo], w=[b_s5t[3]])
                P.op("dve", lambda e, X=X, t1=t1: e.tensor_tensor(out=X, in0=t1, in1=X, op=ALU.subtract),
                     r=[b_s5t[0], b_s5t[3]], w=[b_s5t[3]])
                P.op("act", lambda e, X=X, Xbv=Xbv: e.activation(out=Xbv[:, 0:n], in_=X, func=AF.Copy), r=[b_s5t[3]], w=[bXb])
                if isp:
                    P.op("pool", lambda e, X=X, bt=bt: e.tensor_copy(out=carry[:, bt * 8:(bt + 1) * 8], in_=v3(X)[:, :, 32]),
                         r=[b_s5t[3], b_carry], w=[b_carry])
                else:
                    P.op("pool", lambda e, X=X, bt=bt: e.tensor_copy(out=xfin[:, bt * 8:(bt + 1) * 8, :], in_=v3(X)[:, :, :, 1]),
                         r=[b_s5t[3], b_xfin], w=[b_xfin])

            def s3(bt):
                ov, tv, bo = ops_v(bt)
                Uv, bU = U_v(bt)
                Xbv, bXb = Xb_v(bt)
                for g8 in range(8):
                    xv = v3(Xbv[:, 0:n])
                    xprev = xv[:, g8, 0:32] if isp else xv[:, g8, :, 0]
                    P.op("pe", lambda e, g8=g8, ov=ov, Uv=Uv: e.matmul(banks[5][:, g8 * NCH:(g8 + 1) * NCH], ov[:, 2, g8, :],
                                                                       Uv[:, g8, 0:NCH], start=True, stop=False),
                         r=[bo, bU], w=[b_bank[5]])
                    P.op("pe", lambda e, g8=g8, ov=ov, xprev=xprev: e.matmul(banks[5][:, g8 * NCH:(g8 + 1) * NCH], ov[:, 3, g8, :],
                                                                             xprev, start=False, stop=True),
                         r=[bo, bXb], w=[b_bank[5]])
                P.op("act", lambda e: e.activation(out=gy[:, 0:8 * NCH], in_=banks[5][:, 0:8 * NCH], func=AF.Gelu),
                     r=[b_bank[5]], w=[b_gy])
                for g8 in range(8):
                    P.op("pe", lambda e, g8=g8: e.transpose(out=bank_bf(6)[0:NCH, g8 * 128:(g8 + 1) * 128],
                                                            in_=gy[:, g8 * NCH:(g8 + 1) * NCH], identity=identb[:]),
                         r=[b_gy, b_ident], w=[b_bank[6]])
                P.op("act", lambda e: e.activation(
                    out=gy_cm[0:NCH].rearrange("p r (g c) -> p r g c", g=8),
                    in_=bank_bf(6)[0:NCH, :].rearrange("p (g r c) -> p r g c", g=8, r=8), func=AF.Copy),
                    r=[b_bank[6]], w=[b_gycm])
                for r_ in range(8):
                    P.op("pe", lambda e, r_=r_: e.transpose(out=bank_bf(7)[:, r_ * NCH:(r_ + 1) * NCH], in_=gy_cm[0:NCH, r_, :],
                                                            identity=identb[0:NCH, 0:NCH]), r=[b_gycm, b_ident], w=[b_bank[7]])
                P.op("dve", lambda e, bt=bt: e.tensor_copy(
                    out=gyT[:, bt, 0:ntok].rearrange("p (j r) -> p r j", r=8),
                    in_=bank_bf(7)[:, 0:8 * NCH].rearrange("p (r j) -> p r j", r=8)), r=[b_bank[7]], w=[b_gyT])

            if isp:
                if not s5_pref["glu"]:
                    for ab in range(2):
                        gv_ = Gs[:].rearrange("p a b d -> p (a b d)")[:, ab * 2048:(ab + 1) * 2048].bitcast(BF16).rearrange("p (k c) -> p k c", k=NDT)
                        P.dma("sp", lambda e, ab=ab, gv_=gv_: e.dma_start(
                            out=gv_, in_=wg_scr[ab].rearrange("p (kt n) -> p kt n", kt=NDT)[:, :, 0:512]),
                            r=[b_wscr], w=[s5_bglu[ab]], key="glu%d" % ab)
                s5_pref["glu"] = False
                s1a(0)
                s1b(0)
                for k in range(8):
                    if k + 1 < 8:
                        s1a(k + 1)
                    s2(k)
                    if k + 1 < 8:
                        s1b(k + 1)
                    if k >= 1:
                        s3(k - 1)
                s3(7)
            else:
                s1a(0)
                s1b(0)
                for k in range(8):
                    if k + 1 < 8:
                        s1a(k + 1)
                    s2(k)
                    if k + 1 < 8:
                        s1b(k + 1)
                    s3(k)
            wgsrc = [wg_scr[ab].rearrange("p (kt n) -> p kt n", kt=NDT) for ab in range(2)]
            if isp:
                gviews = [Gs[:].rearrange("p a b d -> p (a b d)")[:, sl_ * 2048:(sl_ + 1) * 2048].bitcast(BF16).rearrange(
                    "p (k c) -> p k c", k=NDT) for sl_ in range(2)]
                gbufs = s5_bglu
            else:
                gsl = (wstate["used"] + NS - 1) % NS
                gviews = [wus[:, gsl].rearrange("p k h c -> p k (h c)")] * 2
                gbufs = [b_ws[gsl], b_ws[gsl]]
            for hf in range(2):
                for ab in range(2):
                    if not (isp and hf == 0):
                        P.dma("sp", lambda e, ab=ab, hf=hf: e.dma_start(out=gviews[ab], in_=wgsrc[ab][:, :, hf * 512:(hf + 1) * 512]),
                              r=[b_wscr], w=[gbufs[ab]], key=("glu%d" % ab) if isp else f"ws{gsl}")
                    for q in range(nt):
                        bk = 2 * q + ab
                        for kt in range(NDT):
                            P.op("pe", lambda e, q=q, kt=kt, bk=bk, ab=ab: e.matmul(
                                banks[bk][:], gyT[:, kt, q * 128:(q + 1) * 128], gviews[ab][:, kt, :],
                                start=(kt == 0), stop=(kt == NDT - 1)), r=[b_gyT, gbufs[ab]], w=[b_bank[bk]])
                for q in range(nt):
                    P.op("act", lambda e, q=q: e.activation(out=pt[:, q].rearrange("p a c -> p (a c)")[:, 0:512], in_=banks[2 * q + 1][:], func=AF.Sigmoid),
                         r=[b_bank[2 * q + 1]], w=[b_sg[q]])
                    P.op("dve", lambda e, q=q, hf=hf: e.tensor_tensor(out=mglu[:, q, hf * 512:(hf + 1) * 512], in0=banks[2 * q][:],
                                                                     in1=pt[:, q].rearrange("p a c -> p (a c)")[:, 0:512], op=ALU.mult),
                         r=[b_bank[2 * q], b_sg[q]], w=[b_mglu[q]])
            for q in range(nt):
                resid_update(blk, slot, q, None, l, 0, srcs="mglu")

        def s5_outputs_sample():
            for b4 in range(0, NSB, 4):
                for bb in range(4):
                    b_ = b4 + bb
                    P.op("pe", lambda e, b_=b_, bb=bb: e.transpose(out=banks[4][0:64, bb * 128:(bb + 1) * 128], in_=xfin[:, :, b_],
                                                                   identity=identf[:]), r=[b_xfin, b_ident], w=[b_bank[4]])
                P.op("act", lambda e, b4=b4: e.activation(out=finsb[:, b4:b4 + 4, :].rearrange("p b c -> p (b c)"), in_=banks[4][0:64, :],
                                                          func=AF.Copy), r=[b_bank[4]], w=[b_mglu[0], b_mglu[1]])
            P.dma("sp", lambda e: e.dma_start(out=nrs.rearrange("b g p -> g b p"), in_=finsb[:, :, 0:64]), r=[b_mglu[0], b_mglu[1]], key="finsb", store=True)
            P.dma("sp", lambda e: e.dma_start(out=nis.rearrange("b g p -> g b p"), in_=finsb[:, :, 64:128]), r=[b_mglu[0], b_mglu[1]], key="finsb", store=True)

        def s5_outputs_prompt():
            P.op("pe", lambda e: e.transpose(out=banks[4][0:64, 0:128], in_=carry[:], identity=identf[:]),
                 r=[b_carry, b_ident], w=[b_bank[4]])
            P.op("act", lambda e: e.activation(out=finsb[:, 0, :], in_=banks[4][0:64, 0:128], func=AF.Copy), r=[b_bank[4], b_mglu[0], b_mglu[1]], w=[b_mglu[0], b_mglu[1]])
            P.dma("sp", lambda e: e.dma_start(out=nrp[:, :], in_=finsb[:, 0, 0:64]), r=[b_mglu[0], b_mglu[1]], key="finsb", store=True)
            P.dma("sp", lambda e: e.dma_start(out=nip[:, :], in_=finsb[:, 0, 64:128]), r=[b_mglu[0], b_mglu[1]], key="finsb", store=True)

        slot = 0
        for bi, blk in enumerate(blocks):
            nseg, L, nt = blk["nseg"], blk["L"], blk["nt"]
            for q in range(nt):
                src = xs[:, :] if blk["kind"] == "s" else xp[blk["pb"] * LB + q * 128: blk["pb"] * LB + (q + 1) * 128, :]
                P.dma("sp", lambda e, q=q, src=src, slot=slot: e.dma_start(out=xres[:, slot, q, :], in_=src),
                      w=[b_xres[slot][q]], key=f"x{slot}{q}")
            if blk["kind"] == "p" and not skip_s5 and nlayers > 1:
                s5_prefetch()
            for l in range(nlayers):
                if l % 2 == 0:
                    pool_mixer(blk, slot, l)
                else:
                    if not skip_s5:
                        s5_mixer(blk, slot, l)
                        if blk["kind"] == "s":
                            s5_outputs_sample()
                        elif blk["pb"] == n_pblocks - 1:
                            s5_outputs_prompt()
                ffn(blk, slot, l)
                if blk["kind"] == "s" or blk["pb"] == n_pblocks - 1:
                    conv_out(blk, l)
            for q in range(nt):
                dst = ys[:, :] if blk["kind"] == "s" else yp[blk["pb"] * LB + q * 128: blk["pb"] * LB + (q + 1) * 128, :]
                P.dma("sp", lambda e, q=q, dst=dst: e.dma_start(out=dst, in_=tmp_tok[:, q % 2, :]),
                      r=[b_tmp[q % 2]], key=f"tmp{q % 2}", store=True)
            if blk["kind"] == "s" and not skip_s5 and nlayers > 1:
                s5_alias_guard()
            slot = 0

        P.enabled = True
        P.barrier_wait("sp", P.stores)
        P.full_barrier()
        P.emit(nc, st)
    return nc


_NC_CACHE = {}


def kernel(**inputs):
    f32 = lambda a: np.ascontiguousarray(np.asarray(a, dtype=np.float32))
    inp = {k: f32(v) for k, v in inputs.items()}
    if "nc" not in _NC_CACHE:
        _NC_CACHE["nc"] = build_nc()
    nc = _NC_CACHE["nc"]
    shared = ["ada_w", "ada_b", "mix_pre_g", "mix_post_g", "ffn_pre_g", "ffn_post_g", "pool_w", "pool_scale",
              "ssm_A_re", "ssm_A_im", "ssm_log_dt", "ssm_B_re", "ssm_B_im", "ssm_C_re", "ssm_C_im", "ssm_D",
              "ssm_glu_a", "ssm_glu_b", "ffn_w_up", "ffn_conv_w", "ffn_conv_b", "ffn_w_down"]
    in_maps = []
    for c in range(NCORES):
        sl = slice(c * NSB, (c + 1) * NSB)
        m = {k: inp[k] for k in shared}
        m["xp"] = inp["x_prompt"][c]
        m["xs"] = inp["x_sample"][sl].reshape(128, D)
        m["cp"] = np.ascontiguousarray(np.broadcast_to(inp["c_prompt"][c][None, :], (128, D)))
        m["cs"] = np.ascontiguousarray(np.repeat(inp["c_sample"][sl], LS, axis=0))
        m["spool"] = inp["state_pool"][0, sl]
        m["sre"] = inp["state_ssm_re"][0, sl]
        m["sim"] = inp["state_ssm_im"][0, sl]
        m["sconv"] = np.ascontiguousarray(inp["state_ffn_conv"][:, sl])
        in_maps.append(m)
    res = run_bass_kernel_spmd(nc, in_maps, core_ids=list(range(NCORES)))
    R = res.results
    y_prompt = np.stack([R[c]["yp"] for c in range(NCORES)], 0)
    y_sample = np.concatenate([R[c]["ys"].reshape(NSB, LS, D) for c in range(NCORES)], 0)
    npp = np.stack([R[c]["npp"] for c in range(NCORES)], 0)[None]
    nps = np.concatenate([R[c]["nps"] for c in range(NCORES)], 0)[None]
    nrp = np.stack([R[c]["nrp"] for c in range(NCORES)], 0)[None]
    nip = np.stack([R[c]["nip"] for c in range(NCORES)], 0)[None]
    nrs = np.concatenate([R[c]["nrs"] for c in range(NCORES)], 0)[None]
    nis = np.concatenate([R[c]["nis"] for c in range(NCORES)], 0)[None]
    ncp = np.stack([R[c]["ncp"] for c in range(NCORES)], 1)
    ncs = np.concatenate([R[c]["ncs"] for c in range(NCORES)], 1)
    return (y_prompt, y_sample, npp, nps, nrp, nip, nrs, nis, ncp, ncs)
```

```python
import contextlib
import numpy as np
import concourse.bass as bass
import concourse.mybir as mybir
from concourse.bass_utils import run_bass_kernel_spmd

F32 = mybir.dt.float32
BF16 = mybir.dt.bfloat16
AF = mybir.ActivationFunctionType
ALU = mybir.AluOpType
AX = mybir.AxisListType

NCORES = 8
D = 1024
NDT = 8
FH = 2816
FUP = 5632
NFT = 44
NGT = 22
SEQ = 2048
LB = 256
NPB = SEQ // LB
NSB = 16
LS = 8
HP = 15
HC = 2
EPS = 1e-6
ENGS = ("pe", "act", "dve", "pool", "sp")


class Buf:
    __slots__ = ("name", "lastw", "readers", "const")

    def __init__(self, name):
        self.name = name
        self.lastw = None
        self.readers = []
        self.const = False


class Op:
    __slots__ = ("eng", "fn", "deps", "signal", "val", "is_dma", "key", "dcount", "wait_all")

    def __init__(self, eng, fn, deps):
        self.eng = eng
        self.fn = fn
        self.deps = deps
        self.signal = False
        self.val = 0
        self.is_dma = False
        self.key = None
        self.dcount = 0
        self.wait_all = False


class Prog:
    def __init__(self):
        self.ops = {e: [] for e in ENGS}
        self.dcnt = {}
        self.nbuf = 0
        self.stores = []
        self.enabled = True

    def buf(self, name=None):
        self.nbuf += 1
        return Buf(name or f"b{self.nbuf}")

    def bufs(self, n, name="b"):
        return [self.buf(f"{name}{i}") for i in range(n)]

    def _mk(self, eng, fn, r, w):
        if not self.enabled:
            return None
        deps = set()
        for b in r:
            if b.lastw is not None:
                deps.add(b.lastw)
        for b in w:
            if b.lastw is not None:
                deps.add(b.lastw)
            deps.update(b.readers)
        o = Op(eng, fn, deps)
        for b in r:
            if not b.const:
                b.readers.append(o)
        for b in w:
            b.lastw = o
            b.readers = []
        self.ops[eng].append(o)
        return o

    def op(self, eng, fn, r=(), w=()):
        return self._mk(eng, fn, r, w)

    def dma(self, eng, fn, r=(), w=(), key=None, wait_all=False, store=False):
        o = self._mk(eng, fn, r, w)
        if o is None:
            return None
        o.is_dma = True
        o.key = key
        o.wait_all = wait_all
        if wait_all:
            o.deps = set(d for d in o.deps if not (d.is_dma and d.key == key))
        self.dcnt[key] = self.dcnt.get(key, 0) + 1
        o.dcount = self.dcnt[key]
        if store:
            self.stores.append(o)
        return o

    def full_barrier(self):
        deps = set()
        for e in ENGS:
            comp = [o for o in self.ops[e] if o.fn is not None and not o.is_dma]
            if comp:
                deps.add(comp[-1])
            last_by_key = {}
            for o in self.ops[e]:
                if o.is_dma:
                    last_by_key[o.key] = o
            deps.update(last_by_key.values())
        for e in ENGS:
            self.ops[e].append(Op(e, None, set(deps)))

    def barrier_wait(self, eng, ops):
        o = Op(eng, None, set(ops))
        self.ops[eng].append(o)
        return o

    def emit(self, nc, stack):
        for e in ENGS:
            for o in self.ops[e]:
                for d in o.deps:
                    d.signal = True
        sems = {}
        for e in ENGS:
            sems[e] = stack.enter_context(nc.semaphore("sem_" + e))
            c = 0
            for o in self.ops[e]:
                if o.is_dma or o.fn is None:
                    continue
                if o.signal:
                    c += 1
                    o.val = c
        dsem = {}
        for k in self.dcnt:
            dsem[k] = stack.enter_context(nc.semaphore("dsem_" + str(k)))

        def ev(d):
            if d.is_dma:
                if d.wait_all:
                    return dsem[d.key], 16 * self.dcnt[d.key], ("d", d.key)
                return dsem[d.key], 16 * d.dcount, ("d", d.key)
            return sems[d.eng], d.val, ("e", d.eng)

        block = stack.enter_context(nc.Block())
        prog = self

        def run(engname, eobj):
            known = {}
            for o in prog.ops[engname]:
                waits = {}
                for d in o.deps:
                    if (not d.is_dma) and d.eng == "pe" and engname == "pe" and not o.is_dma:
                        continue
                    s, v, kk = ev(d)
                    if known.get(kk, 0) >= v:
                        continue
                    if kk not in waits or waits[kk][1] < v:
                        waits[kk] = (s, v)
                for kk, (s, v) in waits.items():
                    known[kk] = v
                wl = list(waits.values())
                attach = None
                if o.fn is not None and wl and engname in ("act", "dve", "pool") and not o.is_dma:
                    attach = wl.pop()
                for s, v in wl:
                    eobj.wait_ge(s, v)
                if o.fn is None:
                    continue
                ins = o.fn(eobj)
                if attach is not None:
                    ins._wait_ge(attach[0], attach[1])
                if o.is_dma:
                    ins.then_inc(dsem[o.key], 16)
                elif o.signal:
                    ins.then_inc(sems[engname], 1)

        @block.tensor
        def _(e):
            run("pe", e)

        @block.scalar
        def _(e):
            run("act", e)

        @block.vector
        def _(e):
            run("dve", e)

        @block.gpsimd
        def _(e):
            run("pool", e)

        @block.sync
        def _(e):
            run("sp", e)


def seg3(ap2, nseg):
    return ap2.rearrange("p (s c) -> p s c", s=nseg)


def seg4(ap3, nseg):
    return ap3.rearrange("p k (s c) -> p k s c", s=nseg)


def build_nc(cfg=None):
    cfg = cfg or {}
    n_pblocks = cfg.get("n_pblocks", NPB)
    do_sample = cfg.get("do_sample", True)
    skip_s5 = cfg.get("skip_s5", False)
    nlayers = cfg.get("nlayers", 2)
    stage = cfg.get("stage", 99)

    nc = bass.Bass("TRN2", target_bir_lowering=False)

    def din(name, shape):
        return nc.dram_tensor(name, list(shape), F32, kind="ExternalInput").ap()

    def dout(name, shape):
        return nc.dram_tensor(name, list(shape), F32, kind="ExternalOutput").ap()

    xp = din("xp", [SEQ, D]); xs = din("xs", [128, D])
    cp = din("cp", [128, D]); cs = din("cs", [128, D])
    spool = din("spool", [NSB, HP, D])
    sre = din("sre", [NSB, 64, 64]); sim = din("sim", [NSB, 64, 64])
    sconv = din("sconv", [2, NSB, HC, FUP])
    ada_w = din("ada_w", [2, D, 6 * D]); ada_b = din("ada_b", [2, 6 * D])
    mix_pre_g = din("mix_pre_g", [2, D]); mix_post_g = din("mix_post_g", [2, D])
    ffn_pre_g = din("ffn_pre_g", [2, D]); ffn_post_g = din("ffn_post_g", [2, D])
    pool_w = din("pool_w", [1, 4, 256, 256]); pool_scale = din("pool_scale", [1, D])
    ssm_A_re = din("ssm_A_re", [1, 64, 64]); ssm_A_im = din("ssm_A_im", [1, 64, 64])
    ssm_log_dt = din("ssm_log_dt", [1, 64])
    ssm_B_re = din("ssm_B_re", [1, 64, 64, 16]); ssm_B_im = din("ssm_B_im", [1, 64, 64, 16])
    ssm_C_re = din("ssm_C_re", [1, 64, 16, 64]); ssm_C_im = din("ssm_C_im", [1, 64, 16, 64])
    ssm_D = din("ssm_D", [1, D])
    ssm_glu_a = din("ssm_glu_a", [1, D, D]); ssm_glu_b = din("ssm_glu_b", [1, D, D])
    ffn_w_up = din("ffn_w_up", [2, D, FUP]); ffn_conv_w = din("ffn_conv_w", [2, 3, FUP])
    ffn_conv_b = din("ffn_conv_b", [2, FUP]); ffn_w_down = din("ffn_w_down", [2, FH, D])

    yp = dout("yp", [SEQ, D]); ys = dout("ys", [128, D])
    npp = dout("npp", [HP, D]); nps = dout("nps", [NSB, HP, D])
    nrp = dout("nrp", [64, 64]); nip = dout("nip", [64, 64])
    nrs = dout("nrs", [NSB, 64, 64]); nis = dout("nis", [NSB, 64, 64])
    ncp = dout("ncp", [2, HC, FUP]); ncs = dout("ncs", [2, NSB, HC, FUP])

    NSS = NGT // 2
    wu_scr = nc.dram_tensor("wu_scr", [2, NSS, 128, NDT * 512], BF16).ap()
    wd_scr = nc.dram_tensor("wd_scr", [2, NSS, 128, 2 * D], BF16).ap()
    wg_scr = nc.dram_tensor("wg_scr", [2, 128, NDT * D], BF16).ap()
    s5ops = nc.dram_tensor("s5ops", [8, 128, 4 * 8 * 128], BF16).ap()
    s5tab_p = nc.dram_tensor("s5tab_p", [3, 128, 64 * 33], F32).ap()
    s5tab_s = nc.dram_tensor("s5tab_s", [3, 128, 64 * 32], F32).ap()

    P = Prog()
    st = contextlib.ExitStack()
    with st:
        def sb(name, shape, dt=F32):
            return st.enter_context(nc.sbuf_tensor(name, list(shape), dt))

        identf = sb("identf", [128, 128]); identb = sb("identb", [128, 128], BF16)
        epsc = sb("epsc", [128, 1])
        fmp = sb("fmp", [128, 512])
        fmp1 = sb("fmp1", [128, 96])
        Wp = sb("Wp", [128, 4, 2, 256], BF16)
        modp = sb("modp", [128, 2, 4, NDT])
        mods = sb("mods", [128, 2, 4, NDT, NSB])
        Gp = sb("Gp", [128, 2, 2, D]); Gs = sb("Gs", [128, 2, 2, D])
        inv15 = sb("inv15", [128, 4, HP])
        stT = sb("stT", [128, 2, NFT, NSB * HC])
        cstage = sb("cstage", [128, 1, NFT, NSB * HC])
        b_ident, b_fmp, b_Wp, b_modp, b_mods, b_Gp, b_Gs, b_inv15, b_stT = P.bufs(9, "c")
        b_cstage = P.buf("cstage")

        FM_ADAB = 0; FM_MPG = 96; FM_FPG = 112; FM_CW = 128; FM_CB = 392

        psall = st.enter_context(nc.psum_tensor("psall", [128, 8, 512], F32))
        banks = [psall[:, i, :] for i in range(8)]
        b_bank = P.bufs(8, "bank")

        def bank_bf(i):
            return banks[i][:].bitcast(BF16)

        pro = contextlib.ExitStack()
        with pro:
            def psb(name, shape, dt=F32):
                return pro.enter_context(nc.sbuf_tensor(name, list(shape), dt))

            P.op("pool", lambda e: e.memset(identf[:], 0.0), w=[b_ident])
            P.op("pool", lambda e: e.affine_select(out=identf[:], in_=identf[:], pattern=[[-1, 128]],
                                                   compare_op=ALU.not_equal, fill=1.0, base=0,
                                                   channel_multiplier=1), r=[b_ident], w=[b_ident])
            P.op("dve", lambda e: e.tensor_copy(out=identb[:], in_=identf[:]), r=[b_ident], w=[b_ident])
            P.op("pool", lambda e: e.memset(epsc[:], EPS), w=[b_ident])

            P.enabled = stage >= 1
            b_wscr = P.buf("wscr")
            NCV = 3
            cvf = psb("cvf", [128, NCV, NDT * 512]); cvb = psb("cvb", [128, NCV, NDT * 512], BF16)
            b_cvf = P.bufs(NCV, "cvf"); b_cvb = P.bufs(NCV, "cvb")
            cvctr = [0]
            cast_rr = ["dve", "act"]

            def cast_op(eng, dst, src, r, w):
                if eng == "act":
                    P.op("act", lambda e: e.activation(out=dst, in_=src, func=AF.Copy), r=r, w=w)
                else:
                    P.op(eng, lambda e: e.tensor_copy(out=dst, in_=src), r=r, w=w)

            def convert(loads, nelem, dst_ap):
                k = cvctr[0]
                cvctr[0] += 1
                sl = k % NCV
                for li, (vf, dap) in enumerate(loads):
                    qn = "sp"
                    P.dma(qn, lambda e, vf=vf, dap=dap, sl=sl: e.dma_start(out=vf(cvf[:, sl, :]), in_=dap),
                          w=[b_cvf[sl]], key=f"cvf{sl}")
                e1 = cast_rr[(2 * k) % len(cast_rr)]
                e2 = cast_rr[(2 * k + 1) % len(cast_rr)]
                h = nelem // 2
                cast_op(e1, cvb[:, sl, 0:h], cvf[:, sl, 0:h], [b_cvf[sl]], [b_cvb[sl]])
                cast_op(e2, cvb[:, sl, h:nelem], cvf[:, sl, h:nelem], [b_cvf[sl]], [b_cvb[sl]])
                P.dma("pool", lambda e, sl=sl: e.dma_start(out=dst_ap, in_=cvb[:, sl, 0:nelem]),
                      r=[b_cvb[sl]], w=[b_wscr], key=f"cvb{sl}")

            cv_jobs = []

            def convert_ffn(l):
                for ss in range(NSS):
                    loads = []
                    for h in range(2):
                        dap = ffn_w_up[l, :, h * FH + ss * 256: h * FH + (ss + 1) * 256].rearrange("(kt p) n -> p kt n", p=128)
                        loads.append((lambda v, h=h: v.rearrange("p (kt h c) -> p kt h c", kt=NDT, h=2)[:, :, h, :], dap))
                    cv_jobs.append((loads, NDT * 512, wu_scr[l, ss]))
                    dap = ffn_w_down[l, ss * 256:(ss + 1) * 256, :].rearrange("(j p) d -> p j d", p=128)
                    cv_jobs.append(([(lambda v: v[:, 0:2 * D].rearrange("p (j d) -> p j d", j=2), dap)], 2 * D, wd_scr[l, ss]))

            def convert_glu():
                for ab, wsrc in enumerate((ssm_glu_a, ssm_glu_b)):
                    for k0 in range(0, NDT, 4):
                        dap = wsrc[0, k0 * 128:(k0 + 4) * 128, :].rearrange("(kt p) n -> p kt n", p=128)
                        cv_jobs.append(([(lambda v: v.rearrange("p (kt n) -> p kt n", kt=4), dap)], 4 * D,
                                        wg_scr[ab, :, k0 * D:(k0 + 4) * D]))

            convert_ffn(0)
            if not skip_s5:
                convert_glu()
            if nlayers > 1:
                convert_ffn(1)

            def pump_convert(k):
                for _ in range(k):
                    if cv_jobs:
                        convert(*cv_jobs.pop(0))

            P.enabled = stage >= 1
            stg = psb("stg", [128, 4, 128])
            b_stg = P.buf("stg")
            P.op("dve", lambda e: e.memset(stg[:], 0.0), w=[b_stg])
            srcs = [
                (0, 0, 96, ada_b.rearrange("l (t p) -> (l t) p", p=128)),
                (0, 96, 16, mix_pre_g.rearrange("l (t p) -> (l t) p", p=128)),
                (0, 112, 16, ffn_pre_g.rearrange("l (t p) -> (l t) p", p=128)),
            ]
            cwv = ffn_conv_w.rearrange("l k (t p) -> (l k t) p", p=128)
            srcs += [(1, 0, 128, cwv[0:128]), (2, 0, 128, cwv[128:256]), (3, 0, 8, cwv[256:264]),
                     (3, 8, 88, ffn_conv_b.rearrange("l (t p) -> (l t) p", p=128))]
            sub = cfg.get("sub", 99)
            for (ti, r0, n, sap) in srcs[:sub]:
                P.dma("sp", lambda e, ti=ti, r0=r0, n=n, sap=sap: e.dma_start(out=stg[r0:r0 + n, ti, :], in_=sap),
                      r=[b_stg], w=[b_stg], key="cst", wait_all=True)
            for ti in range(4 if sub >= 20 else 0):
                P.op("pe", lambda e, ti=ti: e.transpose(out=banks[7][:, ti * 128:(ti + 1) * 128], in_=stg[:, ti, :],
                                                        identity=identf[:]), r=[b_stg, b_ident], w=[b_bank[7]])
            P.op("act", lambda e: e.activation(out=fmp[:], in_=banks[7][:], func=AF.Copy), r=[b_bank[7]], w=[b_fmp])
            P.op("dve", lambda e: e.tensor_scalar(out=fmp1[:], in0=fmp[:, 0:96], scalar1=1.0, scalar2=None, op0=ALU.add),
                 r=[b_fmp], w=[b_fmp])

            P.enabled = stage >= 2
            bc = psb("bc", [128, 9, D])
            b_bc = P.buf("bc")
            bsrc = [mix_post_g[0:1, :], ffn_post_g[0:1, :], mix_post_g[1:2, :], ffn_post_g[1:2, :],
                    ada_b[0:1, 2 * D:3 * D], ada_b[0:1, 5 * D:6 * D], ada_b[1:2, 2 * D:3 * D], ada_b[1:2, 5 * D:6 * D],
                    pool_scale[0:1, :]]
            for i, s in enumerate(bsrc):
                P.dma("sp", lambda e, i=i, s=s: e.dma_start(out=bc[:, i, :], in_=s.partition_broadcast(128)),
                      w=[b_bc], key="cst", wait_all=True)
            for i in range(4):
                P.op("dve", lambda e, i=i: e.tensor_tensor(out=bc[:, 4 + i, :], in0=bc[:, 4 + i, :], in1=bc[:, i, :], op=ALU.mult),
                     r=[b_bc], w=[b_bc])

            P.enabled = stage >= 3
            wpf = psb("wpf", [128, 4, 2, 256])
            b_wpf = P.buf("wpf")
            for g in range(4):
                P.dma("sp", lambda e, g=g: e.dma_start(out=wpf[:, g], in_=pool_w[0, g].rearrange("(kt p) n -> p kt n", p=128)),
                      w=[b_wpf], key="cst", wait_all=True)
            for g in range(4):
                for kt in range(2):
                    P.op("dve", lambda e, g=g, kt=kt: e.tensor_tensor(out=Wp[:, g, kt, :], in0=wpf[:, g, kt, :],
                                                                     in1=bc[:, 8, g * 256:(g + 1) * 256], op=ALU.mult),
                         r=[b_wpf, b_bc], w=[b_Wp])

            P.enabled = stage >= 3
            for gi, w_ in enumerate((2, 4, 8, 16)):
                P.op("pool", lambda e, gi=gi, w_=w_: e.memset(inv15[:, gi, :], 1.0 / w_), w=[b_inv15])
                for t in range(min(w_ - 1, HP)):
                    P.op("pool", lambda e, gi=gi, t=t: e.memset(inv15[:, gi, t:t + 1], 1.0 / (t + 1)), w=[b_inv15])

            P.enabled = stage >= 4
            ctile = psb("ctile", [128, 2, D]); csil = psb("csil", [128, 2, D], BF16)
            cTp = psb("cTp", [128, NDT, 128], BF16); cTs = psb("cTs", [128, NDT, 128], BF16)
            cT17 = psb("cT17", [128, NDT, 17], BF16)
            b_ct, b_cT = P.buf("ct"), P.buf("cT")
            P.dma("sp", lambda e: e.dma_start(out=ctile[:, 0, :], in_=cp[:, :]), w=[b_ct], key="cst", wait_all=True)
            P.dma("sp", lambda e: e.dma_start(out=ctile[:, 1, :], in_=cs[:, :]), w=[b_ct], key="cst", wait_all=True)
            P.op("act", lambda e: e.activation(out=csil[:], in_=ctile[:], func=AF.Silu), r=[b_ct], w=[b_ct])
            for which, dstT in ((0, cTp), (1, cTs)):
                for dt in range(NDT):
                    P.op("pe", lambda e, which=which, dt=dt: e.transpose(
                        out=bank_bf(6)[:, dt * 128:(dt + 1) * 128], in_=csil[:, which, dt * 128:(dt + 1) * 128],
                        identity=identb[:]), r=[b_ct, b_ident], w=[b_bank[6]])
                P.op("act", lambda e, dstT=dstT: e.activation(out=dstT[:], in_=bank_bf(6).rearrange("p (k c) -> p k c", k=NDT),
                                                              func=AF.Copy), r=[b_bank[6]], w=[b_cT])
            P.op("dve", lambda e: e.tensor_copy(out=cT17[:, :, 0:1], in_=cTp[:, :, 0:1]), r=[b_cT], w=[b_cT])
            P.op("dve", lambda e: e.tensor_copy(out=cT17[:, :, 1:17], in_=cTs[:, :, 0:128:8]), r=[b_cT], w=[b_cT])

            P.enabled = stage >= 5
            tmpf = psb("tmpf", [128, 2, 17]); b_tmpf = P.bufs(2, "tmpf")
            for l in range(2):
                for v in range(6):
                    for hf in range(2):
                        k = cvctr[0]
                        cvctr[0] += 1
                        sl = k % NCV
                        c0 = v * D + hf * 512
                        P.dma("sp", lambda e, l=l, c0=c0, sl=sl: e.dma_start(
                            out=cvf[:, sl, :].rearrange("p (kt n) -> p kt n", kt=NDT),
                            in_=ada_w[l, :, c0:c0 + 512].rearrange("(kt p) n -> p kt n", p=128)),
                            w=[b_cvf[sl]], key=f"cvf{sl}")
                        for part, eng in enumerate(("dve", "act", "dve", "act")):
                            cast_op(eng, cvb[:, sl, part * 1024:(part + 1) * 1024], cvf[:, sl, part * 1024:(part + 1) * 1024],
                                    [b_cvf[sl]], [b_cvb[sl]])
                        wv_ = cvb[:, sl, :].rearrange("p (kt n) -> p kt n", kt=NDT)
                        if v in (0, 1, 3, 4):
                            mslot = {0: 1, 1: 0, 3: 3, 4: 2}[v]
                            is_scale = v in (1, 4)
                            gbase = FM_MPG if v == 1 else FM_FPG
                            for d4 in range(4):
                                dt = hf * 4 + d4
                                bk = 4 + (dt % 2)
                                for kt in range(NDT):
                                    P.op("pe", lambda e, kt=kt, d4=d4, bk=bk, wv_=wv_: e.matmul(
                                        banks[bk][:, 0:17], wv_[:, kt, d4 * 128:(d4 + 1) * 128], cT17[:, kt, :],
                                        start=(kt == 0), stop=(kt == NDT - 1)), r=[b_cvb[sl], b_cT], w=[b_bank[bk]])
                                bcol = FM_ADAB + l * 48 + v * 8 + dt
                                ts = dt % 2
                                if is_scale:
                                    P.op("act", lambda e, bk=bk, bcol=bcol, ts=ts: e.activation(
                                        out=tmpf[:, ts, :], in_=banks[bk][:, 0:17], func=AF.Identity,
                                        bias=fmp1[:, bcol:bcol + 1]), r=[b_bank[bk], b_fmp], w=[b_tmpf[ts]])
                                    gcol = gbase + l * 8 + dt
                                    P.op("dve", lambda e, l=l, mslot=mslot, dt=dt, ts=ts, gcol=gcol: e.tensor_scalar(
                                        out=modp[:, l, mslot, dt:dt + 1], in0=tmpf[:, ts, 0:1], scalar1=fmp[:, gcol:gcol + 1],
                                        scalar2=None, op0=ALU.mult), r=[b_tmpf[ts], b_fmp], w=[b_modp])
                                    P.op("dve", lambda e, l=l, mslot=mslot, dt=dt, ts=ts, gcol=gcol: e.tensor_scalar(
                                        out=mods[:, l, mslot, dt, :], in0=tmpf[:, ts, 1:17],
                                        scalar1=fmp[:, gcol:gcol + 1], scalar2=None, op0=ALU.mult),
                                        r=[b_tmpf[ts], b_fmp], w=[b_mods])
                                else:
                                    P.op("act", lambda e, l=l, mslot=mslot, dt=dt, bk=bk, bcol=bcol: e.activation(
                                        out=modp[:, l, mslot, dt:dt + 1], in_=banks[bk][:, 0:1], func=AF.Identity,
                                        bias=fmp[:, bcol:bcol + 1]), r=[b_bank[bk], b_fmp], w=[b_modp])
                                    P.op("act", lambda e, l=l, mslot=mslot, dt=dt, bk=bk, bcol=bcol: e.activation(
                                        out=mods[:, l, mslot, dt, :], in_=banks[bk][:, 1:17], func=AF.Identity,
                                        bias=fmp[:, bcol:bcol + 1]), r=[b_bank[bk], b_fmp], w=[b_mods])
                        else:
                            which = 0 if v == 2 else 1
                            gi = l * 2 + which
                            for grp, (cT_, Gt, bG) in enumerate(((cTp, Gp, b_Gp), (cTs, Gs, b_Gs))):
                                bk = 6 + grp
                                for kt in range(NDT):
                                    P.op("pe", lambda e, kt=kt, bk=bk, cT_=cT_, wv_=wv_: e.matmul(
                                        banks[bk][:], cT_[:, kt, :], wv_[:, kt, :],
                                        start=(kt == 0), stop=(kt == NDT - 1)), r=[b_cvb[sl], b_cT], w=[b_bank[bk]])
                                P.op("dve", lambda e, l=l, which=which, hf=hf, bk=bk, Gt=Gt, gi=gi: e.tensor_tensor(
                                    out=Gt[:, l, which, hf * 512:(hf + 1) * 512], in0=banks[bk][:],
                                    in1=bc[:, gi, hf * 512:(hf + 1) * 512], op=ALU.mult),
                                    r=[b_bank[bk], b_bc], w=[bG])
                                P.op("dve", lambda e, l=l, which=which, hf=hf, Gt=Gt, gi=gi: e.tensor_tensor(
                                    out=Gt[:, l, which, hf * 512:(hf + 1) * 512],
                                    in0=Gt[:, l, which, hf * 512:(hf + 1) * 512],
                                    in1=bc[:, 4 + gi, hf * 512:(hf + 1) * 512], op=ALU.add),
                                    r=[bG, b_bc], w=[bG])
                        if stage >= 7:
                            pump_convert(2)

            P.enabled = stage >= 6
            if do_sample:
                cst_tok = cvf[0:32].rearrange("p a n -> p (a n)")[:, 0:FUP]
                for l in range(2):
                    P.dma("sp", lambda e, l=l: e.dma_start(out=cst_tok, in_=sconv[l].rearrange("b k f -> (b k) f")),
                          w=[b_cvf[0], b_cvf[1]], key="cst_tok")
                    for f0 in range(0, NFT, 16):
                        nf = min(16, NFT - f0)
                        for j in range(nf):
                            ft = f0 + j
                            P.op("pe", lambda e, ft=ft, j=j: e.transpose(
                                out=banks[6][:, j * 32:(j + 1) * 32], in_=cst_tok[:, ft * 128:(ft + 1) * 128],
                                identity=identf[0:32, 0:32]), r=[b_cvf[0], b_cvf[1], b_ident], w=[b_bank[6]])
                        P.op("act", lambda e, l=l, f0=f0, nf=nf: e.activation(
                            out=stT[:, l, f0:f0 + nf, :], in_=banks[6][:, 0:nf * 32].rearrange("p (f c) -> p f c", f=nf),
                            func=AF.Copy), r=[b_bank[6]], w=[b_stT])

            P.enabled = stage >= 7
            pump_convert(len(cv_jobs))

        P.enabled = stage >= 8
        P.full_barrier()


        if not skip_s5 and nlayers > 1:
            pro2 = contextlib.ExitStack()
            with pro2:
                def p2(name, shape, dt=F32):
                    return pro2.enter_context(nc.sbuf_tensor(name, list(shape), dt))
                TWO_PI = float(2 * np.pi)
                C1 = 6.28125
                C2 = 0.0019353071795864769
                MAGIC = 12582912.0
                bq = {n: P.buf("q_" + n) for n in "Ald AT dtv marg ang kang nq trig magk L f W Bst Bsw Cld CT CA CB Dld Dcol alpha phi tab colv Eb stage perm".split()}
                Ald = p2("Ald", [64, 2, 128]); AT = p2("AT", [128, 2, 64]); dtv = p2("dtv", [128, 64])
                marg = p2("marg", [128, 64]); ang = p2("ang", [128, 64])
                for i, src in enumerate((ssm_A_re, ssm_A_im)):
                    for hh in range(2):
                        P.dma("sp", lambda e, i=i, hh=hh, src=src: e.dma_start(out=Ald[:, i, hh * 64:(hh + 1) * 64], in_=src[0]),
                              w=[bq["Ald"]], key="q_c", wait_all=True)
                P.dma("sp", lambda e: e.dma_start(out=dtv[:], in_=ssm_log_dt[0:1, :].partition_broadcast(128)),
                      w=[bq["dtv"]], key="q_c", wait_all=True)
                for i in range(2):
                    P.op("pe", lambda e, i=i: e.transpose(out=banks[7][:, i * 64:(i + 1) * 64], in_=Ald[:, i, :],
                                                          identity=identf[0:64, 0:64]), r=[bq["Ald"], b_ident], w=[b_bank[7]])
                P.op("act", lambda e: e.activation(out=AT[:].rearrange("p a g -> p (a g)"), in_=banks[7][:, 0:128], func=AF.Copy),
                     r=[b_bank[7]], w=[bq["AT"]])
                P.op("act", lambda e: e.activation(out=dtv[:], in_=dtv[:], func=AF.Exp), r=[bq["dtv"]], w=[bq["dtv"]])
                P.op("dve", lambda e: e.tensor_tensor(out=marg[:], in0=AT[:, 0, :], in1=dtv[:], op=ALU.mult),
                     r=[bq["AT"], bq["dtv"]], w=[bq["marg"]])
                P.op("dve", lambda e: e.tensor_tensor(out=ang[:], in0=AT[:, 1, :], in1=dtv[:], op=ALU.mult),
                     r=[bq["AT"], bq["dtv"]], w=[bq["ang"]])

                def reduce_2pi(x, nq_, bx, bn):
                    P.op("dve", lambda e: e.tensor_scalar(out=nq_, in0=x, scalar1=1.0 / TWO_PI, scalar2=MAGIC, op0=ALU.mult, op1=ALU.add),
                         r=[bx], w=[bn])
                    P.op("dve", lambda e: e.tensor_scalar(out=nq_, in0=nq_, scalar1=MAGIC, scalar2=None, op0=ALU.subtract),
                         r=[bn], w=[bn])
                    P.op("dve", lambda e: e.scalar_tensor_tensor(out=x, in0=nq_, scalar=-C1, in1=x, op0=ALU.mult, op1=ALU.add),
                         r=[bn, bx], w=[bx])
                    P.op("dve", lambda e: e.scalar_tensor_tensor(out=x, in0=nq_, scalar=-C2, in1=x, op0=ALU.mult, op1=ALU.add),
                         r=[bn, bx], w=[bx])

                kang = p2("kang", [128, 2, 9, 64]); nq = p2("nq", [128, 2 * 33 * 64]); trig = p2("trig", [128, 2, 9, 64])
                magk = p2("magk", [128, 9, 64]); Lt = p2("Lt", [128, 2, 9, 64])
                for k in range(9):
                    P.op("dve", lambda e, k=k: e.tensor_scalar(out=kang[:, 0, k, :], in0=ang[:], scalar1=float(k), scalar2=None, op0=ALU.mult),
                         r=[bq["ang"]], w=[bq["kang"]])
                    P.op("dve", lambda e, k=k: e.tensor_scalar(out=kang[:, 1, k, :], in0=ang[:], scalar1=float(k), scalar2=float(np.pi / 2),
                                                               op0=ALU.mult, op1=ALU.add), r=[bq["ang"]], w=[bq["kang"]])
                    P.op("act", lambda e, k=k: e.activation(out=magk[:, k, :], in_=marg[:], func=AF.Exp, scale=float(k)),
                         r=[bq["marg"]], w=[bq["magk"]])
                kflat = kang[:].rearrange("p a k g -> p (a k g)")
                reduce_2pi(kflat, nq[:, 0:2 * 9 * 64], bq["kang"], bq["nq"])
                P.op("act", lambda e: e.activation(out=trig[:].rearrange("p a k g -> p (a k g)"), in_=kflat, func=AF.Sin),
                     r=[bq["kang"]], w=[bq["trig"]])
                P.op("dve", lambda e: e.tensor_tensor(out=Lt[:, 0], in0=magk[:], in1=trig[:, 1], op=ALU.mult),
                     r=[bq["magk"], bq["trig"]], w=[bq["L"]])
                P.op("dve", lambda e: e.tensor_tensor(out=Lt[:, 1], in0=magk[:], in1=trig[:, 0], op=ALU.mult),
                     r=[bq["magk"], bq["trig"]], w=[bq["L"]])
                ft = p2("ft", [128, 8, 64])
                P.op("dve", lambda e: e.tensor_scalar(out=ft[:, 0, :], in0=Lt[:, 0, 1, :], scalar1=-1.0, scalar2=None, op0=ALU.add),
                     r=[bq["L"]], w=[bq["f"]])
                P.op("dve", lambda e: e.tensor_tensor(out=ft[:, 1, :], in0=AT[:, 0, :], in1=AT[:, 0, :], op=ALU.mult), r=[bq["AT"], bq["f"]], w=[bq["f"]])
                P.op("dve", lambda e: e.tensor_tensor(out=ft[:, 2, :], in0=AT[:, 1, :], in1=AT[:, 1, :], op=ALU.mult), r=[bq["AT"], bq["f"]], w=[bq["f"]])
                P.op("dve", lambda e: e.tensor_tensor(out=ft[:, 1, :], in0=ft[:, 1, :], in1=ft[:, 2, :], op=ALU.add), r=[bq["f"]], w=[bq["f"]])
                P.op("dve", lambda e: e.reciprocal(out=ft[:, 1, :], in_=ft[:, 1, :]), r=[bq["f"]], w=[bq["f"]])
                P.op("dve", lambda e: e.tensor_tensor(out=ft[:, 2, :], in0=ft[:, 0, :], in1=AT[:, 0, :], op=ALU.mult), r=[bq["f"], bq["AT"]], w=[bq["f"]])
                P.op("dve", lambda e: e.tensor_tensor(out=ft[:, 3, :], in0=Lt[:, 1, 1, :], in1=AT[:, 1, :], op=ALU.mult), r=[bq["L"], bq["AT"], bq["f"]], w=[bq["f"]])
                P.op("dve", lambda e: e.tensor_tensor(out=ft[:, 2, :], in0=ft[:, 2, :], in1=ft[:, 3, :], op=ALU.add), r=[bq["f"]], w=[bq["f"]])
                P.op("dve", lambda e: e.tensor_tensor(out=ft[:, 4, :], in0=ft[:, 2, :], in1=ft[:, 1, :], op=ALU.mult), r=[bq["f"]], w=[bq["f"]])
                P.op("dve", lambda e: e.tensor_tensor(out=ft[:, 2, :], in0=Lt[:, 1, 1, :], in1=AT[:, 0, :], op=ALU.mult), r=[bq["L"], bq["AT"], bq["f"]], w=[bq["f"]])
                P.op("dve", lambda e: e.tensor_tensor(out=ft[:, 3, :], in0=ft[:, 0, :], in1=AT[:, 1, :], op=ALU.mult), r=[bq["f"], bq["AT"]], w=[bq["f"]])
                P.op("dve", lambda e: e.tensor_tensor(out=ft[:, 2, :], in0=ft[:, 2, :], in1=ft[:, 3, :], op=ALU.subtract), r=[bq["f"]], w=[bq["f"]])
                P.op("dve", lambda e: e.tensor_tensor(out=ft[:, 5, :], in0=ft[:, 2, :], in1=ft[:, 1, :], op=ALU.mult), r=[bq["f"]], w=[bq["f"]])
                Wt = p2("Wt", [128, 2, 8, 64]); Wtmp = nq[:, 3584:4096].rearrange("p (k g) -> p k g", k=8)
                fre_b = ft[:, 4, :].unsqueeze(1).to_broadcast([128, 8, 64])
                fim_b = ft[:, 5, :].unsqueeze(1).to_broadcast([128, 8, 64])
                P.op("dve", lambda e: e.tensor_tensor(out=Wt[:, 0], in0=Lt[:, 0, 0:8, :], in1=fre_b, op=ALU.mult), r=[bq["L"], bq["f"]], w=[bq["W"]])
                P.op("dve", lambda e: e.tensor_tensor(out=Wtmp[:], in0=Lt[:, 1, 0:8, :], in1=fim_b, op=ALU.mult), r=[bq["L"], bq["f"]], w=[bq["nq"]])
                P.op("dve", lambda e: e.tensor_tensor(out=Wt[:, 0], in0=Wt[:, 0], in1=Wtmp[:], op=ALU.subtract), r=[bq["W"], bq["nq"]], w=[bq["W"]])
                P.op("dve", lambda e: e.tensor_tensor(out=Wt[:, 1], in0=Lt[:, 0, 0:8, :], in1=fim_b, op=ALU.mult), r=[bq["L"], bq["f"]], w=[bq["W"]])
                P.op("dve", lambda e: e.tensor_tensor(out=Wtmp[:], in0=Lt[:, 1, 0:8, :], in1=fre_b, op=ALU.mult), r=[bq["L"], bq["f"], bq["W"]], w=[bq["nq"]])
                P.op("dve", lambda e: e.tensor_tensor(out=Wt[:, 1], in0=Wt[:, 1], in1=Wtmp[:], op=ALU.add), r=[bq["W"], bq["nq"]], w=[bq["W"]])

                Bst = p2("Bst", [128, 64, 16]); Bsw = p2("Bsw", [128, 64, 16])
                for (dst, bname, top, bot) in ((Bst, "Bst", ssm_B_re, ssm_B_im), (Bsw, "Bsw", ssm_B_im, ssm_B_re)):
                    for hh, src in enumerate((top, bot)):
                        for g0 in range(0, 64, 16):
                            P.dma("sp", lambda e, dst=dst, hh=hh, src=src, g0=g0: e.dma_start(
                                out=dst[hh * 64:(hh + 1) * 64, g0:g0 + 16, :], in_=src[0, g0:g0 + 16].rearrange("g p c -> p g c")),
                                w=[bq[bname]], key="q_c", wait_all=True)
                P.op("dve", lambda e: e.tensor_scalar(out=Bsw[0:64], in0=Bsw[0:64], scalar1=-1.0, scalar2=None, op0=ALU.mult),
                     r=[bq["Bsw"]], w=[bq["Bsw"]])
                Cld = p2("Cld", [128, 2, 8, 128]); CA = p2("CA", [128, 64, 16]); CB = p2("CB", [128, 64, 16])
                for v, (left, right) in enumerate(((ssm_C_re, ssm_C_im), (ssm_C_im, ssm_C_re))):
                    for hh, src in enumerate((left, right)):
                        P.dma("sp", lambda e, v=v, hh=hh, src=src: e.dma_start(
                            out=Cld[:, v, :, hh * 64:(hh + 1) * 64], in_=src[0].rearrange("(t g) c p -> (g c) t p", g=8)),
                            w=[bq["Cld"]], key="q_c", wait_all=True)
                for v, (dst, bname) in enumerate(((CA, "CA"), (CB, "CB"))):
                    for t4 in range(2):
                        for tt in range(4):
                            t = t4 * 4 + tt
                            P.op("pe", lambda e, v=v, t=t, tt=tt: e.transpose(out=banks[6][:, tt * 128:(tt + 1) * 128], in_=Cld[:, v, t, :],
                                                                              identity=identf[:]), r=[bq["Cld"], b_ident], w=[b_bank[6]])
                        P.op("act", lambda e, dst=dst, t4=t4: e.activation(
                            out=dst[:, t4 * 32:(t4 + 1) * 32, :].rearrange("p g c -> p (g c)"), in_=banks[6][:], func=AF.Copy),
                            r=[b_bank[6]], w=[bq[bname]])
                P.op("dve", lambda e: e.tensor_scalar(out=CA[64:128], in0=CA[64:128], scalar1=-1.0, scalar2=None, op0=ALU.mult), r=[bq["CA"]], w=[bq["CA"]])
                P.op("dve", lambda e: e.tensor_scalar(out=CB[:], in0=CB[:], scalar1=-1.0, scalar2=None, op0=ALU.mult), r=[bq["CB"]], w=[bq["CB"]])
                Dld = p2("Dld", [64, 8, 16]); Dcol = p2("Dcol", [128, 64])
                P.dma("sp", lambda e: e.dma_start(out=Dld[:, 0, :], in_=ssm_D[0].rearrange("(g c) -> g c", c=16)), w=[bq["Dld"]], key="q_c", wait_all=True)
                for r_ in range(1, 8):
                    P.op("dve", lambda e, r_=r_: e.tensor_copy(out=Dld[:, r_, :], in_=Dld[:, 0, :]), r=[bq["Dld"]], w=[bq["Dld"]])
                P.op("pe", lambda e: e.transpose(out=banks[7][:, 0:64], in_=Dld[:].rearrange("g r c -> g (r c)"), identity=identf[0:64, 0:64]),
                     r=[bq["Dld"], b_ident], w=[b_bank[7]])
                P.op("act", lambda e: e.activation(out=Dcol[:], in_=banks[7][:, 0:64], func=AF.Copy), r=[b_bank[7]], w=[bq["Dcol"]])

                alpha = p2("alpha", [128, 64]); colv = p2("colv", [128, 33])
                phi = p2("phi", [128, 2, 64, 33]); tab = p2("tab", [128, 3, 64, 33])
                P.op("dve", lambda e: e.tensor_scalar(out=alpha[:], in0=ang[:], scalar1=8.0, scalar2=None, op0=ALU.mult), r=[bq["ang"]], w=[bq["alpha"]])
                reduce_2pi(alpha[:], nq[:, 0:64], bq["alpha"], bq["nq"])
                for c_ in range(33):
                    P.op("pool", lambda e, c_=c_: e.memset(colv[:, c_:c_ + 1], float(c_)), w=[bq["colv"]])
                a_b = alpha[:].unsqueeze(2).to_broadcast([128, 64, 33])
                c_b = colv[:].unsqueeze(1).to_broadcast([128, 64, 33])
                P.op("dve", lambda e: e.tensor_tensor(out=phi[:, 0], in0=a_b, in1=c_b, op=ALU.mult), r=[bq["alpha"], bq["colv"]], w=[bq["phi"]])
                P.op("dve", lambda e: e.tensor_scalar(out=phi[:, 1], in0=phi[:, 0], scalar1=float(np.pi / 2), scalar2=None, op0=ALU.add),
                     r=[bq["phi"]], w=[bq["phi"]])
                pflat = phi[:].rearrange("p a g c -> p (a g c)")
                reduce_2pi(pflat, nq[:, 0:2 * 64 * 33], bq["phi"], bq["nq"])
                P.op("act", lambda e: e.activation(out=tab[:, 1], in_=phi[:, 0], func=AF.Sin), r=[bq["phi"]], w=[bq["tab"]])
                P.op("act", lambda e: e.activation(out=tab[:, 0], in_=phi[:, 1], func=AF.Sin), r=[bq["phi"]], w=[bq["tab"]])
                P.op("pool", lambda e: e.memset(tab[:, 2, :, 0:1], 0.0), w=[bq["tab"]])
                P.op("dve", lambda e: e.tensor_copy(out=tab[:, 2, :, 1:33], in_=magk[:, 8, :].unsqueeze(2).to_broadcast([128, 64, 32])),
                     r=[bq["magk"], bq["tab"]], w=[bq["tab"]])
                tabs_sa = p2("tabs_sa", [128, 2, 16, NSB, 2]); b_tabs_sa = P.bufs(2, "tabs_sa")
                qi = 0
                for kk in range(3):
                    P.dma("sp", lambda e, kk=kk: e.dma_start(out=s5tab_p[kk], in_=tab[:, kk].rearrange("p g c -> p (g c)")),
                          r=[bq["tab"]], w=[b_wscr], key="q_st")
                    for g0 in range(0, 64, 16):
                        sl_ = qi % 2
                        qi += 1
                        P.op("pool", lambda e, kk=kk, g0=g0, sl_=sl_: e.tensor_copy(
                            out=tabs_sa[:, sl_], in_=tab[:, kk, g0:g0 + 16, 0:2].unsqueeze(2).to_broadcast([128, 16, NSB, 2])),
                            r=[bq["tab"]], w=[b_tabs_sa[sl_]])
                        P.dma("sp", lambda e, kk=kk, g0=g0, sl_=sl_: e.dma_start(
                            out=s5tab_s[kk, :, g0 * 32:(g0 + 16) * 32], in_=tabs_sa[:, sl_].rearrange("p g b c -> p (g b c)")),
                            r=[b_tabs_sa[sl_]], w=[b_wscr], key="q_ts%d" % sl_)

                Eb = p2("Eb", [128, 2, 8, 15, 16]); Etmp = p2("Etmp", [128, 1, 8, 8, 16]); opstg = p2("opstg", [128, 2, 4, 8, 128], BF16)
                Ctmp = nq[:, 0:1024].rearrange("p (a g r c) -> p a g r c", a=1, g=8, r=8); Cp = nq[:, 1024:2048].rearrange("p (a g r c) -> p a g r c", a=1, g=8, r=8)
                Wrev = nq[:, 2048:3072].rearrange("p (a k g) -> p a k g", a=2, k=8)
                b_Eb, b_Etmp, b_Cp, b_Ctmp = P.bufs(2, "Eb"), P.bufs(2, "Etmp"), P.bufs(2, "Cp"), P.bufs(2, "Ctmp")
                b_stage = P.bufs(2, "stage"); b_Wrev = P.buf("Wrev")
                P.op("pool", lambda e: e.memset(Eb[:], 0.0), w=b_Eb)
                phif = phi[:].rearrange("p a g c -> p (a g c)")
                Ebb = phif[:, 0:1920].bitcast(BF16).rearrange("p (a g k c) -> p a g k c", a=2, g=8, k=15)
                CAb = phif[:, 1920:1920 + 512].bitcast(BF16).rearrange("p (g c) -> p g c", c=16)
                b_Ebb = P.bufs(2, "Ebb")
                P.op("act", lambda e: e.activation(out=CAb, in_=CA[:], func=AF.Copy), r=[bq["CA"], bq["phi"], bq["tab"]], w=[bq["CA"], bq["phi"]])
                for k_ in range(8):
                    P.op("pool", lambda e, k_=k_: e.tensor_copy(out=Wrev[:, :, k_, :], in_=Wt[:, :, 7 - k_, :]), r=[bq["W"], bq["nq"]], w=[b_Wrev])
                def sB1(bt):
                        gs = slice(bt * 8, (bt + 1) * 8)
                        sl = bt % 2
                        bst_b = Bst[:, gs, :].unsqueeze(2).to_broadcast([128, 8, 8, 16])
                        bsw_b = Bsw[:, gs, :].unsqueeze(2).to_broadcast([128, 8, 8, 16])
                        wr_b = Wrev[:, 0, :, gs].rearrange("p k g -> p g k").unsqueeze(3).to_broadcast([128, 8, 8, 16])
                        wi_b = Wrev[:, 1, :, gs].rearrange("p k g -> p g k").unsqueeze(3).to_broadcast([128, 8, 8, 16])
                        P.op("dve", lambda e, sl=sl, bst_b=bst_b, wr_b=wr_b: e.tensor_tensor(out=Eb[:, sl, :, 0:8, :], in0=bst_b, in1=wr_b, op=ALU.mult),
                             r=[bq["Bst"], b_Wrev], w=[b_Eb[sl]])
                        P.op("dve", lambda e, sl=sl, bsw_b=bsw_b, wi_b=wi_b: e.tensor_tensor(out=Etmp[:, 0], in0=bsw_b, in1=wi_b, op=ALU.mult),
                             r=[bq["Bsw"], b_Wrev], w=[b_Etmp[0]])
                        P.op("dve", lambda e, sl=sl: e.tensor_tensor(out=Eb[:, sl, :, 0:8, :], in0=Eb[:, sl, :, 0:8, :], in1=Etmp[:, 0], op=ALU.add),
                             r=[b_Eb[sl], b_Etmp[0]], w=[b_Eb[sl]])
                        P.op("act", lambda e, sl=sl: e.activation(out=Ebb[:, sl].rearrange("p g k c -> p (g k c)"),
                                                                  in_=Eb[:, sl].rearrange("p g k c -> p (g k c)"), func=AF.Copy),
                             r=[b_Eb[sl], bq["phi"]], w=[b_Ebb[sl]])

                def sB2(bt):
                        gs = slice(bt * 8, (bt + 1) * 8)
                        sl = bt % 2
                        for g4 in range(2):
                            bk = 4 + g4
                            for gg in range(4):
                                g8 = g4 * 4 + gg
                                P.op("pe", lambda e, g8=g8, gg=gg, bk=bk, sl=sl: e.transpose(
                                    out=banks[bk][:, gg * 128:(gg + 1) * 128], in_=Eb[:, sl, g8, 0:8, :].rearrange("p k c -> p (k c)"),
                                    identity=identf[:]), r=[b_Eb[sl], b_ident], w=[b_bank[bk]])
                            bv = banks[bk][:].rearrange("p (g c) -> p g c", g=4)
                            P.op("act", lambda e, g4=g4, bv=bv, sl=sl: e.activation(out=opstg[:, sl, 0, g4 * 4:(g4 + 1) * 4, :], in_=bv, func=AF.Copy),
                                 r=[b_bank[bk]], w=[b_stage[sl]])
                            P.op("act", lambda e, g4=g4, bv=bv, sl=sl: e.activation(out=opstg[:, sl, 1, g4 * 4:(g4 + 1) * 4, 0:64], in_=bv[:, :, 64:128], func=AF.Copy),
                                 r=[b_bank[bk]], w=[b_stage[sl]])
                            P.op("act", lambda e, g4=g4, bv=bv, sl=sl: e.activation(out=opstg[:, sl, 1, g4 * 4:(g4 + 1) * 4, 64:128], in_=bv[:, :, 0:64], func=AF.Copy, scale=-1.0),
                                 r=[b_bank[bk]], w=[b_stage[sl]])
                        for g8 in range(8):
                            g = bt * 8 + g8
                            bk2 = 6 + (g8 % 2)
                            for r_ in range(8):
                                P.op("pe", lambda e, g8=g8, r_=r_, g=g, bk2=bk2, sl=sl: e.matmul(
                                    banks[bk2][:, r_ * 16:(r_ + 1) * 16], Ebb[:, sl, g8, 7 - r_:15 - r_, :].rearrange("p k c -> p (k c)"),
                                    CAb[:, g, :], start=True, stop=True), r=[b_Ebb[sl], bq["CA"]], w=[b_bank[bk2]])
                            P.op("dve", lambda e, g8=g8, g=g, bk2=bk2, sl=sl: e.scalar_tensor_tensor(
                                out=opstg[:, sl, 2, g8, :], in0=identf[:], scalar=Dcol[:, g:g + 1], in1=banks[bk2][:, 0:128],
                                op0=ALU.mult, op1=ALU.add), r=[b_bank[bk2], bq["Dcol"], b_ident], w=[b_stage[sl]])
                        ca_b = CA[:, gs, :].unsqueeze(2).to_broadcast([128, 8, 8, 16])
                        cb_b = CB[:, gs, :].unsqueeze(2).to_broadcast([128, 8, 8, 16])
                        lr_b = Lt[:, 0, 1:9, gs].rearrange("p k g -> p g k").unsqueeze(3).to_broadcast([128, 8, 8, 16])
                        li_b = Lt[:, 1, 1:9, gs].rearrange("p k g -> p g k").unsqueeze(3).to_broadcast([128, 8, 8, 16])
                        P.op("pool", lambda e, sl=sl, ca_b=ca_b, lr_b=lr_b: e.tensor_tensor(out=Cp[:, 0], in0=ca_b, in1=lr_b, op=ALU.mult),
                             r=[bq["CA"], bq["L"], bq["nq"]], w=[b_Cp[0]])
                        P.op("pool", lambda e, sl=sl, cb_b=cb_b, li_b=li_b: e.tensor_tensor(out=Ctmp[:, 0], in0=cb_b, in1=li_b, op=ALU.mult),
                             r=[bq["CB"], bq["L"], bq["nq"]], w=[b_Ctmp[0]])
                        P.op("pool", lambda e, sl=sl: e.tensor_tensor(
                            out=opstg[:, sl, 3].rearrange("p g (r c) -> p g r c", r=8), in0=Cp[:, 0], in1=Ctmp[:, 0], op=ALU.add),
                            r=[b_Cp[0], b_Ctmp[0]], w=[b_stage[sl]])
                        P.dma("sp", lambda e, bt=bt, sl=sl: e.dma_start(out=s5ops[bt], in_=opstg[:, sl].rearrange("p k g c -> p (k g c)")),
                              r=[b_stage[sl]], w=[b_wscr], key=f"q_so{sl}")

                sB1(0)
                for bt in range(8):
                    if bt + 1 < 8:
                        sB1(bt + 1)
                    sB2(bt)
            P.full_barrier()
        P.enabled = stage >= 9

        xres = sb("xres", [128, 1, 2, D])
        b_xres = [[P.buf(f"xres{s}{q}") for q in range(2)] for s in range(1)]
        xn = sb("xn", [128, 2, D], BF16); b_xn = P.bufs(2, "xn")
        junk = sb("junk", [128, D], BF16); b_junk = P.buf("junk")
        ssb = sb("ssb", [128, 16]); b_ss = P.bufs(16, "ss")
        hT0 = sb("hT0", [128, 1, NDT, 368], BF16); b_hT0 = P.bufs(1, "hT0")
        hist0 = sb("hist0", [128, NDT, HP], BF16); b_hist0 = P.buf("hist0")
        h2T = sb("h2T", [128, 1, NDT, HC + LB], BF16); b_h2T = P.bufs(1, "h2T")
        hist2 = sb("hist2", [128, 2, NDT, HC], BF16); b_hist2 = P.bufs(2, "hist2")
        pooled = sb("pooled", [128, NDT, LB], BF16); b_pooled = P.buf("pooled")
        pt = sb("pt", [128, 2, 2, 368]); b_pt = P.bufs(2, "pt")
        tmp_tok = sb("tmp_tok", [128, 2, D]); b_tmp = P.bufs(2, "tmp")
        NS = 3
        wus = sb("wus", [128, NS, NDT, 2, 256], BF16); wds = sb("wds", [128, NS, 2, D], BF16)
        b_ws = P.bufs(NS, "ws")
        cg = sb("cg", [128, 2, LB]); cv = sb("cv", [128, 2, LB]); b_cg = P.bufs(2, "cg"); b_cv = P.bufs(2, "cv")
        gl = sb("gl", [128, 2, LB]); b_gl = P.bufs(2, "gl")
        gv = sb("gv", [128, 3, LB], BF16); b_gv = P.bufs(3, "gv")
        cso = sb("cso", [32, 512]); b_cso = P.buf("cso")

        P.op("dve", lambda e: e.memset(ssb[:], 0.0), w=b_ss)

        for b in (b_ident, b_fmp, b_Wp, b_modp, b_mods, b_Gp, b_Gs, b_inv15, b_stT):
            b.const = True

        ss_ctr = [0]

        def new_ss():
            i = ss_ctr[0] % 16
            ss_ctr[0] += 1
            return i

        def rms_stat(src_ap, r_bufs):
            i = new_ss()
            P.op("act", lambda e, i=i, src_ap=src_ap: e.activation(out=junk[:], in_=src_ap, func=AF.Square,
                                                                   accum_out=ssb[:, i:i + 1]),
                 r=list(r_bufs) + [b_ss[i]], w=[b_junk, b_ss[i]])
            P.op("act", lambda e, i=i: e.activation(out=ssb[:, i:i + 1], in_=ssb[:, i:i + 1], func=AF.Sqrt,
                                                    scale=1.0 / D, bias=epsc[:, 0:1]), r=[b_ss[i]], w=[b_ss[i]])
            P.op("dve", lambda e, i=i: e.reciprocal(out=ssb[:, i:i + 1], in_=ssb[:, i:i + 1]), r=[b_ss[i]], w=[b_ss[i]])
            return i

        def tok_cols(view3, nseg, q, H, L):
            if nseg == 1:
                return view3[:, 0, H + q * 128:H + (q + 1) * 128]
            return view3[:, :, H:H + L]

        def prenorm(blk, slot, q, l, which, dst_tile, dst_buf, H, W):
            nseg, L = blk["nseg"], blk["L"]
            xr = xres[:, slot, q, :]
            i = rms_stat(xr, [b_xres[slot][q]])
            xs_ = q % 2
            P.op("dve", lambda e, i=i, xs_=xs_, xr=xr: e.tensor_scalar(out=xn[:, xs_, :], in0=xr, scalar1=ssb[:, i:i + 1],
                                                                       scalar2=None, op0=ALU.mult),
                 r=[b_xres[slot][q], b_ss[i]], w=[b_xn[xs_]])
            bk = 6 + (q % 2)
            for dt in range(NDT):
                P.op("pe", lambda e, dt=dt, xs_=xs_, bk=bk: e.transpose(
                    out=bank_bf(bk)[:, dt * 128:(dt + 1) * 128], in_=xn[:, xs_, dt * 128:(dt + 1) * 128],
                    identity=identb[:]), r=[b_xn[xs_], b_ident], w=[b_bank[bk]])
            a_slot, b_slot = (0, 1) if which == 0 else (2, 3)
            if blk["kind"] == "p":
                for dt in range(NDT):
                    dstv = seg3(dst_tile[:, dt, 0:nseg * W], nseg)
                    if dt % 2 == 0:
                        P.op("act", lambda e, dt=dt, bk=bk, dstv=dstv, l=l, a_slot=a_slot, b_slot=b_slot: e.activation(
                            out=tok_cols(dstv, nseg, q, H, L), in_=bank_bf(bk)[:, dt * 128:(dt + 1) * 128], func=AF.Identity,
                            scale=modp[:, l, a_slot, dt:dt + 1], bias=modp[:, l, b_slot, dt:dt + 1]),
                            r=[b_bank[bk], b_modp], w=[dst_buf])
                    else:
                        P.op("dve", lambda e, dt=dt, bk=bk, dstv=dstv, l=l, a_slot=a_slot, b_slot=b_slot: e.scalar_tensor_tensor(
                            out=tok_cols(dstv, nseg, q, H, L), in0=bank_bf(bk)[:, dt * 128:(dt + 1) * 128],
                            scalar=modp[:, l, a_slot, dt:dt + 1],
                            in1=modp[:, l, b_slot, dt:dt + 1].to_broadcast([128, 128]), op0=ALU.mult, op1=ALU.add),
                            r=[b_bank[bk], b_modp], w=[dst_buf])
                return i
            else:
                am = mods[:, l, a_slot, :, :].unsqueeze(3).to_broadcast([128, NDT, NSB, LS])
                bm = mods[:, l, b_slot, :, :].unsqueeze(3).to_broadcast([128, NDT, NSB, LS])
                P.op("dve", lambda e, bk=bk, am=am: e.tensor_tensor(
                    out=tmp_tok[:, 0, :].rearrange("p (k s c) -> p k s c", k=NDT, s=NSB),
                    in0=bank_bf(bk).rearrange("p (k s c) -> p k s c", k=NDT, s=NSB), in1=am, op=ALU.mult),
                    r=[b_bank[bk], b_mods], w=[b_tmp[0]])
                dstv = seg4(dst_tile[:, :, 0:nseg * W], nseg)[:, :, :, H:H + L]
                P.op("dve", lambda e, bm=bm, dstv=dstv: e.tensor_tensor(
                    out=dstv, in0=tmp_tok[:, 0, :].rearrange("p (k s c) -> p k s c", k=NDT, s=NSB),
                    in1=bm, op=ALU.add), r=[b_tmp[0], b_mods], w=[dst_buf])
            return i

        def resid_update(blk, slot, q, acc_banks, l, which, srcs=None, final_out=False):
            full = None
            if srcs is None:
                srcs = [(banks[acc_banks[0]][:], b_bank[acc_banks[0]]), (banks[acc_banks[1]][:], b_bank[acc_banks[1]])]
                if acc_banks[1] == acc_banks[0] + 1:
                    full = (psall[:, acc_banks[0]:acc_banks[0] + 2, :], [b_bank[acc_banks[0]], b_bank[acc_banks[1]]])
            elif srcs == "mglu":
                full = (mglu[:, q, :].rearrange("p (a c) -> p a c", a=2), [b_mglu[q]])
            Gt = Gp if blk["kind"] == "p" else Gs
            bG = b_Gp if blk["kind"] == "p" else b_Gs
            i0 = None
            i = new_ss()
            if full is not None:
                fap, fbufs = full
                P.op("act", lambda e, fap=fap, i=i: e.activation(out=junk[:].rearrange("p (a c) -> p a c", a=2), in_=fap, func=AF.Square,
                                                                 accum_out=ssb[:, i:i + 1]),
                     r=list(fbufs) + [b_ss[i]], w=[b_junk, b_ss[i]])
            else:
                j = new_ss()
                for half, col in ((0, i), (1, j)):
                    sap, sbuf_ = srcs[half]
                    P.op("act", lambda e, sap=sap, col=col: e.activation(out=junk[:, 0:512], in_=sap, func=AF.Square,
                                                                         accum_out=ssb[:, col:col + 1]),
                         r=[sbuf_, b_ss[col]], w=[b_junk, b_ss[col]])
                P.op("dve", lambda e, i=i, j=j: e.tensor_tensor(out=ssb[:, i:i + 1], in0=ssb[:, i:i + 1], in1=ssb[:, j:j + 1],
                                                                op=ALU.add), r=[b_ss[i], b_ss[j]], w=[b_ss[i]])
            P.op("act", lambda e, i=i: e.activation(out=ssb[:, i:i + 1], in_=ssb[:, i:i + 1], func=AF.Sqrt,
                                                    scale=1.0 / D, bias=epsc[:, 0:1]), r=[b_ss[i]], w=[b_ss[i]])
            P.op("dve", lambda e, i=i: e.reciprocal(out=ssb[:, i:i + 1], in_=ssb[:, i:i + 1]), r=[b_ss[i]], w=[b_ss[i]])
            ts = q % 2
            if full is not None:
                fap, fbufs = full
                P.op("dve", lambda e, fap=fap, i=i, ts=ts, Gt=Gt: e.scalar_tensor_tensor(
                    out=tmp_tok[:, ts, :].rearrange("p (a c) -> p a c", a=2), in0=fap, scalar=ssb[:, i:i + 1],
                    in1=Gt[:, l, which, :].rearrange("p (a c) -> p a c", a=2), op0=ALU.mult, op1=ALU.mult),
                    r=list(fbufs) + [b_ss[i], bG], w=[b_tmp[ts]])
            for half in range(2 if full is None else 0):
                sap, sbuf_ = srcs[half]
                P.op("dve", lambda e, sap=sap, half=half, i=i, ts=ts, Gt=Gt: e.scalar_tensor_tensor(
                    out=tmp_tok[:, ts, half * 512:(half + 1) * 512], in0=sap, scalar=ssb[:, i:i + 1],
                    in1=Gt[:, l, which, half * 512:(half + 1) * 512], op0=ALU.mult, op1=ALU.mult),
                    r=[sbuf_, b_ss[i], bG], w=[b_tmp[ts]])
            if final_out:
                P.op("dve", lambda e, ts=ts: e.tensor_tensor(out=tmp_tok[:, ts, :], in0=xres[:, slot, q, :], in1=tmp_tok[:, ts, :],
                                                             op=ALU.add), r=[b_tmp[ts], b_xres[slot][q]], w=[b_tmp[ts]])
            else:
                P.op("dve", lambda e, ts=ts: e.tensor_tensor(out=xres[:, slot, q, :], in0=xres[:, slot, q, :], in1=tmp_tok[:, ts, :],
                                                             op=ALU.add), r=[b_tmp[ts], b_xres[slot][q]], w=[b_xres[slot][q]])

        wsteps = []
        wstate = {"issued": 0, "used": 0}

        def issue_weights(upto):
            while wstate["issued"] < min(upto, len(wsteps)):
                n = wstate["issued"]
                l, i = wsteps[n]
                s = n % NS
                P.dma("sp", lambda e, l=l, i=i, s=s: e.dma_start(out=wus[:, s].rearrange("p k h c -> p (k h c)"), in_=wu_scr[l, i]),
                      r=[b_wscr], w=[b_ws[s]], key=f"ws{s}")
                P.dma("sp", lambda e, l=l, i=i, s=s: e.dma_start(out=wds[:, s].rearrange("p j d -> p (j d)"), in_=wd_scr[l, i]),
                      r=[b_wscr], w=[b_ws[s]], key=f"ws{s}")
                wstate["issued"] += 1

        blocks = []
        if do_sample:
            blocks.append(dict(kind="s", nseg=NSB, L=LS, nt=1, pb=0))
        for pb in range(n_pblocks):
            blocks.append(dict(kind="p", nseg=1, L=LB, nt=2, pb=pb))
        for blk in blocks:
            for l in range(nlayers):
                for i in range(NSS):
                    wsteps.append((l, i))

        hT0_pp = [0]
        h2T_pp = [0]

        def ffn(blk, slot, l):
            nseg, L, nt = blk["nseg"], blk["L"], blk["nt"]
            W = HC + L
            ncols = nseg * W
            cur = 0
            h2 = h2T[:, cur]
            if blk["kind"] == "p":
                if blk["pb"] == 0:
                    P.op("pool", lambda e, h2=h2: e.memset(h2[:, :, 0:HC], 0.0), w=[b_h2T[cur]])
                else:
                    P.op("pool", lambda e, h2=h2: e.tensor_copy(out=h2[:, :, 0:HC], in_=hist2[:, l]),
                         r=[b_hist2[l]], w=[b_h2T[cur]])
            if blk["kind"] == "s":
                P.op("pool", lambda e, h2=h2: e.memset(h2[:, :, 0:ncols], 0.0), w=[b_h2T[cur]])
            for q in range(nt):
                prenorm(blk, slot, q, l, 1, h2, b_h2T[cur], HC, W)
            if blk["kind"] == "p":
                P.op("pool", lambda e, h2=h2: e.tensor_copy(out=hist2[:, l], in_=h2[:, :, LB:LB + HC]),
                     r=[b_h2T[cur]], w=[b_hist2[l]])
            last_p = blk["kind"] == "p" and blk["pb"] == n_pblocks - 1

            def up_mm(i, half, bk, s):
                j = i % 2
                for kt in range(NDT):
                    outp = banks[bk][:, 0:ncols]
                    rhs = h2[:, kt, 0:ncols]
                    P.op("pe", lambda e, kt=kt, half=half, s=s, outp=outp, rhs=rhs, j=j: e.matmul(
                        outp, wus[:, s, kt, half, j * 128:(j + 1) * 128], rhs, start=(kt == 0), stop=(kt == NDT - 1)),
                        r=[b_ws[s], b_h2T[cur]], w=[b_bank[bk]])

            def elementwise(i, bkg, bkv):
                ps_ = i % 2
                for half, bk, ct, bct in ((0, bkg, cg, b_cg), (1, bkv, cv, b_cv)):
                    ft = i + half * NGT
                    upv = seg3(banks[bk][:, 0:ncols], nseg)
                    if blk["kind"] == "s":
                        P.op("act", lambda e, upv=upv, ft=ft: e.activation(out=upv[:, :, 0:HC],
                                                                           in_=seg3(stT[:, l, ft, :], nseg), func=AF.Copy),
                             r=[b_stT, b_bank[bk]], w=[b_bank[bk]])
                    cvw = seg3(ct[:, ps_, 0:nseg * L], nseg)
                    wc = [fmp[:, FM_CW + (l * 3 + k) * NFT + ft:FM_CW + (l * 3 + k) * NFT + ft + 1] for k in range(3)]
                    bcol = fmp[:, FM_CB + l * NFT + ft:FM_CB + l * NFT + ft + 1]
                    P.op("act", lambda e, upv=upv, cvw=cvw, wc=wc, bcol=bcol: e.activation(
                        out=cvw, in_=upv[:, :, 2:W], func=AF.Identity, scale=wc[2], bias=bcol),
                        r=[b_bank[bk], b_fmp], w=[bct[ps_]])
                    P.op("dve", lambda e, upv=upv, cvw=cvw, wc=wc: e.scalar_tensor_tensor(
                        out=cvw, in0=upv[:, :, 1:W - 1], scalar=wc[1], in1=cvw, op0=ALU.mult, op1=ALU.add),
                        r=[b_bank[bk], b_fmp, bct[ps_]], w=[bct[ps_]])
                    P.op("dve", lambda e, upv=upv, cvw=cvw, wc=wc: e.scalar_tensor_tensor(
                        out=cvw, in0=upv[:, :, 0:W - 2], scalar=wc[0], in1=cvw, op0=ALU.mult, op1=ALU.add),
                        r=[b_bank[bk], b_fmp, bct[ps_]], w=[bct[ps_]])
                    if blk["kind"] == "s" or last_p:
                        dstc = seg3(cstage[:, 0, ft, 0:nseg * HC], nseg)
                        P.op("act", lambda e, upv=upv, dstc=dstc: e.activation(out=dstc, in_=upv[:, :, W - HC:W], func=AF.Copy),
                             r=[b_bank[bk]], w=[b_cstage])
                P.op("act", lambda e, ps_=ps_: e.activation(out=gl[:, ps_, 0:nseg * L], in_=cg[:, ps_, 0:nseg * L], func=AF.Gelu),
                     r=[b_cg[ps_]], w=[b_gl[ps_]])
                g3 = i % 3
                P.op("dve", lambda e, ps_=ps_, g3=g3: e.tensor_tensor(out=gv[:, g3, 0:nseg * L], in0=gl[:, ps_, 0:nseg * L],
                                                                       in1=cv[:, ps_, 0:nseg * L], op=ALU.mult),
                     r=[b_gl[ps_], b_cv[ps_]], w=[b_gv[g3]])

            def down_mm(i, s):
                g3 = i % 3
                j = i % 2
                for q in range(nt):
                    for half in range(2):
                        bk = 2 * q + half
                        P.op("pe", lambda e, q=q, half=half, bk=bk, s=s, g3=g3, j=j: e.matmul(
                            banks[bk][:], gv[:, g3, q * 128:(q + 1) * 128], wds[:, s, j, half * 512:(half + 1) * 512],
                            start=(i == 0), stop=(i == NGT - 1)), r=[b_gv[g3], b_ws[s]], w=[b_bank[bk]])

            base = wstate["used"]
            issue_weights(base + NS - 1)
            pend = None
            for i in range(NGT):
                n = base + i // 2
                s = n % NS
                bkg, bkv = (4, 5) if i % 2 == 0 else (6, 7)
                up_mm(i, 0, bkg, s)
                up_mm(i, 1, bkv, s)
                elementwise(i, bkg, bkv)
                if pend is not None:
                    down_mm(*pend)
                    issue_weights(base + i // 2 + NS - 1)
                pend = (i, s)
            down_mm(*pend)
            wstate["used"] = base + NSS
            issue_weights(wstate["used"] + NS - 1)
            for q in range(nt):
                resid_update(blk, slot, q, (2 * q, 2 * q + 1), l, 1, final_out=(l == nlayers - 1))

        def pool_mixer(blk, slot, l):
            nseg, L, nt = blk["nseg"], blk["L"], blk["nt"]
            W = HP + L
            ncols = nseg * W
            cur = 0
            hT = hT0[:, cur]
            if blk["kind"] == "p":
                if blk["pb"] == 0:
                    P.op("pool", lambda e, hT=hT: e.memset(hT[:, :, 0:HP], 0.0), w=[b_hT0[cur]])
                else:
                    P.op("pool", lambda e, hT=hT: e.tensor_copy(out=hT[:, :, 0:HP], in_=hist0[:]),
                         r=[b_hist0], w=[b_hT0[cur]])
            else:
                for hb in range(2):
                    P.dma("sp", lambda e, hb=hb: e.dma_start(out=tmp_tok[0:120, hb, :],
                                                             in_=spool[hb * 8:(hb + 1) * 8].rearrange("b k d -> (b k) d")),
                          w=[b_tmp[hb]], key=f"tmp{hb}")
                    P.op("dve", lambda e, hb=hb: e.tensor_copy(out=xn[0:120, hb, :], in_=tmp_tok[0:120, hb, :]),
                         r=[b_tmp[hb]], w=[b_xn[hb]])
                    bk = 6 + hb
                    for dt in range(NDT):
                        P.op("pe", lambda e, hb=hb, dt=dt, bk=bk: e.transpose(
                            out=bank_bf(bk)[:, dt * 128:dt * 128 + 120], in_=xn[0:120, hb, dt * 128:(dt + 1) * 128],
                            identity=identb[0:120, 0:120]), r=[b_xn[hb], b_ident], w=[b_bank[bk]])
                    dstv = seg4(hT[:, :, 0:ncols], nseg)[:, :, hb * 8:(hb + 1) * 8, 0:HP]
                    P.op("act", lambda e, bk=bk, dstv=dstv: e.activation(
                        out=dstv, in_=bank_bf(bk).rearrange("p (k c) -> p k c", k=NDT)[:, :, 0:120].rearrange(
                            "p k (b c) -> p k b c", b=8), func=AF.Copy), r=[b_bank[bk]], w=[b_hT0[cur]])
            for q in range(nt):
                i_ss = prenorm(blk, slot, q, l, 0, hT, b_hT0[cur], HP, W)
                if blk["kind"] == "s" or (blk["pb"] == n_pblocks - 1 and q == nt - 1):
                    new_pool_rows(blk, slot, q, l, i_ss)
            if blk["kind"] == "p":
                P.op("pool", lambda e, hT=hT: e.tensor_copy(out=hist0[:], in_=hT[:, :, LB:LB + HP]),
                     r=[b_hT0[cur]], w=[b_hist0])
            for gi, w_ in enumerate((2, 4, 8, 16)):
                src = seg4(hT[:, 2 * gi:2 * gi + 2, 0:ncols], nseg)
                ta = seg4(pt[:, 0, :, 0:ncols], nseg)
                tb = seg4(pt[:, 1, :, 0:ncols], nseg)
                P.op("dve", lambda e, src=src, ta=ta: e.tensor_tensor(out=ta[:, :, :, 1:W], in0=src[:, :, :, 1:W],
                                                                       in1=src[:, :, :, 0:W - 1], op=ALU.add),
                     r=[b_hT0[cur]], w=[b_pt[0]])
                curt, curb, oth, othb = ta, b_pt[0], tb, b_pt[1]
                lo = 1
                sh = 2
                while sh < w_:
                    P.op("dve", lambda e, curt=curt, oth=oth, lo=lo, sh=sh: e.tensor_tensor(
                        out=oth[:, :, :, lo + sh:W], in0=curt[:, :, :, lo + sh:W], in1=curt[:, :, :, lo:W - sh], op=ALU.add),
                        r=[curb], w=[othb])
                    curt, curb, oth, othb = oth, othb, curt, curb
                    lo += sh
                    sh *= 2
                pv = seg4(pooled[:, 2 * gi:2 * gi + 2, 0:nseg * L], nseg)
                P.op("dve", lambda e, curt=curt, src=src, pv=pv, w_=w_: e.scalar_tensor_tensor(
                    out=pv, in0=curt[:, :, :, HP:W], scalar=1.0 / w_, in1=src[:, :, :, HP:W], op0=ALU.mult, op1=ALU.subtract),
                    r=[curb, b_hT0[cur]], w=[b_pooled])
                if blk["kind"] == "p" and blk["pb"] == 0:
                    for d2 in range(2):
                        P.op("dve", lambda e, curt=curt, gi=gi, d2=d2: e.tensor_tensor(
                            out=pt[:, 0, d2, 0:HP], in0=curt[:, d2, 0, HP:2 * HP], in1=inv15[:, gi, :], op=ALU.mult),
                            r=[curb, b_inv15], w=[b_pt[0]] if curb is not b_pt[0] else [b_pt[0]])
                        P.op("dve", lambda e, gi=gi, d2=d2, src=src: e.tensor_tensor(
                            out=pooled[:, 2 * gi + d2, 0:HP], in0=pt[:, 0, d2, 0:HP], in1=src[:, d2, 0, HP:2 * HP], op=ALU.subtract),
                            r=[b_pt[0], b_hT0[cur]], w=[b_pooled])
            for q in range(nt):
                for gi in range(4):
                    bk = 2 * q + gi // 2
                    for kt in range(2):
                        pv = seg3(pooled[:, 2 * gi + kt, 0:nseg * L], nseg)
                        lhsT = tok_cols(pv, nseg, q, 0, L)
                        P.op("pe", lambda e, gi=gi, kt=kt, bk=bk, lhsT=lhsT: e.matmul(
                            banks[bk][:, (gi % 2) * 256:(gi % 2 + 1) * 256], lhsT, Wp[:, gi, kt, :],
                            start=(kt == 0), stop=(kt == 1)), r=[b_pooled, b_Wp], w=[b_bank[bk]])
                resid_update(blk, slot, q, (2 * q, 2 * q + 1), l, 0)


        def new_pool_rows(blk, slot, q, l, i):
            isp = blk["kind"] == "p"
            xr = xres[:, slot, q, :]
            P.op("dve", lambda e, xr=xr, i=i: e.tensor_scalar(out=tmp_tok[:, 0, :], in0=xr, scalar1=ssb[:, i:i + 1], scalar2=None, op0=ALU.mult),
                 r=[b_xres[slot][q], b_ss[i]], w=[b_tmp[0]])
            for dt in range(NDT):
                bk = 4 + dt // 4
                P.op("pe", lambda e, dt=dt, bk=bk: e.transpose(out=banks[bk][:, (dt % 4) * 128:(dt % 4 + 1) * 128],
                                                               in_=tmp_tok[:, 0, dt * 128:(dt + 1) * 128], identity=identf[:]),
                     r=[b_tmp[0], b_ident], w=[b_bank[bk]])
            hTf = tmp_tok[:, 1, :].rearrange("p (k c) -> p k c", k=NDT)
            if isp:
                for dt in range(NDT):
                    bk = 4 + dt // 4
                    P.op("act", lambda e, dt=dt, bk=bk: e.activation(
                        out=hTf[:, dt, :], in_=banks[bk][:, (dt % 4) * 128:(dt % 4 + 1) * 128], func=AF.Identity,
                        scale=modp[:, l, 0, dt:dt + 1], bias=modp[:, l, 1, dt:dt + 1]), r=[b_bank[bk], b_modp], w=[b_tmp[1]])
            else:
                am = mods[:, l, 0, :, :].unsqueeze(3).to_broadcast([128, NDT, NSB, LS])
                bm = mods[:, l, 1, :, :].unsqueeze(3).to_broadcast([128, NDT, NSB, LS])
                for h4 in range(2):
                    bk = 4 + h4
                    hv = tmp_tok[:, 1, h4 * 512:(h4 + 1) * 512].rearrange("p (k s c) -> p k s c", k=4, s=NSB)
                    P.op("dve", lambda e, bk=bk, hv=hv, am=am, h4=h4: e.tensor_tensor(
                        out=hv, in0=banks[bk][:].rearrange("p (k s c) -> p k s c", k=4, s=NSB), in1=am[:, h4 * 4:(h4 + 1) * 4], op=ALU.mult),
                        r=[b_bank[bk], b_mods], w=[b_tmp[1]])
                    P.op("dve", lambda e, hv=hv, bm=bm, h4=h4: e.tensor_tensor(out=hv, in0=hv, in1=bm[:, h4 * 4:(h4 + 1) * 4], op=ALU.add),
                         r=[b_tmp[1], b_mods], w=[b_tmp[1]])
            for dt in range(NDT):
                bk = 6 + dt // 4
                P.op("pe", lambda e, dt=dt, bk=bk: e.transpose(out=banks[bk][:, (dt % 4) * 128:(dt % 4 + 1) * 128],
                                                               in_=hTf[:, dt, :], identity=identf[:]),
                     r=[b_tmp[1], b_ident], w=[b_bank[bk]])
            for h4 in range(2):
                P.op("act", lambda e, h4=h4: e.activation(out=tmp_tok[:, 0, h4 * 512:(h4 + 1) * 512], in_=banks[6 + h4][:], func=AF.Copy),
                     r=[b_bank[6 + h4]], w=[b_tmp[0]])
            if isp:
                P.dma("sp", lambda e: e.dma_start(out=npp[:, :], in_=tmp_tok[128 - HP:128, 0, :]), r=[b_tmp[0]], key="tmp0", store=True)
            else:
                for b_ in range(NSB):
                    P.dma("sp", lambda e, b_=b_: e.dma_start(out=nps[b_, HP - LS:HP, :], in_=tmp_tok[b_ * LS:(b_ + 1) * LS, 0, :]),
                          r=[b_tmp[0]], key="tmp0", store=True)
                P.dma("sp", lambda e: e.dma_start(out=nps[:, 0:HP - LS, :], in_=spool[:, LS:HP, :]), key="npsd", store=True)

        def conv_out(blk, l):
            isp = blk["kind"] == "p"
            ncol = 2 if isp else NSB * HC
            dst = ncp[l] if isp else ncs[l].rearrange("b k f -> (b k) f")
            for f0 in range(0, NFT, 4):
                for j in range(4):
                    P.op("pe", lambda e, f0=f0, j=j: e.transpose(out=banks[4][0:ncol, j * 128:(j + 1) * 128],
                                                                 in_=cstage[:, 0, f0 + j, 0:ncol], identity=identf[:]),
                         r=[b_cstage, b_ident], w=[b_bank[4]])
                P.op("act", lambda e: e.activation(out=cso[0:ncol, :], in_=banks[4][0:ncol, :], func=AF.Copy), r=[b_bank[4]], w=[b_cso])
                P.dma("sp", lambda e, f0=f0, dst=dst: e.dma_start(out=dst[:, f0 * 128:(f0 + 4) * 128], in_=cso[0:ncol, :]),
                      r=[b_cso], key="cso", store=True)

        if not skip_s5 and nlayers > 1:
            hT1 = sb("hT1", [128, NDT, LB], BF16); b_hT1 = P.buf("hT1")
            opsb = sb("opsb", [128, 2, 4, 8, 128], BF16); tabs = sb("tabs", [128, 2, 3 * 8 * 33]); b_opsb = P.bufs(2, "opsb")
            h_cm = sb("h_cm", [32, 8, 128], BF16); b_hcm = P.buf("hcm")
            Ub = sb("Ub", [128, 2, 8, 32], BF16); b_U = P.bufs(2, "U")
            s5t = sb("s5t", [128, 4, 264]); b_s5t = P.bufs(4, "s5t")
            Xb = sb("Xb", [128, 264], BF16); b_Xb = P.buf("Xb")
            carry = sb("carry", [128, 64]); b_carry = P.buf("carry")
            x0b = sb("x0b", [128, 8, NSB]); b_x0b = P.buf("x0b")
            x0tok = sb("x0tok", [NSB, 8, 128]); b_x0tok = P.buf("x0tok")
            xfin = sb("xfin", [128, 64, NSB]); b_xfin = P.buf("xfin")
            gy = sb("gy", [128, 8 * 32], BF16); b_gy = P.buf("gy")
            gy_cm = sb("gy_cm", [32, 8, 128], BF16); b_gycm = P.buf("gycm")
            gyT = sb("gyT", [128, NDT, LB], BF16); b_gyT = P.buf("gyT")
            perm = sb("perm", [128, 128]); b_perm = P.buf("perm")
            mglu = sb("mglu", [128, 2, D]); b_mglu = P.bufs(2, "mglu")
            b_sg = b_pt
            finsb = mglu[0:64].rearrange("p q (b c) -> p (q b) c", c=128)
            P.op("dve", lambda e: e.tensor_copy(out=perm[:, 0:64], in_=identf[:, 64:128]), r=[b_ident], w=[b_perm])
            P.op("dve", lambda e: e.tensor_scalar(out=perm[:, 64:128], in0=identf[:, 0:64], scalar1=-1.0, scalar2=None, op0=ALU.mult),
                 r=[b_ident], w=[b_perm])
            P.op("pool", lambda e: e.memset(carry[:], 0.0), w=[b_carry])
            stTf = stT[:].rearrange("p l f c -> p (l f c)")
            xff = xfin[:].rearrange("p g b -> p (g b)")
            s5_opsv = [opsb[:, 0], opsb[:, 1], stTf[:, 0:2048].bitcast(BF16).rearrange("p (k g c) -> p k g c", k=4, g=8)]
            s5_tabv = [tabs[:, 0], tabs[:, 1], xff[:, 0:792]]
            s5_bops = [b_opsb[0], b_opsb[1], P.buf("opsb2")]
            s5_Uv = [Ub[:, 0], Ub[:, 1], xff[:, 792:920].bitcast(BF16).rearrange("p (g j) -> p g j", g=8)]
            s5_bU = [b_U[0], b_U[1], P.buf("U2")]
            s5_Xbv = [Xb[:], stTf[:, 2048:2180].bitcast(BF16)]
            s5_bXb = [b_Xb, P.buf("Xb1")]
            s5_bglu = P.bufs(2, "glu")

        def s5_alias_guard():
            P.op("act", lambda e: e.activation(out=epsc[:, 0:1], in_=epsc[:, 0:1], func=AF.Copy),
                 r=[b_xfin, b_stT], w=[b_xfin, s5_bops[2], s5_bU[2], s5_bXb[1]])
            P.op("dve", lambda e: e.tensor_copy(out=ssb[:, 15:16], in_=ssb[:, 15:16]), r=[b_ss[15]], w=[b_ss[15], s5_bglu[0], s5_bglu[1]])

        s5_pref = {"ops": set(), "glu": False}

        def s5_prefetch():
            n_ = 8 * 33
            for bt in (0, 1):
                ov, tv, bo = s5_opsv[bt % 3], s5_tabv[bt % 3], s5_bops[bt % 3]
                key = f"opsb{bt % 3}"
                P.dma("sp", lambda e, bt=bt, ov=ov: e.dma_start(out=ov.rearrange("p k g c -> p (k g c)"), in_=s5ops[bt]),
                      r=[b_wscr], w=[bo], key=key)
                tsrc = s5tab_p[:, :, bt * n_:(bt + 1) * n_].rearrange("k p c -> p k c")
                P.dma("sp", lambda e, tv=tv, tsrc=tsrc: e.dma_start(out=tv[:, 0:3 * n_].rearrange("p (k c) -> p k c", k=3), in_=tsrc),
                      r=[b_wscr], w=[bo], key=key)
                s5_pref["ops"].add(bt)
            for ab in range(2):
                gv_ = Gs[:].rearrange("p a b d -> p (a b d)")[:, ab * 2048:(ab + 1) * 2048].bitcast(BF16).rearrange("p (k c) -> p k c", k=NDT)
                P.dma("sp", lambda e, ab=ab, gv_=gv_: e.dma_start(
                    out=gv_, in_=wg_scr[ab].rearrange("p (kt n) -> p kt n", kt=NDT)[:, :, 0:512]),
                    r=[b_wscr], w=[s5_bglu[ab]], key="glu%d" % ab)
            s5_pref["glu"] = True

        def s5_mixer(blk, slot, l):
            nseg, L, nt = blk["nseg"], blk["L"], blk["nt"]
            isp = blk["kind"] == "p"
            ntok = nseg * L
            NCH = ntok // 8
            CW = 33 if isp else 32
            n = 8 * CW
            NR = 3 if isp else 2
            NX = 2 if isp else 1
            for q in range(nt):
                prenorm(blk, slot, q, l, 0, hT1, b_hT1, 0, L)
            if isp and blk["pb"] == 0:
                P.op("pool", lambda e: e.memset(carry[:], 0.0), r=[b_carry], w=[b_carry])

            def v3(ap):
                if isp:
                    return ap.rearrange("p (g c) -> p g c", g=8)
                return ap.rearrange("p (g b c) -> p g b c", g=8, b=NSB)

            def ops_v(bt):
                return s5_opsv[bt % NR], s5_tabv[bt % NR], s5_bops[bt % NR]

            def U_v(bt):
                return s5_Uv[bt % NR], s5_bU[bt % NR]

            def Xb_v(bt):
                return s5_Xbv[bt % NX], s5_bXb[bt % NX]

            def load_ops(bt):
                ov, tv, bo = ops_v(bt)
                key = f"opsb{bt % NR}"
                P.dma("sp", lambda e, bt=bt, ov=ov: e.dma_start(out=ov.rearrange("p k g c -> p (k g c)"), in_=s5ops[bt]),
                      r=[b_wscr], w=[bo], key=key)
                tsrc = (s5tab_p if isp else s5tab_s)[:, :, bt * n:(bt + 1) * n].rearrange("k p c -> p k c")
                P.dma("sp", lambda e, tv=tv, tsrc=tsrc: e.dma_start(out=tv[:, 0:3 * n].rearrange("p (k c) -> p k c", k=3), in_=tsrc),
                      r=[b_wscr], w=[bo], key=key)

            def s1a(bt):
                if isp and bt in s5_pref["ops"]:
                    s5_pref["ops"].discard(bt)
                else:
                    load_ops(bt)
                for r_ in range(8):
                    P.op("pe", lambda e, r_=r_, bt=bt: e.transpose(out=bank_bf(0)[0:NCH, r_ * 128:(r_ + 1) * 128],
                                                                   in_=hT1[:, bt, r_:ntok:8], identity=identb[:]),
                         r=[b_hT1, b_ident], w=[b_bank[0]])
                P.op("act", lambda e: e.activation(
                    out=h_cm[0:NCH].rearrange("p g (r c) -> p r g c", r=8),
                    in_=bank_bf(0)[0:NCH, :].rearrange("p (r g c) -> p r g c", r=8, g=8), func=AF.Copy),
                     r=[b_bank[0]], w=[b_hcm])
                for g8 in range(8):
                    P.op("pe", lambda e, g8=g8: e.transpose(out=bank_bf(1)[:, g8 * NCH:(g8 + 1) * NCH],
                                                            in_=h_cm[0:NCH, g8, :], identity=identb[0:NCH, 0:NCH]),
                         r=[b_hcm, b_ident], w=[b_bank[1]])
                Uv, bU = U_v(bt)
                P.op("act", lambda e, Uv=Uv: e.activation(out=Uv[:, :, 0:NCH], in_=bank_bf(1)[:, 0:8 * NCH].rearrange("p (g j) -> p g j", g=8),
                                                          func=AF.Copy), r=[b_bank[1]], w=[bU])

            def s1b(bt):
                ov, tv, bo = ops_v(bt)
                Uv, bU = U_v(bt)
                for kind, bk in ((0, 2), (1, 3)):
                    for g8 in range(8):
                        pv = v3(banks[bk][:, 0:n])
                        outp = pv[:, g8, 1:33] if isp else pv[:, g8, :, 1]
                        P.op("pe", lambda e, kind=kind, g8=g8, outp=outp, ov=ov, Uv=Uv: e.matmul(
                            outp, ov[:, kind, g8, :], Uv[:, g8, 0:NCH], start=True, stop=True),
                            r=[bo, bU], w=[b_bank[bk]])

            def s2(bt):
                ov, tv, bo = ops_v(bt)
                Xbv, bXb = Xb_v(bt)
                if not isp:
                    for hh, src in enumerate((sre, sim)):
                        P.dma("sp", lambda e, hh=hh, src=src, bt=bt: e.dma_start(out=x0tok[:, :, hh * 64:(hh + 1) * 64],
                                                                                 in_=src[:, bt * 8:(bt + 1) * 8, :]),
                              w=[b_x0tok], key="x0tok")
                    for g8 in range(8):
                        P.op("pe", lambda e, g8=g8: e.transpose(out=banks[7][:, g8 * NSB:(g8 + 1) * NSB], in_=x0tok[:, g8, :],
                                                                identity=identf[0:NSB, 0:NSB]), r=[b_x0tok, b_ident], w=[b_bank[7]])
                    P.op("act", lambda e: e.activation(out=x0b[:].rearrange("p g b -> p (g b)"), in_=banks[7][:, 0:8 * NSB], func=AF.Copy),
                         r=[b_bank[7]], w=[b_x0b])
                COS = tv[:, 0:n]; SIN = tv[:, n:2 * n]; M2 = tv[:, 2 * n:3 * n]
                t1 = s5t[:, 0, 0:n]; Sp = s5t[:, 1, 0:n]; V = s5t[:, 2, 0:n]; X = s5t[:, 3, 0:n]
                def nc_(ap):
                    return v3(ap)[:, :, 1:33] if isp else v3(ap)[:, :, :, 1]
                P.op("dve", lambda e, COS=COS, t1=t1: e.tensor_tensor(out=nc_(t1), in0=nc_(banks[2][:, 0:n]), in1=nc_(COS), op=ALU.mult),
                     r=[b_bank[2], bo], w=[b_s5t[0]])
                P.op("dve", lambda e, SIN=SIN, Sp=Sp: e.tensor_tensor(out=nc_(Sp), in0=nc_(banks[3][:, 0:n]), in1=nc_(SIN), op=ALU.mult),
                     r=[b_bank[3], bo], w=[b_s5t[1]])
                P.op("dve", lambda e, Sp=Sp, t1=t1: e.tensor_tensor(out=nc_(Sp), in0=nc_(Sp), in1=nc_(t1), op=ALU.add),
                     r=[b_s5t[0], b_s5t[1]], w=[b_s5t[1]])
                if isp:
                    P.op("dve", lambda e, Sp=Sp, bt=bt: e.tensor_copy(out=v3(Sp)[:, :, 0], in_=carry[:, bt * 8:(bt + 1) * 8]),
                         r=[b_carry, b_s5t[1]], w=[b_s5t[1]])
                else:
                    P.op("dve", lambda e, Sp=Sp: e.tensor_copy(out=v3(Sp)[:, :, :, 0], in_=x0b[:]),
                         r=[b_x0b, b_s5t[1]], w=[b_s5t[1]])
                P.op("dve", lambda e, M2=M2, Sp=Sp, V=V: e.tensor_tensor_scan(out=V, data0=M2, data1=Sp, initial=0.0,
                                                                             op0=ALU.mult, op1=ALU.add),
                     r=[b_s5t[1], bo], w=[b_s5t[2]])
                P.op("pe", lambda e, V=V: e.matmul(banks[4][:, 0:n], perm[:], V, start=True, stop=True),
                     r=[b_perm, b_s5t[2]], w=[b_bank[4]])
                P.op("dve", lambda e, COS=COS, t1=t1, V=V: e.tensor_tensor(out=t1, in0=V, in1=COS, op=ALU.mult),
                     r=[b_s5t[2], bo, b_s5t[0]], w=[b_s5t[0]])
                P.op("dve", lambda e, SIN=SIN, X=X: e.tensor_tensor(out=X, in0=banks[4][:, 0:n], in1=SIN, op=ALU.mult),
                     r=[b_bank[4], bo], w=[b_s5t[3]])
                P.op("dve", lambda e, X=X, t1=t1: e.tensor_tensor(out=X, in0=t1, in1=X, op=ALU.subtract),
                     r=[b_s5t[0], b_s5t[3]], w=[b_s5t[3]])
                P.op("act", lambda e, X=X, Xbv=Xbv: e.activation(out=Xbv[:, 0:n], in_=X, func=AF.Copy), r=[b_s5t[3]], w=[bXb])
                if isp:
                    P.op("pool", lambda e, X=X, bt=bt: e.tensor_copy(out=carry[:, bt * 8:(bt + 1) * 8], in_=v3(X)[:, :, 32]),
                         r=[b_s5t[3], b_carry], w=[b_carry])
                else:
                    P.op("pool", lambda e, X=X, bt=bt: e.tensor_copy(out=xfin[:, bt * 8:(bt + 1) * 8, :], in_=v3(X)[:, :, :, 1]),
                         r=[b_s5t[3], b_xfin], w=[b_xfin])

            def s3(bt):
                ov, tv, bo = ops_v(bt)
                Uv, bU = U_v(bt)
                Xbv, bXb = Xb_v(bt)
                for g8 in range(8):
                    xv = v3(Xbv[:, 0:n])
                    xprev = xv[:, g8, 0:32] if isp else xv[:, g8, :, 0]
                    P.op("pe", lambda e, g8=g8, ov=ov, Uv=Uv: e.matmul(banks[5][:, g8 * NCH:(g8 + 1) * NCH], ov[:, 2, g8, :],
                                                                       Uv[:, g8, 0:NCH], start=True, stop=False),
                         r=[bo, bU], w=[b_bank[5]])
                    P.op("pe", lambda e, g8=g8, ov=ov, xprev=xprev: e.matmul(banks[5][:, g8 * NCH:(g8 + 1) * NCH], ov[:, 3, g8, :],
                                                                             xprev, start=False, stop=True),
                         r=[bo, bXb], w=[b_bank[5]])
                P.op("act", lambda e: e.activation(out=gy[:, 0:8 * NCH], in_=banks[5][:, 0:8 * NCH], func=AF.Gelu),
                     r=[b_bank[5]], w=[b_gy])
                for g8 in range(8):
                    P.op("pe", lambda e, g8=g8: e.transpose(out=bank_bf(6)[0:NCH, g8 * 128:(g8 + 1) * 128],
                                                            in_=gy[:, g8 * NCH:(g8 + 1) * NCH], identity=identb[:]),
                         r=[b_gy, b_ident], w=[b_bank[6]])
                P.op("act", lambda e: e.activation(
                    out=gy_cm[0:NCH].rearrange("p r (g c) -> p r g c", g=8),
                    in_=bank_bf(6)[0:NCH, :].rearrange("p (g r c) -> p r g c", g=8, r=8), func=AF.Copy),
                    r=[b_bank[6]], w=[b_gycm])
                for r_ in range(8):
                    P.op("pe", lambda e, r_=r_: e.transpose(out=bank_bf(7)[:, r_ * NCH:(r_ + 1) * NCH], in_=gy_cm[0:NCH, r_, :],
                                                            identity=identb[0:NCH, 0:NCH]), r=[b_gycm, b_ident], w=[b_bank[7]])
                P.op("dve", lambda e, bt=bt: e.tensor_copy(
                    out=gyT[:, bt, 0:ntok].rearrange("p (j r) -> p r j", r=8),
                    in_=bank_bf(7)[:, 0:8 * NCH].rearrange("p (r j) -> p r j", r=8)), r=[b_bank[7]], w=[b_gyT])

            if isp:
                if not s5_pref["glu"]:
                    for ab in range(2):
                        gv_ = Gs[:].rearrange("p a b d -> p (a b d)")[:, ab * 2048:(ab + 1) * 2048].bitcast(BF16).rearrange("p (k c) -> p k c", k=NDT)
                        P.dma("sp", lambda e, ab=ab, gv_=gv_: e.dma_start(
                            out=gv_, in_=wg_scr[ab].rearrange("p (kt n) -> p kt n", kt=NDT)[:, :, 0:512]),
                            r=[b_wscr], w=[s5_bglu[ab]], key="glu%d" % ab)
                s5_pref["glu"] = False
                s1a(0)
                s1b(0)
                for k in range(8):
                    if k + 1 < 8:
                        s1a(k + 1)
                    s2(k)
                    if k + 1 < 8:
                        s1b(k + 1)
                    if k >= 1:
                        s3(k - 1)
                s3(7)
            else:
                s1a(0)
                s1b(0)
                for k in range(8):
                    if k + 1 < 8:
                        s1a(k + 1)
                    s2(k)
                    if k + 1 < 8:
                        s1b(k + 1)
                    s3(k)
            wgsrc = [wg_scr[ab].rearrange("p (kt n) -> p kt n", kt=NDT) for ab in range(2)]
            if isp:
                gviews = [Gs[:].rearrange("p a b d -> p (a b d)")[:, sl_ * 2048:(sl_ + 1) * 2048].bitcast(BF16).rearrange(
                    "p (k c) -> p k c", k=NDT) for sl_ in range(2)]
                gbufs = s5_bglu
            else:
                gsl = (wstate["used"] + NS - 1) % NS
                gviews = [wus[:, gsl].rearrange("p k h c -> p k (h c)")] * 2
                gbufs = [b_ws[gsl], b_ws[gsl]]
            for hf in range(2):
                for ab in range(2):
                    if not (isp and hf == 0):
                        P.dma("sp", lambda e, ab=ab, hf=hf: e.dma_start(out=gviews[ab], in_=wgsrc[ab][:, :, hf * 512:(hf + 1) * 512]),
                              r=[b_wscr], w=[gbufs[ab]], key=("glu%d" % ab) if isp else f"ws{gsl}")
                    for q in range(nt):
                        bk = 2 * q + ab
                        for kt in range(NDT):
                            P.op("pe", lambda e, q=q, kt=kt, bk=bk, ab=ab: e.matmul(
                                banks[bk][:], gyT[:, kt, q * 128:(q + 1) * 128], gviews[ab][:, kt, :],
                                start=(kt == 0), stop=(kt == NDT - 1)), r=[b_gyT, gbufs[ab]], w=[b_bank[bk]])
                for q in range(nt):
                    P.op("act", lambda e, q=q: e.activation(out=pt[:, q].rearrange("p a c -> p (a c)")[:, 0:512], in_=banks[2 * q + 1][:], func=AF.Sigmoid),
                         r=[b_bank[2 * q + 1]], w=[b_sg[q]])
                    P.op("dve", lambda e, q=q, hf=hf: e.tensor_tensor(out=mglu[:, q, hf * 512:(hf + 1) * 512], in0=banks[2 * q][:],
                                                                     in1=pt[:, q].rearrange("p a c -> p (a c)")[:, 0:512], op=ALU.mult),
                         r=[b_bank[2 * q], b_sg[q]], w=[b_mglu[q]])
            for q in range(nt):
                resid_update(blk, slot, q, None, l, 0, srcs="mglu")

        def s5_outputs_sample():
            for b4 in range(0, NSB, 4):
                for bb in range(4):
                    b_ = b4 + bb
                    P.op("pe", lambda e, b_=b_, bb=bb: e.transpose(out=banks[4][0:64, bb * 128:(bb + 1) * 128], in_=xfin[:, :, b_],
                                                                   identity=identf[:]), r=[b_xfin, b_ident], w=[b_bank[4]])
                P.op("act", lambda e, b4=b4: e.activation(out=finsb[:, b4:b4 + 4, :].rearrange("p b c -> p (b c)"), in_=banks[4][0:64, :],
                                                          func=AF.Copy), r=[b_bank[4]], w=[b_mglu[0], b_mglu[1]])
            P.dma("sp", lambda e: e.dma_start(out=nrs.rearrange("b g p -> g b p"), in_=finsb[:, :, 0:64]), r=[b_mglu[0], b_mglu[1]], key="finsb", store=True)
            P.dma("sp", lambda e: e.dma_start(out=nis.rearrange("b g p -> g b p"), in_=finsb[:, :, 64:128]), r=[b_mglu[0], b_mglu[1]], key="finsb", store=True)

        def s5_outputs_prompt():
            P.op("pe", lambda e: e.transpose(out=banks[4][0:64, 0:128], in_=carry[:], identity=identf[:]),
                 r=[b_carry, b_ident], w=[b_bank[4]])
            P.op("act", lambda e: e.activation(out=finsb[:, 0, :], in_=banks[4][0:64, 0:128], func=AF.Copy), r=[b_bank[4], b_mglu[0], b_mglu[1]], w=[b_mglu[0], b_mglu[1]])
            P.dma("sp", lambda e: e.dma_start(out=nrp[:, :], in_=finsb[:, 0, 0:64]), r=[b_mglu[0], b_mglu[1]], key="finsb", store=True)
            P.dma("sp", lambda e: e.dma_start(out=nip[:, :], in_=finsb[:, 0, 64:128]), r=[b_mglu[0], b_mglu[1]], key="finsb", store=True)

        slot = 0
        for bi, blk in enumerate(blocks):
            nseg, L, nt = blk["nseg"], blk["L"], blk["nt"]
            for q in range(nt):
                src = xs[:, :] if blk["kind"] == "s" else xp[blk["pb"] * LB + q * 128: blk["pb"] * LB + (q + 1) * 128, :]
                P.dma("sp", lambda e, q=q, src=src, slot=slot: e.dma_start(out=xres[:, slot, q, :], in_=src),
                      w=[b_xres[slot][q]], key=f"x{slot}{q}")
            if blk["kind"] == "p" and not skip_s5 and nlayers > 1:
                s5_prefetch()
            for l in range(nlayers):
                if l % 2 == 0:
                    pool_mixer(blk, slot, l)
                else:
                    if not skip_s5:
                        s5_mixer(blk, slot, l)
                        if blk["kind"] == "s":
                            s5_outputs_sample()
                        elif blk["pb"] == n_pblocks - 1:
                            s5_outputs_prompt()
                ffn(blk, slot, l)
                if blk["kind"] == "s" or blk["pb"] == n_pblocks - 1:
                    conv_out(blk, l)
            for q in range(nt):
                dst = ys[:, :] if blk["kind"] == "s" else yp[blk["pb"] * LB + q * 128: blk["pb"] * LB + (q + 1) * 128, :]
                P.dma("sp", lambda e, q=q, dst=dst: e.dma_start(out=dst, in_=tmp_tok[:, q % 2, :]),
                      r=[b_tmp[q % 2]], key=f"tmp{q % 2}", store=True)
            if blk["kind"] == "s" and not skip_s5 and nlayers > 1:
                s5_alias_guard()
            slot = 0

        P.enabled = True
        P.barrier_wait("sp", P.stores)
        P.full_barrier()
        P.emit(nc, st)
    return nc


_NC_CACHE = {}


def kernel(**inputs):
    f32 = lambda a: np.ascontiguousarray(np.asarray(a, dtype=np.float32))
    inp = {k: f32(v) for k, v in inputs.items()}
    if "nc" not in _NC_CACHE:
        _NC_CACHE["nc"] = build_nc()
    nc = _NC_CACHE["nc"]
    shared = ["ada_w", "ada_b", "mix_pre_g", "mix_post_g", "ffn_pre_g", "ffn_post_g", "pool_w", "pool_scale",
              "ssm_A_re", "ssm_A_im", "ssm_log_dt", "ssm_B_re", "ssm_B_im", "ssm_C_re", "ssm_C_im", "ssm_D",
              "ssm_glu_a", "ssm_glu_b", "ffn_w_up", "ffn_conv_w", "ffn_conv_b", "ffn_w_down"]
    in_maps = []
    for c in range(NCORES):
        sl = slice(c * NSB, (c + 1) * NSB)
        m = {k: inp[k] for k in shared}
        m["xp"] = inp["x_prompt"][c]
        m["xs"] = inp["x_sample"][sl].reshape(128, D)
        m["cp"] = np.ascontiguousarray(np.broadcast_to(inp["c_prompt"][c][None, :], (128, D)))
        m["cs"] = np.ascontiguousarray(np.repeat(inp["c_sample"][sl], LS, axis=0))
        m["spool"] = inp["state_pool"][0, sl]
        m["sre"] = inp["state_ssm_re"][0, sl]
        m["sim"] = inp["state_ssm_im"][0, sl]
        m["sconv"] = np.ascontiguousarray(inp["state_ffn_conv"][:, sl])
        in_maps.append(m)
    res = run_bass_kernel_spmd(nc, in_maps, core_ids=list(range(NCORES)))
    R = res.results
    y_prompt = np.stack([R[c]["yp"] for c in range(NCORES)], 0)
    y_sample = np.concatenate([R[c]["ys"].reshape(NSB, LS, D) for c in range(NCORES)], 0)
    npp = np.stack([R[c]["npp"] for c in range(NCORES)], 0)[None]
    nps = np.concatenate([R[c]["nps"] for c in range(NCORES)], 0)[None]
    nrp = np.stack([R[c]["nrp"] for c in range(NCORES)], 0)[None]
    nip = np.stack([R[c]["nip"] for c in range(NCORES)], 0)[None]
    nrs = np.concatenate([R[c]["nrs"] for c in range(NCORES)], 0)[None]
    nis = np.concatenate([R[c]["nis"] for c in range(NCORES)], 0)[None]
    ncp = np.stack([R[c]["ncp"] for c in range(NCORES)], 1)
    ncs = np.concatenate([R[c]["ncs"] for c in range(NCORES)], 1)
    return (y_prompt, y_sample, npp, nps, nrp, nip, nrs, nis, ncp, ncs)
```

```python
import contextlib
import numpy as np
import concourse.bass as bass
import concourse.mybir as mybir
from concourse.bass_utils import run_bass_kernel_spmd

F32 = mybir.dt.float32
BF16 = mybir.dt.bfloat16
AF = mybir.ActivationFunctionType
ALU = mybir.AluOpType
AX = mybir.AxisListType

NCORES = 8
D = 1024
NDT = 8
FH = 2816
FUP = 5632
NFT = 44
NGT = 22
SEQ = 2048
LB = 256
NPB = SEQ // LB
NSB = 16
LS = 8
HP = 15
HC = 2
EPS = 1e-6
ENGS = ("pe", "act", "dve", "pool", "sp")


class Buf:
    __slots__ = ("name", "lastw", "readers", "const")

    def __init__(self, name):
        self.name = name
        self.lastw = None
        self.readers = []
        self.const = False


class Op:
    __slots__ = ("eng", "fn", "deps", "signal", "val", "is_dma", "key", "dcount", "wait_all")

    def __init__(self, eng, fn, deps):
        self.eng = eng
        self.fn = fn
        self.deps = deps
        self.signal = False
        self.val = 0
        self.is_dma = False
        self.key = None
        self.dcount = 0
        self.wait_all = False


class Prog:
    def __init__(self):
        self.ops = {e: [] for e in ENGS}
        self.dcnt = {}
        self.nbuf = 0
        self.stores = []
        self.enabled = True

    def buf(self, name=None):
        self.nbuf += 1
        return Buf(name or f"b{self.nbuf}")

    def bufs(self, n, name="b"):
        return [self.buf(f"{name}{i}") for i in range(n)]

    def _mk(self, eng, fn, r, w):
        if not self.enabled:
            return None
        deps = set()
        for b in r:
            if b.lastw is not None:
                deps.add(b.lastw)
        for b in w:
            if b.lastw is not None:
                deps.add(b.lastw)
            deps.update(b.readers)
        o = Op(eng, fn, deps)
        for b in r:
            if not b.const:
                b.readers.append(o)
        for b in w:
            b.lastw = o
            b.readers = []
        self.ops[eng].append(o)
        return o

    def op(self, eng, fn, r=(), w=()):
        return self._mk(eng, fn, r, w)

    def dma(self, eng, fn, r=(), w=(), key=None, wait_all=False, store=False):
        o = self._mk(eng, fn, r, w)
        if o is None:
            return None
        o.is_dma = True
        o.key = key
        o.wait_all = wait_all
        if wait_all:
            o.deps = set(d for d in o.deps if not (d.is_dma and d.key == key))
        self.dcnt[key] = self.dcnt.get(key, 0) + 1
        o.dcount = self.dcnt[key]
        if store:
            self.stores.append(o)
        return o

    def full_barrier(self):
        deps = set()
        for e in ENGS:
            comp = [o for o in self.ops[e] if o.fn is not None and not o.is_dma]
            if comp:
                deps.add(comp[-1])
            last_by_key = {}
            for o in self.ops[e]:
                if o.is_dma:
                    last_by_key[o.key] = o
            deps.update(last_by_key.values())
        for e in ENGS:
            self.ops[e].append(Op(e, None, set(deps)))

    def barrier_wait(self, eng, ops):
        o = Op(eng, None, set(ops))
        self.ops[eng].append(o)
        return o

    def emit(self, nc, stack):
        for e in ENGS:
            for o in self.ops[e]:
                for d in o.deps:
                    d.signal = True
        sems = {}
        for e in ENGS:
            sems[e] = stack.enter_context(nc.semaphore("sem_" + e))
            c = 0
            for o in self.ops[e]:
                if o.is_dma or o.fn is None:
                    continue
                if o.signal:
                    c += 1
                    o.val = c
        dsem = {}
        for k in self.dcnt:
            dsem[k] = stack.enter_context(nc.semaphore("dsem_" + str(k)))

        def ev(d):
            if d.is_dma:
                if d.wait_all:
                    return dsem[d.key], 16 * self.dcnt[d.key], ("d", d.key)
                return dsem[d.key], 16 * d.dcount, ("d", d.key)
            return sems[d.eng], d.val, ("e", d.eng)

        block = stack.enter_context(nc.Block())
        prog = self

        def run(engname, eobj):
            known = {}
            for o in prog.ops[engname]:
                waits = {}
                for d in o.deps:
                    if (not d.is_dma) and d.eng == "pe" and engname == "pe" and not o.is_dma:
                        continue
                    s, v, kk = ev(d)
                    if known.get(kk, 0) >= v:
                        continue
                    if kk not in waits or waits[kk][1] < v:
                        waits[kk] = (s, v)
                for kk, (s, v) in waits.items():
                    known[kk] = v
                wl = list(waits.values())
                attach = None
                if o.fn is not None and wl and engname in ("act", "dve", "pool") and not o.is_dma:
                    attach = wl.pop()
                for s, v in wl:
                    eobj.wait_ge(s, v)
                if o.fn is None:
                    continue
                ins = o.fn(eobj)
                if attach is not None:
                    ins._wait_ge(attach[0], attach[1])
                if o.is_dma:
                    ins.then_inc(dsem[o.key], 16)
                elif o.signal:
                    ins.then_inc(sems[engname], 1)

        @block.tensor
        def _(e):
            run("pe", e)

        @block.scalar
        def _(e):
            run("act", e)

        @block.vector
        def _(e):
            run("dve", e)

        @block.gpsimd
        def _(e):
            run("pool", e)

        @block.sync
        def _(e):
            run("sp", e)


def seg3(ap2, nseg):
    return ap2.rearrange("p (s c) -> p s c", s=nseg)


def seg4(ap3, nseg):
    return ap3.rearrange("p k (s c) -> p k s c", s=nseg)


def build_nc(cfg=None):
    cfg = cfg or {}
    n_pblocks = cfg.get("n_pblocks", NPB)
    do_sample = cfg.get("do_sample", True)
    skip_s5 = cfg.get("skip_s5", False)
    nlayers = cfg.get("nlayers", 2)
    stage = cfg.get("stage", 99)

    nc = bass.Bass("TRN2", target_bir_lowering=False)

    def din(name, shape):
        return nc.dram_tensor(name, list(shape), F32, kind="ExternalInput").ap()

    def dout(name, shape):
        return nc.dram_tensor(name, list(shape), F32, kind="ExternalOutput").ap()

    xp = din("xp", [SEQ, D]); xs = din("xs", [128, D])
    cp = din("cp", [128, D]); cs = din("cs", [128, D])
    spool = din("spool", [NSB, HP, D])
    sre = din("sre", [NSB, 64, 64]); sim = din("sim", [NSB, 64, 64])
    sconv = din("sconv", [2, NSB, HC, FUP])
    ada_w = din("ada_w", [2, D, 6 * D]); ada_b = din("ada_b", [2, 6 * D])
    mix_pre_g = din("mix_pre_g", [2, D]); mix_post_g = din("mix_post_g", [2, D])
    ffn_pre_g = din("ffn_pre_g", [2, D]); ffn_post_g = din("ffn_post_g", [2, D])
    pool_w = din("pool_w", [1, 4, 256, 256]); pool_scale = din("pool_scale", [1, D])
    ssm_A_re = din("ssm_A_re", [1, 64, 64]); ssm_A_im = din("ssm_A_im", [1, 64, 64])
    ssm_log_dt = din("ssm_log_dt", [1, 64])
    ssm_B_re = din("ssm_B_re", [1, 64, 64, 16]); ssm_B_im = din("ssm_B_im", [1, 64, 64, 16])
    ssm_C_re = din("ssm_C_re", [1, 64, 16, 64]); ssm_C_im = din("ssm_C_im", [1, 64, 16, 64])
    ssm_D = din("ssm_D", [1, D])
    ssm_glu_a = din("ssm_glu_a", [1, D, D]); ssm_glu_b = din("ssm_glu_b", [1, D, D])
    ffn_w_up = din("ffn_w_up", [2, D, FUP]); ffn_conv_w = din("ffn_conv_w", [2, 3, FUP])
    ffn_conv_b = din("ffn_conv_b", [2, FUP]); ffn_w_down = din("ffn_w_down", [2, FH, D])

    yp = dout("yp", [SEQ, D]); ys = dout("ys", [128, D])
    npp = dout("npp", [HP, D]); nps = dout("nps", [NSB, HP, D])
    nrp = dout("nrp", [64, 64]); nip = dout("nip", [64, 64])
    nrs = dout("nrs", [NSB, 64, 64]); nis = dout("nis", [NSB, 64, 64])
    ncp = dout("ncp", [2, HC, FUP]); ncs = dout("ncs", [2, NSB, HC, FUP])

    NSS = NGT // 2
    wu_scr = nc.dram_tensor("wu_scr", [2, NSS, 128, NDT * 512], BF16).ap()
    wd_scr = nc.dram_tensor("wd_scr", [2, NSS, 128, 2 * D], BF16).ap()
    wg_scr = nc.dram_tensor("wg_scr", [2, 128, NDT * D], BF16).ap()
    s5ops = nc.dram_tensor("s5ops", [8, 128, 4 * 8 * 128], BF16).ap()
    s5tab_p = nc.dram_tensor("s5tab_p", [3, 128, 64 * 33], F32).ap()
    s5tab_s = nc.dram_tensor("s5tab_s", [3, 128, 64 * 32], F32).ap()

    P = Prog()
    st = contextlib.ExitStack()
    with st:
        def sb(name, shape, dt=F32):
            return st.enter_context(nc.sbuf_tensor(name, list(shape), dt))

        identf = sb("identf", [128, 128]); identb = sb("identb", [128, 128], BF16)
        epsc = sb("epsc", [128, 1])
        fmp = sb("fmp", [128, 512])
        fmp1 = sb("fmp1", [128, 96])
        Wp = sb("Wp", [128, 4, 2, 256], BF16)
        modp = sb("modp", [128, 2, 4, NDT])
        mods = sb("mods", [128, 2, 4, NDT, NSB])
        Gp = sb("Gp", [128, 2, 2, D]); Gs = sb("Gs", [128, 2, 2, D])
        inv15 = sb("inv15", [128, 4, HP])
        stT = sb("stT", [128, 2, NFT, NSB * HC])
        cstage = sb("cstage", [128, 1, NFT, NSB * HC])
        b_ident, b_fmp, b_Wp, b_modp, b_mods, b_Gp, b_Gs, b_inv15, b_stT = P.bufs(9, "c")
        b_cstage = P.buf("cstage")

        FM_ADAB = 0; FM_MPG = 96; FM_FPG = 112; FM_CW = 128; FM_CB = 392

        psall = st.enter_context(nc.psum_tensor("psall", [128, 8, 512], F32))
        banks = [psall[:, i, :] for i in range(8)]
        b_bank = P.bufs(8, "bank")

        def bank_bf(i):
            return banks[i][:].bitcast(BF16)

        pro = contextlib.ExitStack()
        with pro:
            def psb(name, shape, dt=F32):
                return pro.enter_context(nc.sbuf_tensor(name, list(shape), dt))

            P.op("pool", lambda e: e.memset(identf[:], 0.0), w=[b_ident])
            P.op("pool", lambda e: e.affine_select(out=identf[:], in_=identf[:], pattern=[[-1, 128]],
                                                   compare_op=ALU.not_equal, fill=1.0, base=0,
                                                   channel_multiplier=1), r=[b_ident], w=[b_ident])
            P.op("dve", lambda e: e.tensor_copy(out=identb[:], in_=identf[:]), r=[b_ident], w=[b_ident])
            P.op("pool", lambda e: e.memset(epsc[:], EPS), w=[b_ident])

            P.enabled = stage >= 1
            b_wscr = P.buf("wscr")
            NCV = 3
            cvf = psb("cvf", [128, NCV, NDT * 512]); cvb = psb("cvb", [128, NCV, NDT * 512], BF16)
            b_cvf = P.bufs(NCV, "cvf"); b_cvb = P.bufs(NCV, "cvb")
            cvctr = [0]
            cast_rr = ["dve", "act"]

            def cast_op(eng, dst, src, r, w):
                if eng == "act":
                    P.op("act", lambda e: e.activation(out=dst, in_=src, func=AF.Copy), r=r, w=w)
                else:
                    P.op(eng, lambda e: e.tensor_copy(out=dst, in_=src), r=r, w=w)

            def convert(loads, nelem, dst_ap):
                k = cvctr[0]
                cvctr[0] += 1
                sl = k % NCV
                for li, (vf, dap) in enumerate(loads):
                    qn = "sp"
                    P.dma(qn, lambda e, vf=vf, dap=dap, sl=sl: e.dma_start(out=vf(cvf[:, sl, :]), in_=dap),
                          w=[b_cvf[sl]], key=f"cvf{sl}")
                e1 = cast_rr[(2 * k) % len(cast_rr)]
                e2 = cast_rr[(2 * k + 1) % len(cast_rr)]
                h = nelem // 2
                cast_op(e1, cvb[:, sl, 0:h], cvf[:, sl, 0:h], [b_cvf[sl]], [b_cvb[sl]])
                cast_op(e2, cvb[:, sl, h:nelem], cvf[:, sl, h:nelem], [b_cvf[sl]], [b_cvb[sl]])
                P.dma("pool", lambda e, sl=sl: e.dma_start(out=dst_ap, in_=cvb[:, sl, 0:nelem]),
                      r=[b_cvb[sl]], w=[b_wscr], key=f"cvb{sl}")

            cv_jobs = []

            def convert_ffn(l):
                for ss in range(NSS):
                    loads = []
                    for h in range(2):
                        dap = ffn_w_up[l, :, h * FH + ss * 256: h * FH + (ss + 1) * 256].rearrange("(kt p) n -> p kt n", p=128)
                        loads.append((lambda v, h=h: v.rearrange("p (kt h c) -> p kt h c", kt=NDT, h=2)[:, :, h, :], dap))
                    cv_jobs.append((loads, NDT * 512, wu_scr[l, ss]))
                    dap = ffn_w_down[l, ss * 256:(ss + 1) * 256, :].rearrange("(j p) d -> p j d", p=128)
                    cv_jobs.append(([(lambda v: v[:, 0:2 * D].rearrange("p (j d) -> p j d", j=2), dap)], 2 * D, wd_scr[l, ss]))

            def convert_glu():
                for ab, wsrc in enumerate((ssm_glu_a, ssm_glu_b)):
                    for k0 in range(0, NDT, 4):
                        dap = wsrc[0, k0 * 128:(k0 + 4) * 128, :].rearrange("(kt p) n -> p kt n", p=128)
                        cv_jobs.append(([(lambda v: v.rearrange("p (kt n) -> p kt n", kt=4), dap)], 4 * D,
                                        wg_scr[ab, :, k0 * D:(k0 + 4) * D]))

            convert_ffn(0)
            if not skip_s5:
                convert_glu()
            if nlayers > 1:
                convert_ffn(1)

            def pump_convert(k):
                for _ in range(k):
                    if cv_jobs:
                        convert(*cv_jobs.pop(0))

            P.enabled = stage >= 1
            stg = psb("stg", [128, 4, 128])
            b_stg = P.buf("stg")
            P.op("dve", lambda e: e.memset(stg[:], 0.0), w=[b_stg])
            srcs = [
                (0, 0, 96, ada_b.rearrange("l (t p) -> (l t) p", p=128)),
                (0, 96, 16, mix_pre_g.rearrange("l (t p) -> (l t) p", p=128)),
                (0, 112, 16, ffn_pre_g.rearrange("l (t p) -> (l t) p", p=128)),
            ]
            cwv = ffn_conv_w.rearrange("l k (t p) -> (l k t) p", p=128)
            srcs += [(1, 0, 128, cwv[0:128]), (2, 0, 128, cwv[128:256]), (3, 0, 8, cwv[256:264]),
                     (3, 8, 88, ffn_conv_b.rearrange("l (t p) -> (l t) p", p=128))]
            sub = cfg.get("sub", 99)
            for (ti, r0, n, sap) in srcs[:sub]:
                P.dma("sp", lambda e, ti=ti, r0=r0, n=n, sap=sap: e.dma_start(out=stg[r0:r0 + n, ti, :], in_=sap),
                      r=[b_stg], w=[b_stg], key="cst", wait_all=True)
            for ti in range(4 if sub >= 20 else 0):
                P.op("pe", lambda e, ti=ti: e.transpose(out=banks[7][:, ti * 128:(ti + 1) * 128], in_=stg[:, ti, :],
                                                        identity=identf[:]), r=[b_stg, b_ident], w=[b_bank[7]])
            P.op("act", lambda e: e.activation(out=fmp[:], in_=banks[7][:], func=AF.Copy), r=[b_bank[7]], w=[b_fmp])
            P.op("dve", lambda e: e.tensor_scalar(out=fmp1[:], in0=fmp[:, 0:96], scalar1=1.0, scalar2=None, op0=ALU.add),
                 r=[b_fmp], w=[b_fmp])

            P.enabled = stage >= 2
            bc = psb("bc", [128, 9, D])
            b_bc = P.buf("bc")
            bsrc = [mix_post_g[0:1, :], ffn_post_g[0:1, :], mix_post_g[1:2, :], ffn_post_g[1:2, :],
                    ada_b[0:1, 2 * D:3 * D], ada_b[0:1, 5 * D:6 * D], ada_b[1:2, 2 * D:3 * D], ada_b[1:2, 5 * D:6 * D],
                    pool_scale[0:1, :]]
            for i, s in enumerate(bsrc):
                P.dma("sp", lambda e, i=i, s=s: e.dma_start(out=bc[:, i, :], in_=s.partition_broadcast(128)),
                      w=[b_bc], key="cst", wait_all=True)
            for i in range(4):
                P.op("dve", lambda e, i=i: e.tensor_tensor(out=bc[:, 4 + i, :], in0=bc[:, 4 + i, :], in1=bc[:, i, :], op=ALU.mult),
                     r=[b_bc], w=[b_bc])

            P.enabled = stage >= 3
            wpf = psb("wpf", [128, 4, 2, 256])
            b_wpf = P.buf("wpf")
            for g in range(4):
                P.dma("sp", lambda e, g=g: e.dma_start(out=wpf[:, g], in_=pool_w[0, g].rearrange("(kt p) n -> p kt n", p=128)),
                      w=[b_wpf], key="cst", wait_all=True)
            for g in range(4):
                for kt in range(2):
                    P.op("dve", lambda e, g=g, kt=kt: e.tensor_tensor(out=Wp[:, g, kt, :], in0=wpf[:, g, kt, :],
                                                                     in1=bc[:, 8, g * 256:(g + 1) * 256], op=ALU.mult),
                         r=[b_wpf, b_bc], w=[b_Wp])

            P.enabled = stage >= 3
            for gi, w_ in enumerate((2, 4, 8, 16)):
                P.op("pool", lambda e, gi=gi, w_=w_: e.memset(inv15[:, gi, :], 1.0 / w_), w=[b_inv15])
                for t in range(min(w_ - 1, HP)):
                    P.op("pool", lambda e, gi=gi, t=t: e.memset(inv15[:, gi, t:t + 1], 1.0 / (t + 1)), w=[b_inv15])

            P.enabled = stage >= 4
            ctile = psb("ctile", [128, 2, D]); csil = psb("csil", [128, 2, D], BF16)
            cTp = psb("cTp", [128, NDT, 128], BF16); cTs = psb("cTs", [128, NDT, 128], BF16)
            cT17 = psb("cT17", [128, NDT, 17], BF16)
            b_ct, b_cT = P.buf("ct"), P.buf("cT")
            P.dma("sp", lambda e: e.dma_start(out=ctile[:, 0, :], in_=cp[:, :]), w=[b_ct], key="cst", wait_all=True)
            P.dma("sp", lambda e: e.dma_start(out=ctile[:, 1, :], in_=cs[:, :]), w=[b_ct], key="cst", wait_all=True)
            P.op("act", lambda e: e.activation(out=csil[:], in_=ctile[:], func=AF.Silu), r=[b_ct], w=[b_ct])
            for which, dstT in ((0, cTp), (1, cTs)):
                for dt in range(NDT):
                    P.op("pe", lambda e, which=which, dt=dt: e.transpose(
                        out=bank_bf(6)[:, dt * 128:(dt + 1) * 128], in_=csil[:, which, dt * 128:(dt + 1) * 128],
                        identity=identb[:]), r=[b_ct, b_ident], w=[b_bank[6]])
                P.op("act", lambda e, dstT=dstT: e.activation(out=dstT[:], in_=bank_bf(6).rearrange("p (k c) -> p k c", k=NDT),
                                                              func=AF.Copy), r=[b_bank[6]], w=[b_cT])
            P.op("dve", lambda e: e.tensor_copy(out=cT17[:, :, 0:1], in_=cTp[:, :, 0:1]), r=[b_cT], w=[b_cT])
            P.op("dve", lambda e: e.tensor_copy(out=cT17[:, :, 1:17], in_=cTs[:, :, 0:128:8]), r=[b_cT], w=[b_cT])

            P.enabled = stage >= 5
            tmpf = psb("tmpf", [128, 2, 17]); b_tmpf = P.bufs(2, "tmpf")
            for l in range(2):
                for v in range(6):
                    for hf in range(2):
                        k = cvctr[0]
                        cvctr[0] += 1
                        sl = k % NCV
                        c0 = v * D + hf * 512
                        P.dma("sp", lambda e, l=l, c0=c0, sl=sl: e.dma_start(
                            out=cvf[:, sl, :].rearrange("p (kt n) -> p kt n", kt=NDT),
                            in_=ada_w[l, :, c0:c0 + 512].rearrange("(kt p) n -> p kt n", p=128)),
                            w=[b_cvf[sl]], key=f"cvf{sl}")
                        for part, eng in enumerate(("dve", "act", "dve", "act")):
                            cast_op(eng, cvb[:, sl, part * 1024:(part + 1) * 1024], cvf[:, sl, part * 1024:(part + 1) * 1024],
                                    [b_cvf[sl]], [b_cvb[sl]])
                        wv_ = cvb[:, sl, :].rearrange("p (kt n) -> p kt n", kt=NDT)
                        if v in (0, 1, 3, 4):
                            mslot = {0: 1, 1: 0, 3: 3, 4: 2}[v]
                            is_scale = v in (1, 4)
                            gbase = FM_MPG if v == 1 else FM_FPG
                            for d4 in range(4):
                                dt = hf * 4 + d4
                                bk = 4 + (dt % 2)
                                for kt in range(NDT):
                                    P.op("pe", lambda e, kt=kt, d4=d4, bk=bk, wv_=wv_: e.matmul(
                                        banks[bk][:, 0:17], wv_[:, kt, d4 * 128:(d4 + 1) * 128], cT17[:, kt, :],
                                        start=(kt == 0), stop=(kt == NDT - 1)), r=[b_cvb[sl], b_cT], w=[b_bank[bk]])
                                bcol = FM_ADAB + l * 48 + v * 8 + dt
                                ts = dt % 2
                                if is_scale:
                                    P.op("act", lambda e, bk=bk, bcol=bcol, ts=ts: e.activation(
                                        out=tmpf[:, ts, :], in_=banks[bk][:, 0:17], func=AF.Identity,
                                        bias=fmp1[:, bcol:bcol + 1]), r=[b_bank[bk], b_fmp], w=[b_tmpf[ts]])
                                    gcol = gbase + l * 8 + dt
                                    P.op("dve", lambda e, l=l, mslot=mslot, dt=dt, ts=ts, gcol=gcol: e.tensor_scalar(
                                        out=modp[:, l, mslot, dt:dt + 1], in0=tmpf[:, ts, 0:1], scalar1=fmp[:, gcol:gcol + 1],
                                        scalar2=None, op0=ALU.mult), r=[b_tmpf[ts], b_fmp], w=[b_modp])
                                    P.op("dve", lambda e, l=l, mslot=mslot, dt=dt, ts=ts, gcol=gcol: e.tensor_scalar(
                                        out=mods[:, l, mslot, dt, :], in0=tmpf[:, ts, 1:17],
                                        scalar1=fmp[:, gcol:gcol + 1], scalar2=None, op0=ALU.mult),
                                        r=[b_tmpf[ts], b_fmp], w=[b_mods])
                                else:
                                    P.op("act", lambda e, l=l, mslot=mslot, dt=dt, bk=bk, bcol=bcol: e.activation(
                                        out=modp[:, l, mslot, dt:dt + 1], in_=banks[bk][:, 0:1], func=AF.Identity,
                                        bias=fmp[:, bcol:bcol + 1]), r=[b_bank[bk], b_fmp], w=[b_modp])
                                    P.op("act", lambda e, l=l, mslot=mslot, dt=dt, bk=bk, bcol=bcol: e.activation(
                                        out=mods[:, l, mslot, dt, :], in_=banks[bk][:, 1:17], func=AF.Identity,
                                        bias=fmp[:, bcol:bcol + 1]), r=[b_bank[bk], b_fmp], w=[b_mods])
                        else:
                            which = 0 if v == 2 else 1
                            gi = l * 2 + which
                            for grp, (cT_, Gt, bG) in enumerate(((cTp, Gp, b_Gp), (cTs, Gs, b_Gs))):
                                bk = 6 + grp
                                for kt in range(NDT):
                                    P.op("pe", lambda e, kt=kt, bk=bk, cT_=cT_, wv_=wv_: e.matmul(
                                        banks[bk][:], cT_[:, kt, :], wv_[:, kt, :],
                                        start=(kt == 0), stop=(kt == NDT - 1)), r=[b_cvb[sl], b_cT], w=[b_bank[bk]])
                                P.op("dve", lambda e, l=l, which=which, hf=hf, bk=bk, Gt=Gt, gi=gi: e.tensor_tensor(
                                    out=Gt[:, l, which, hf * 512:(hf + 1) * 512], in0=banks[bk][:],
                                    in1=bc[:, gi, hf * 512:(hf + 1) * 512], op=ALU.mult),
                                    r=[b_bank[bk], b_bc], w=[bG])
                                P.op("dve", lambda e, l=l, which=which, hf=hf, Gt=Gt, gi=gi: e.tensor_tensor(
                                    out=Gt[:, l, which, hf * 512:(hf + 1) * 512],
                                    in0=Gt[:, l, which, hf * 512:(hf + 1) * 512],
                                    in1=bc[:, 4 + gi, hf * 512:(hf + 1) * 512], op=ALU.add),
                                    r=[bG, b_bc], w=[bG])
                        if stage >= 7:
                            pump_convert(2)

            P.enabled = stage >= 6
            if do_sample:
                cst_tok = cvf[0:32].rearrange("p a n -> p (a n)")[:, 0:FUP]
                for l in range(2):
                    P.dma("sp", lambda e, l=l: e.dma_start(out=cst_tok, in_=sconv[l].rearrange("b k f -> (b k) f")),
                          w=[b_cvf[0], b_cvf[1]], key="cst_tok")
                    for f0 in range(0, NFT, 16):
                        nf = min(16, NFT - f0)
                        for j in range(nf):
                            ft = f0 + j
                            P.op("pe", lambda e, ft=ft, j=j: e.transpose(
                                out=banks[6][:, j * 32:(j + 1) * 32], in_=cst_tok[:, ft * 128:(ft + 1) * 128],
                                identity=identf[0:32, 0:32]), r=[b_cvf[0], b_cvf[1], b_ident], w=[b_bank[6]])
                        P.op("act", lambda e, l=l, f0=f0, nf=nf: e.activation(
                            out=stT[:, l, f0:f0 + nf, :], in_=banks[6][:, 0:nf * 32].rearrange("p (f c) -> p f c", f=nf),
                            func=AF.Copy), r=[b_bank[6]], w=[b_stT])

            P.enabled = stage >= 7
            pump_convert(len(cv_jobs))

        P.enabled = stage >= 8
        P.full_barrier()


        if not skip_s5 and nlayers > 1:
            pro2 = contextlib.ExitStack()
            with pro2:
                def p2(name, shape, dt=F32):
                    return pro2.enter_context(nc.sbuf_tensor(name, list(shape), dt))
                TWO_PI = float(2 * np.pi)
                C1 = 6.28125
                C2 = 0.0019353071795864769
                MAGIC = 12582912.0
                bq = {n: P.buf("q_" + n) for n in "Ald AT dtv marg ang kang nq trig magk L f W Bst Bsw Cld CT CA CB Dld Dcol alpha phi tab colv Eb stage perm".split()}
                Ald = p2("Ald", [64, 2, 128]); AT = p2("AT", [128, 2, 64]); dtv = p2("dtv", [128, 64])
                marg = p2("marg", [128, 64]); ang = p2("ang", [128, 64])
                for i, src in enumerate((ssm_A_re, ssm_A_im)):
                    for hh in range(2):
                        P.dma("sp", lambda e, i=i, hh=hh, src=src: e.dma_start(out=Ald[:, i, hh * 64:(hh + 1) * 64], in_=src[0]),
                              w=[bq["Ald"]], key="q_c", wait_all=True)
                P.dma("sp", lambda e: e.dma_start(out=dtv[:], in_=ssm_log_dt[0:1, :].partition_broadcast(128)),
                      w=[bq["dtv"]], key="q_c", wait_all=True)
                for i in range(2):
                    P.op("pe", lambda e, i=i: e.transpose(out=banks[7][:, i * 64:(i + 1) * 64], in_=Ald[:, i, :],
                                                          identity=identf[0:64, 0:64]), r=[bq["Ald"], b_ident], w=[b_bank[7]])
                P.op("act", lambda e: e.activation(out=AT[:].rearrange("p a g -> p (a g)"), in_=banks[7][:, 0:128], func=AF.Copy),
                     r=[b_bank[7]], w=[bq["AT"]])
                P.op("act", lambda e: e.activation(out=dtv[:], in_=dtv[:], func=AF.Exp), r=[bq["dtv"]], w=[bq["dtv"]])
                P.op("dve", lambda e: e.tensor_tensor(out=marg[:], in0=AT[:, 0, :], in1=dtv[:], op=ALU.mult),
                     r=[bq["AT"], bq["dtv"]], w=[bq["marg"]])
                P.op("dve", lambda e: e.tensor_tensor(out=ang[:], in0=AT[:, 1, :], in1=dtv[:], op=ALU.mult),
                     r=[bq["AT"], bq["dtv"]], w=[bq["ang"]])

                def reduce_2pi(x, nq_, bx, bn):
                    P.op("dve", lambda e: e.tensor_scalar(out=nq_, in0=x, scalar1=1.0 / TWO_PI, scalar2=MAGIC, op0=ALU.mult, op1=ALU.add),
                         r=[bx], w=[bn])
                    P.op("dve", lambda e: e.tensor_scalar(out=nq_, in0=nq_, scalar1=MAGIC, scalar2=None, op0=ALU.subtract),
                         r=[bn], w=[bn])
                    P.op("dve", lambda e: e.scalar_tensor_tensor(out=x, in0=nq_, scalar=-C1, in1=x, op0=ALU.mult, op1=ALU.add),
                         r=[bn, bx], w=[bx])
                    P.op("dve", lambda e: e.scalar_tensor_tensor(out=x, in0=nq_, scalar=-C2, in1=x, op0=ALU.mult, op1=ALU.add),
                         r=[bn, bx], w=[bx])

                kang = p2("kang", [128, 2, 9, 64]); nq = p2("nq", [128, 2 * 33 * 64]); trig = p2("trig", [128, 2, 9, 64])
                magk = p2("magk", [128, 9, 64]); Lt = p2("Lt", [128, 2, 9, 64])
                for k in range(9):
                    P.op("dve", lambda e, k=k: e.tensor_scalar(out=kang[:, 0, k, :], in0=ang[:], scalar1=float(k), scalar2=None, op0=ALU.mult),
                         r=[bq["ang"]], w=[bq["kang"]])
                    P.op("dve", lambda e, k=k: e.tensor_scalar(out=kang[:, 1, k, :], in0=ang[:], scalar1=float(k), scalar2=float(np.pi / 2),
                                                               op0=ALU.mult, op1=ALU.add), r=[bq["ang"]], w=[bq["kang"]])
                    P.op("act", lambda e, k=k: e.activation(out=magk[:, k, :], in_=marg[:], func=AF.Exp, scale=float(k)),
                         r=[bq["marg"]], w=[bq["magk"]])
                kflat = kang[:].rearrange("p a k g -> p (a k g)")
                reduce_2pi(kflat, nq[:, 0:2 * 9 * 64], bq["kang"], bq["nq"])
                P.op("act", lambda e: e.activation(out=trig[:].rearrange("p a k g -> p (a k g)"), in_=kflat, func=AF.Sin),
                     r=[bq["kang"]], w=[bq["trig"]])
                P.op("dve", lambda e: e.tensor_tensor(out=Lt[:, 0], in0=magk[:], in1=trig[:, 1], op=ALU.mult),
                     r=[bq["magk"], bq["trig"]], w=[bq["L"]])
                P.op("dve", lambda e: e.tensor_tensor(out=Lt[:, 1], in0=magk[:], in1=trig[:, 0], op=ALU.mult),
                     r=[bq["magk"], bq["trig"]], w=[bq["L"]])
                ft = p2("ft", [128, 8, 64])
                P.op("dve", lambda e: e.tensor_scalar(out=ft[:, 0, :], in0=Lt[:, 0, 1, :], scalar1=-1.0, scalar2=None, op0=ALU.add),
                     r=[bq["L"]], w=[bq["f"]])
                P.op("dve", lambda e: e.tensor_tensor(out=ft[:, 1, :], in0=AT[:, 0, :], in1=AT[:, 0, :], op=ALU.mult), r=[bq["AT"], bq["f"]], w=[bq["f"]])
                P.op("dve", lambda e: e.tensor_tensor(out=ft[:, 2, :], in0=AT[:, 1, :], in1=AT[:, 1, :], op=ALU.mult), r=[bq["AT"], bq["f"]], w=[bq["f"]])
                P.op("dve", lambda e: e.tensor_tensor(out=ft[:, 1, :], in0=ft[:, 1, :], in1=ft[:, 2, :], op=ALU.add), r=[bq["f"]], w=[bq["f"]])
                P.op("dve", lambda e: e.reciprocal(out=ft[:, 1, :], in_=ft[:, 1, :]), r=[bq["f"]], w=[bq["f"]])
                P.op("dve", lambda e: e.tensor_tensor(out=ft[:, 2, :], in0=ft[:, 0, :], in1=AT[:, 0, :], op=ALU.mult), r=[bq["f"], bq["AT"]], w=[bq["f"]])
                P.op("dve", lambda e: e.tensor_tensor(out=ft[:, 3, :], in0=Lt[:, 1, 1, :], in1=AT[:, 1, :], op=ALU.mult), r=[bq["L"], bq["AT"], bq["f"]], w=[bq["f"]])
                P.op("dve", lambda e: e.tensor_tensor(out=ft[:, 2, :], in0=ft[:, 2, :], in1=ft[:, 3, :], op=ALU.add), r=[bq["f"]], w=[bq["f"]])
                P.op("dve", lambda e: e.tensor_tensor(out=ft[:, 4, :], in0=ft[:, 2, :], in1=ft[:, 1, :], op=ALU.mult), r=[bq["f"]], w=[bq["f"]])
                P.op("dve", lambda e: e.tensor_tensor(out=ft[:, 2, :], in0=Lt[:, 1, 1, :], in1=AT[:, 0, :], op=ALU.mult), r=[bq["L"], bq["AT"], bq["f"]], w=[bq["f"]])
                P.op("dve", lambda e: e.tensor_tensor(out=ft[:, 3, :], in0=ft[:, 0, :], in1=AT[:, 1, :], op=ALU.mult), r=[bq["f"], bq["AT"]], w=[bq["f"]])
                P.op("dve", lambda e: e.tensor_tensor(out=ft[:, 2, :], in0=ft[:, 2, :], in1=ft[:, 3, :], op=ALU.subtract), r=[bq["f"]], w=[bq["f"]])
                P.op("dve", lambda e: e.tensor_tensor(out=ft[:, 5, :], in0=ft[:, 2, :], in1=ft[:, 1, :], op=ALU.mult), r=[bq["f"]], w=[bq["f"]])
                Wt = p2("Wt", [128, 2, 8, 64]); Wtmp = nq[:, 3584:4096].rearrange("p (k g) -> p k g", k=8)
                fre_b = ft[:, 4, :].unsqueeze(1).to_broadcast([128, 8, 64])
                fim_b = ft[:, 5, :].unsqueeze(1).to_broadcast([128, 8, 64])
                P.op("dve", lambda e: e.tensor_tensor(out=Wt[:, 0], in0=Lt[:, 0, 0:8, :], in1=fre_b, op=ALU.mult), r=[bq["L"], bq["f"]], w=[bq["W"]])
                P.op("dve", lambda e: e.tensor_tensor(out=Wtmp[:], in0=Lt[:, 1, 0:8, :], in1=fim_b, op=ALU.mult), r=[bq["L"], bq["f"]], w=[bq["nq"]])
                P.op("dve", lambda e: e.tensor_tensor(out=Wt[:, 0], in0=Wt[:, 0], in1=Wtmp[:], op=ALU.subtract), r=[bq["W"], bq["nq"]], w=[bq["W"]])
                P.op("dve", lambda e: e.tensor_tensor(out=Wt[:, 1], in0=Lt[:, 0, 0:8, :], in1=fim_b, op=ALU.mult), r=[bq["L"], bq["f"]], w=[bq["W"]])
                P.op("dve", lambda e: e.tensor_tensor(out=Wtmp[:], in0=Lt[:, 1, 0:8, :], in1=fre_b, op=ALU.mult), r=[bq["L"], bq["f"], bq["W"]], w=[bq["nq"]])
                P.op("dve", lambda e: e.tensor_tensor(out=Wt[:, 1], in0=Wt[:, 1], in1=Wtmp[:], op=ALU.add), r=[bq["W"], bq["nq"]], w=[bq["W"]])

                Bst = p2("Bst", [128, 64, 16]); Bsw = p2("Bsw", [128, 64, 16])
                for (dst, bname, top, bot) in ((Bst, "Bst", ssm_B_re, ssm_B_im), (Bsw, "Bsw", ssm_B_im, ssm_B_re)):
                    for hh, src in enumerate((top, bot)):
                        for g0 in range(0, 64, 16):
                            P.dma("sp", lambda e, dst=dst, hh=hh, src=src, g0=g0: e.dma_start(
                                out=dst[hh * 64:(hh + 1) * 64, g0:g0 + 16, :], in_=src[0, g0:g0 + 16].rearrange("g p c -> p g c")),
                                w=[bq[bname]], key="q_c", wait_all=True)
                P.op("dve", lambda e: e.tensor_scalar(out=Bsw[0:64], in0=Bsw[0:64], scalar1=-1.0, scalar2=None, op0=ALU.mult),
                     r=[bq["Bsw"]], w=[bq["Bsw"]])
                Cld = p2("Cld", [128, 2, 8, 128]); CA = p2("CA", [128, 64, 16]); CB = p2("CB", [128, 64, 16])
                for v, (left, right) in enumerate(((ssm_C_re, ssm_C_im), (ssm_C_im, ssm_C_re))):
                    for hh, src in enumerate((left, right)):
                        P.dma("sp", lambda e, v=v, hh=hh, src=src: e.dma_start(
                            out=Cld[:, v, :, hh * 64:(hh + 1) * 64], in_=src[0].rearrange("(t g) c p -> (g c) t p", g=8)),
                            w=[bq["Cld"]], key="q_c", wait_all=True)
                for v, (dst, bname) in enumerate(((CA, "CA"), (CB, "CB"))):
                    for t4 in range(2):
                        for tt in range(4):
                            t = t4 * 4 + tt
                            P.op("pe", lambda e, v=v, t=t, tt=tt: e.transpose(out=banks[6][:, tt * 128:(tt + 1) * 128], in_=Cld[:, v, t, :],
                                                                              identity=identf[:]), r=[bq["Cld"], b_ident], w=[b_bank[6]])
                        P.op("act", lambda e, dst=dst, t4=t4: e.activation(
                            out=dst[:, t4 * 32:(t4 + 1) * 32, :].rearrange("p g c -> p (g c)"), in_=banks[6][:], func=AF.Copy),
                            r=[b_bank[6]], w=[bq[bname]])
                P.op("dve", lambda e: e.tensor_scalar(out=CA[64:128], in0=CA[64:128], scalar1=-1.0, scalar2=None, op0=ALU.mult), r=[bq["CA"]], w=[bq["CA"]])
                P.op("dve", lambda e: e.tensor_scalar(out=CB[:], in0=CB[:], scalar1=-1.0, scalar2=None, op0=ALU.mult), r=[bq["CB"]], w=[bq["CB"]])
                Dld = p2("Dld", [64, 8, 16]); Dcol = p2("Dcol", [128, 64])
                P.dma("sp", lambda e: e.dma_start(out=Dld[:, 0, :], in_=ssm_D[0].rearrange("(g c) -> g c", c=16)), w=[bq["Dld"]], key="q_c", wait_all=True)
                for r_ in range(1, 8):
                    P.op("dve", lambda e, r_=r_: e.tensor_copy(out=Dld[:, r_, :], in_=Dld[:, 0, :]), r=[bq["Dld"]], w=[bq["Dld"]])
                P.op("pe", lambda e: e.transpose(out=banks[7][:, 0:64], in_=Dld[:].rearrange("g r c -> g (r c)"), identity=identf[0:64, 0:64]),
                     r=[bq["Dld"], b_ident], w=[b_bank[7]])
                P.op("act", lambda e: e.activation(out=Dcol[:], in_=banks[7][:, 0:64], func=AF.Copy), r=[b_bank[7]], w=[bq["Dcol"]])

                alpha = p2("alpha", [128, 64]); colv = p2("colv", [128, 33])
                phi = p2("phi", [128, 2, 64, 33]); tab = p2("tab", [128, 3, 64, 33])
                P.op("dve", lambda e: e.tensor_scalar(out=alpha[:], in0=ang[:], scalar1=8.0, scalar2=None, op0=ALU.mult), r=[bq["ang"]], w=[bq["alpha"]])
                reduce_2pi(alpha[:], nq[:, 0:64], bq["alpha"], bq["nq"])
                for c_ in range(33):
                    P.op("pool", lambda e, c_=c_: e.memset(colv[:, c_:c_ + 1], float(c_)), w=[bq["colv"]])
                a_b = alpha[:].unsqueeze(2).to_broadcast([128, 64, 33])
                c_b = colv[:].unsqueeze(1).to_broadcast([128, 64, 33])
                P.op("dve", lambda e: e.tensor_tensor(out=phi[:, 0], in0=a_b, in1=c_b, op=ALU.mult), r=[bq["alpha"], bq["colv"]], w=[bq["phi"]])
                P.op("dve", lambda e: e.tensor_scalar(out=phi[:, 1], in0=phi[:, 0], scalar1=float(np.pi / 2), scalar2=None, op0=ALU.add),
                     r=[bq["phi"]], w=[bq["phi"]])
                pflat = phi[:].rearrange("p a g c -> p (a g c)")
                reduce_2pi(pflat, nq[:, 0:2 * 64 * 33], bq["phi"], bq["nq"])
                P.op("act", lambda e: e.activation(out=tab[:, 1], in_=phi[:, 0], func=AF.Sin), r=[bq["phi"]], w=[bq["tab"]])
                P.op("act", lambda e: e.activation(out=tab[:, 0], in_=phi[:, 1], func=AF.Sin), r=[bq["phi"]], w=[bq["tab"]])
                P.op("pool", lambda e: e.memset(tab[:, 2, :, 0:1], 0.0), w=[bq["tab"]])
                P.op("dve", lambda e: e.tensor_copy(out=tab[:, 2, :, 1:33], in_=magk[:, 8, :].unsqueeze(2).to_broadcast([128, 64, 32])),
                     r=[bq["magk"], bq["tab"]], w=[bq["tab"]])
                tabs_sa = p2("tabs_sa", [128, 2, 16, NSB, 2]); b_tabs_sa = P.bufs(2, "tabs_sa")
                qi = 0
                for kk in range(3):
                    P.dma("sp", lambda e, kk=kk: e.dma_start(out=s5tab_p[kk], in_=tab[:, kk].rearrange("p g c -> p (g c)")),
                          r=[bq["tab"]], w=[b_wscr], key="q_st")
                    for g0 in range(0, 64, 16):
                        sl_ = qi % 2
                        qi += 1
                        P.op("pool", lambda e, kk=kk, g0=g0, sl_=sl_: e.tensor_copy(
                            out=tabs_sa[:, sl_], in_=tab[:, kk, g0:g0 + 16, 0:2].unsqueeze(2).to_broadcast([128, 16, NSB, 2])),
                            r=[bq["tab"]], w=[b_tabs_sa[sl_]])
                        P.dma("sp", lambda e, kk=kk, g0=g0, sl_=sl_: e.dma_start(
                            out=s5tab_s[kk, :, g0 * 32:(g0 + 16) * 32], in_=tabs_sa[:, sl_].rearrange("p g b c -> p (g b c)")),
                            r=[b_tabs_sa[sl_]], w=[b_wscr], key="q_ts%d" % sl_)

                Eb = p2("Eb", [128, 2, 8, 15, 16]); Etmp = p2("Etmp", [128, 1, 8, 8, 16]); opstg = p2("opstg", [128, 2, 4, 8, 128], BF16)
                Ctmp = nq[:, 0:1024].rearrange("p (a g r c) -> p a g r c", a=1, g=8, r=8); Cp = nq[:, 1024:2048].rearrange("p (a g r c) -> p a g r c", a=1, g=8, r=8)
                Wrev = nq[:, 2048:3072].rearrange("p (a k g) -> p a k g", a=2, k=8)
                b_Eb, b_Etmp, b_Cp, b_Ctmp = P.bufs(2, "Eb"), P.bufs(2, "Etmp"), P.bufs(2, "Cp"), P.bufs(2, "Ctmp")
                b_stage = P.bufs(2, "stage"); b_Wrev = P.buf("Wrev")
                P.op("pool", lambda e: e.memset(Eb[:], 0.0), w=b_Eb)
                phif = phi[:].rearrange("p a g c -> p (a g c)")
                Ebb = phif[:, 0:1920].bitcast(BF16).rearrange("p (a g k c) -> p a g k c", a=2, g=8, k=15)
                CAb = phif[:, 1920:1920 + 512].bitcast(BF16).rearrange("p (g c) -> p g c", c=16)
                b_Ebb = P.bufs(2, "Ebb")
                P.op("act", lambda e: e.activation(out=CAb, in_=CA[:], func=AF.Copy), r=[bq["CA"], bq["phi"], bq["tab"]], w=[bq["CA"], bq["phi"]])
                for k_ in range(8):
                    P.op("pool", lambda e, k_=k_: e.tensor_copy(out=Wrev[:, :, k_, :], in_=Wt[:, :, 7 - k_, :]), r=[bq["W"], bq["nq"]], w=[b_Wrev])
                def sB1(bt):
                        gs = slice(bt * 8, (bt + 1) * 8)
                        sl = bt % 2
                        bst_b = Bst[:, gs, :].unsqueeze(2).to_broadcast([128, 8, 8, 16])
                        bsw_b = Bsw[:, gs, :].unsqueeze(2).to_broadcast([128, 8, 8, 16])
                        wr_b = Wrev[:, 0, :, gs].rearrange("p k g -> p g k").unsqueeze(3).to_broadcast([128, 8, 8, 16])
                        wi_b = Wrev[:, 1, :, gs].rearrange("p k g -> p g k").unsqueeze(3).to_broadcast([128, 8, 8, 16])
                        P.op("dve", lambda e, sl=sl, bst_b=bst_b, wr_b=wr_b: e.tensor_tensor(out=Eb[:, sl, :, 0:8, :], in0=bst_b, in1=wr_b, op=ALU.mult),
                             r=[bq["Bst"], b_Wrev], w=[b_Eb[sl]])
                        P.op("dve", lambda e, sl=sl, bsw_b=bsw_b, wi_b=wi_b: e.tensor_tensor(out=Etmp[:, 0], in0=bsw_b, in1=wi_b, op=ALU.mult),
                             r=[bq["Bsw"], b_Wrev], w=[b_Etmp[0]])
                        P.op("dve", lambda e, sl=sl: e.tensor_tensor(out=Eb[:, sl, :, 0:8, :], in0=Eb[:, sl, :, 0:8, :], in1=Etmp[:, 0], op=ALU.add),
                             r=[b_Eb[sl], b_Etmp[0]], w=[b_Eb[sl]])
                        P.op("act", lambda e, sl=sl: e.activation(out=Ebb[:, sl].rearrange("p g k c -> p (g k c)"),
                                                                  in_=Eb[:, sl].rearrange("p g k c -> p (g k c)"), func=AF.Copy),
                             r=[b_Eb[sl], bq["phi"]], w=[b_Ebb[sl]])

                def sB2(bt):
                        gs = slice(bt * 8, (bt + 1) * 8)
                        sl = bt % 2
                        for g4 in range(2):
                            bk = 4 + g4
                            for gg in range(4):
                                g8 = g4 * 4 + gg
                                P.op("pe", lambda e, g8=g8, gg=gg, bk=bk, sl=sl: e.transpose(
                                    out=banks[bk][:, gg * 128:(gg + 1) * 128], in_=Eb[:, sl, g8, 0:8, :].rearrange("p k c -> p (k c)"),
                                    identity=identf[:]), r=[b_Eb[sl], b_ident], w=[b_bank[bk]])
                            bv = banks[bk][:].rearrange("p (g c) -> p g c", g=4)
                            P.op("act", lambda e, g4=g4, bv=bv, sl=sl: e.activation(out=opstg[:, sl, 0, g4 * 4:(g4 + 1) * 4, :], in_=bv, func=AF.Copy),
                                 r=[b_bank[bk]], w=[b_stage[sl]])
                            P.op("act", lambda e, g4=g4, bv=bv, sl=sl: e.activation(out=opstg[:, sl, 1, g4 * 4:(g4 + 1) * 4, 0:64], in_=bv[:, :, 64:128], func=AF.Copy),
                                 r=[b_bank[bk]], w=[b_stage[sl]])
                            P.op("act", lambda e, g4=g4, bv=bv, sl=sl: e.activation(out=opstg[:, sl, 1, g4 * 4:(g4 + 1) * 4, 64:128], in_=bv[:, :, 0:64], func=AF.Copy, scale=-1.0),
                                 r=[b_bank[bk]], w=[b_stage[sl]])
                        for g8 in range(8):
                            g = bt * 8 + g8
                            bk2 = 6 + (g8 % 2)
                            for r_ in range(8):
                                P.op("pe", lambda e, g8=g8, r_=r_, g=g, bk2=bk2, sl=sl: e.matmul(
                                    banks[bk2][:, r_ * 16:(r_ + 1) * 16], Ebb[:, sl, g8, 7 - r_:15 - r_, :].rearrange("p k c -> p (k c)"),
                                    CAb[:, g, :], start=True, stop=True), r=[b_Ebb[sl], bq["CA"]], w=[b_bank[bk2]])
                            P.op("dve", lambda e, g8=g8, g=g, bk2=bk2, sl=sl: e.scalar_tensor_tensor(
                                out=opstg[:, sl, 2, g8, :], in0=identf[:], scalar=Dcol[:, g:g + 1], in1=banks[bk2][:, 0:128],
                                op0=ALU.mult, op1=ALU.add), r=[b_bank[bk2], bq["Dcol"], b_ident], w=[b_stage[sl]])
                        ca_b = CA[:, gs, :].unsqueeze(2).to_broadcast([128, 8, 8, 16])
                        cb_b = CB[:, gs, :].unsqueeze(2).to_broadcast([128, 8, 8, 16])
                        lr_b = Lt[:, 0, 1:9, gs].rearrange("p k g -> p g k").unsqueeze(3).to_broadcast([128, 8, 8, 16])
                        li_b = Lt[:, 1, 1:9, gs].rearrange("p k g -> p g k").unsqueeze(3).to_broadcast([128, 8, 8, 16])
                        P.op("pool", lambda e, sl=sl, ca_b=ca_b, lr_b=lr_b: e.tensor_tensor(out=Cp[:, 0], in0=ca_b, in1=lr_b, op=ALU.mult),
                             r=[bq["CA"], bq["L"], bq["nq"]], w=[b_Cp[0]])
                        P.op("pool", lambda e, sl=sl, cb_b=cb_b, li_b=li_b: e.tensor_tensor(out=Ctmp[:, 0], in0=cb_b, in1=li_b, op=ALU.mult),
                             r=[bq["CB"], bq["L"], bq["nq"]], w=[b_Ctmp[0]])
                        P.op("pool", lambda e, sl=sl: e.tensor_tensor(
                            out=opstg[:, sl, 3].rearrange("p g (r c) -> p g r c", r=8), in0=Cp[:, 0], in1=Ctmp[:, 0], op=ALU.add),
                            r=[b_Cp[0], b_Ctmp[0]], w=[b_stage[sl]])
                        P.dma("sp", lambda e, bt=bt, sl=sl: e.dma_start(out=s5ops[bt], in_=opstg[:, sl].rearrange("p k g c -> p (k g c)")),
                              r=[b_stage[sl]], w=[b_wscr], key=f"q_so{sl}")

                sB1(0)
                for bt in range(8):
                    if bt + 1 < 8:
                        sB1(bt + 1)
                    sB2(bt)
            P.full_barrier()
        P.enabled = stage >= 9

        xres = sb("xres", [128, 1, 2, D])
        b_xres = [[P.buf(f"xres{s}{q}") for q in range(2)] for s in range(1)]
        xn = sb("xn", [128, 2, D], BF16); b_xn = P.bufs(2, "xn")
        junk = sb("junk", [128, D], BF16); b_junk = P.buf("junk")
        ssb = sb("ssb", [128, 16]); b_ss = P.bufs(16, "ss")
        hT0 = sb("hT0", [128, 1, NDT, 368], BF16); b_hT0 = P.bufs(1, "hT0")
        hist0 = sb("hist0", [128, NDT, HP], BF16); b_hist0 = P.buf("hist0")
        h2T = sb("h2T", [128, 1, NDT, HC + LB], BF16); b_h2T = P.bufs(1, "h2T")
        hist2 = sb("hist2", [128, 2, NDT, HC], BF16); b_hist2 = P.bufs(2, "hist2")
        pooled = sb("pooled", [128, NDT, LB], BF16); b_pooled = P.buf("pooled")
        pt = sb("pt", [128, 2, 2, 368]); b_pt = P.bufs(2, "pt")
        tmp_tok = sb("tmp_tok", [128, 2, D]); b_tmp = P.bufs(2, "tmp")
        NS = 3
        wus = sb("wus", [128, NS, NDT, 2, 256], BF16); wds = sb("wds", [128, NS, 2, D], BF16)
        b_ws = P.bufs(NS, "ws")
        cg = sb("cg", [128, 2, LB]); cv = sb("cv", [128, 2, LB]); b_cg = P.bufs(2, "cg"); b_cv = P.bufs(2, "cv")
        gl = sb("gl", [128, 2, LB]); b_gl = P.bufs(2, "gl")
        gv = sb("gv", [128, 3, LB], BF16); b_gv = P.bufs(3, "gv")
        cso = sb("cso", [32, 512]); b_cso = P.buf("cso")

        P.op("dve", lambda e: e.memset(ssb[:], 0.0), w=b_ss)

        for b in (b_ident, b_fmp, b_Wp, b_modp, b_mods, b_Gp, b_Gs, b_inv15, b_stT):
            b.const = True

        ss_ctr = [0]

        def new_ss():
            i = ss_ctr[0] % 16
            ss_ctr[0] += 1
            return i

        def rms_stat(src_ap, r_bufs):
            i = new_ss()
            P.op("act", lambda e, i=i, src_ap=src_ap: e.activation(out=junk[:], in_=src_ap, func=AF.Square,
                                                                   accum_out=ssb[:, i:i + 1]),
                 r=list(r_bufs) + [b_ss[i]], w=[b_junk, b_ss[i]])
            P.op("act", lambda e, i=i: e.activation(out=ssb[:, i:i + 1], in_=ssb[:, i:i + 1], func=AF.Sqrt,
                                                    scale=1.0 / D, bias=epsc[:, 0:1]), r=[b_ss[i]], w=[b_ss[i]])
            P.op("dve", lambda e, i=i: e.reciprocal(out=ssb[:, i:i + 1], in_=ssb[:, i:i + 1]), r=[b_ss[i]], w=[b_ss[i]])
            return i

        def tok_cols(view3, nseg, q, H, L):
            if nseg == 1:
                return view3[:, 0, H + q * 128:H + (q + 1) * 128]
            return view3[:, :, H:H + L]

        def prenorm(blk, slot, q, l, which, dst_tile, dst_buf, H, W):
            nseg, L = blk["nseg"], blk["L"]
            xr = xres[:, slot, q, :]
            i = rms_stat(xr, [b_xres[slot][q]])
            xs_ = q % 2
            P.op("dve", lambda e, i=i, xs_=xs_, xr=xr: e.tensor_scalar(out=xn[:, xs_, :], in0=xr, scalar1=ssb[:, i:i + 1],
                                                                       scalar2=None, op0=ALU.mult),
                 r=[b_xres[slot][q], b_ss[i]], w=[b_xn[xs_]])
            bk = 6 + (q % 2)
            for dt in range(NDT):
                P.op("pe", lambda e, dt=dt, xs_=xs_, bk=bk: e.transpose(
                    out=bank_bf(bk)[:, dt * 128:(dt + 1) * 128], in_=xn[:, xs_, dt * 128:(dt + 1) * 128],
                    identity=identb[:]), r=[b_xn[xs_], b_ident], w=[b_bank[bk]])
            a_slot, b_slot = (0, 1) if which == 0 else (2, 3)
            if blk["kind"] == "p":
                for dt in range(NDT):
                    dstv = seg3(dst_tile[:, dt, 0:nseg * W], nseg)
                    P.op("act", lambda e, dt=dt, bk=bk, dstv=dstv, l=l, a_slot=a_slot, b_slot=b_slot: e.activation(
                        out=tok_cols(dstv, nseg, q, H, L), in_=bank_bf(bk)[:, dt * 128:(dt + 1) * 128], func=AF.Identity,
                        scale=modp[:, l, a_slot, dt:dt + 1], bias=modp[:, l, b_slot, dt:dt + 1]),
                        r=[b_bank[bk], b_modp], w=[dst_buf])
                return i
            else:
                am = mods[:, l, a_slot, :, :].unsqueeze(3).to_broadcast([128, NDT, NSB, LS])
                bm = mods[:, l, b_slot, :, :].unsqueeze(3).to_broadcast([128, NDT, NSB, LS])
                P.op("dve", lambda e, bk=bk, am=am: e.tensor_tensor(
                    out=tmp_tok[:, 0, :].rearrange("p (k s c) -> p k s c", k=NDT, s=NSB),
                    in0=bank_bf(bk).rearrange("p (k s c) -> p k s c", k=NDT, s=NSB), in1=am, op=ALU.mult),
                    r=[b_bank[bk], b_mods], w=[b_tmp[0]])
                dstv = seg4(dst_tile[:, :, 0:nseg * W], nseg)[:, :, :, H:H + L]
                P.op("dve", lambda e, bm=bm, dstv=dstv: e.tensor_tensor(
                    out=dstv, in0=tmp_tok[:, 0, :].rearrange("p (k s c) -> p k s c", k=NDT, s=NSB),
                    in1=bm, op=ALU.add), r=[b_tmp[0], b_mods], w=[dst_buf])
            return i

        def resid_update(blk, slot, q, acc_banks, l, which, srcs=None, final_out=False):
            full = None
            if srcs is None:
                srcs = [(banks[acc_banks[0]][:], b_bank[acc_banks[0]]), (banks[acc_banks[1]][:], b_bank[acc_banks[1]])]
                if acc_banks[1] == acc_banks[0] + 1:
                    full = (psall[:, acc_banks[0]:acc_banks[0] + 2, :], [b_bank[acc_banks[0]], b_bank[acc_banks[1]]])
            elif srcs == "mglu":
                full = (mglu[:, q, :].rearrange("p (a c) -> p a c", a=2), [b_mglu[q]])
            Gt = Gp if blk["kind"] == "p" else Gs
            bG = b_Gp if blk["kind"] == "p" else b_Gs
            i0 = None
            i = new_ss()
            if full is not None:
                fap, fbufs = full
                P.op("act", lambda e, fap=fap, i=i: e.activation(out=junk[:].rearrange("p (a c) -> p a c", a=2), in_=fap, func=AF.Square,
                                                                 accum_out=ssb[:, i:i + 1]),
                     r=list(fbufs) + [b_ss[i]], w=[b_junk, b_ss[i]])
            else:
                j = new_ss()
                for half, col in ((0, i), (1, j)):
                    sap, sbuf_ = srcs[half]
                    P.op("act", lambda e, sap=sap, col=col: e.activation(out=junk[:, 0:512], in_=sap, func=AF.Square,
                                                                         accum_out=ssb[:, col:col + 1]),
                         r=[sbuf_, b_ss[col]], w=[b_junk, b_ss[col]])
                P.op("dve", lambda e, i=i, j=j: e.tensor_tensor(out=ssb[:, i:i + 1], in0=ssb[:, i:i + 1], in1=ssb[:, j:j + 1],
                                                                op=ALU.add), r=[b_ss[i], b_ss[j]], w=[b_ss[i]])
            P.op("act", lambda e, i=i: e.activation(out=ssb[:, i:i + 1], in_=ssb[:, i:i + 1], func=AF.Sqrt,
                                                    scale=1.0 / D, bias=epsc[:, 0:1]), r=[b_ss[i]], w=[b_ss[i]])
            P.op("dve", lambda e, i=i: e.reciprocal(out=ssb[:, i:i + 1], in_=ssb[:, i:i + 1]), r=[b_ss[i]], w=[b_ss[i]])
            ts = q % 2
            if full is not None:
                fap, fbufs = full
                P.op("dve", lambda e, fap=fap, i=i, ts=ts, Gt=Gt: e.scalar_tensor_tensor(
                    out=tmp_tok[:, ts, :].rearrange("p (a c) -> p a c", a=2), in0=fap, scalar=ssb[:, i:i + 1],
                    in1=Gt[:, l, which, :].rearrange("p (a c) -> p a c", a=2), op0=ALU.mult, op1=ALU.mult),
                    r=list(fbufs) + [b_ss[i], bG], w=[b_tmp[ts]])
            for half in range(2 if full is None else 0):
                sap, sbuf_ = srcs[half]
                P.op("dve", lambda e, sap=sap, half=half, i=i, ts=ts, Gt=Gt: e.scalar_tensor_tensor(
                    out=tmp_tok[:, ts, half * 512:(half + 1) * 512], in0=sap, scalar=ssb[:, i:i + 1],
                    in1=Gt[:, l, which, half * 512:(half + 1) * 512], op0=ALU.mult, op1=ALU.mult),
                    r=[sbuf_, b_ss[i], bG], w=[b_tmp[ts]])
            if final_out:
                P.op("dve", lambda e, ts=ts: e.tensor_tensor(out=tmp_tok[:, ts, :], in0=xres[:, slot, q, :], in1=tmp_tok[:, ts, :],
                                                             op=ALU.add), r=[b_tmp[ts], b_xres[slot][q]], w=[b_tmp[ts]])
            else:
                P.op("dve", lambda e, ts=ts: e.tensor_tensor(out=xres[:, slot, q, :], in0=xres[:, slot, q, :], in1=tmp_tok[:, ts, :],
                                                             op=ALU.add), r=[b_tmp[ts], b_xres[slot][q]], w=[b_xres[slot][q]])

        wsteps = []
        wstate = {"issued": 0, "used": 0}

        def issue_weights(upto):
            while wstate["issued"] < min(upto, len(wsteps)):
                n = wstate["issued"]
                l, i = wsteps[n]
                s = n % NS
                P.dma("sp", lambda e, l=l, i=i, s=s: e.dma_start(out=wus[:, s].rearrange("p k h c -> p (k h c)"), in_=wu_scr[l, i]),
                      r=[b_wscr], w=[b_ws[s]], key=f"ws{s}")
                P.dma("sp", lambda e, l=l, i=i, s=s: e.dma_start(out=wds[:, s].rearrange("p j d -> p (j d)"), in_=wd_scr[l, i]),
                      r=[b_wscr], w=[b_ws[s]], key=f"ws{s}")
                wstate["issued"] += 1

        blocks = []
        if do_sample:
            blocks.append(dict(kind="s", nseg=NSB, L=LS, nt=1, pb=0))
        for pb in range(n_pblocks):
            blocks.append(dict(kind="p", nseg=1, L=LB, nt=2, pb=pb))
        for blk in blocks:
            for l in range(nlayers):
                for i in range(NSS):
                    wsteps.append((l, i))

        hT0_pp = [0]
        h2T_pp = [0]

        def ffn(blk, slot, l):
            nseg, L, nt = blk["nseg"], blk["L"], blk["nt"]
            W = HC + L
            ncols = nseg * W
            cur = 0
            h2 = h2T[:, cur]
            if blk["kind"] == "p":
                if blk["pb"] == 0:
                    P.op("pool", lambda e, h2=h2: e.memset(h2[:, :, 0:HC], 0.0), w=[b_h2T[cur]])
                else:
                    P.op("pool", lambda e, h2=h2: e.tensor_copy(out=h2[:, :, 0:HC], in_=hist2[:, l]),
                         r=[b_hist2[l]], w=[b_h2T[cur]])
            if blk["kind"] == "s":
                P.op("pool", lambda e, h2=h2: e.memset(h2[:, :, 0:ncols], 0.0), w=[b_h2T[cur]])
            for q in range(nt):
                prenorm(blk, slot, q, l, 1, h2, b_h2T[cur], HC, W)
            if blk["kind"] == "p":
                P.op("pool", lambda e, h2=h2: e.tensor_copy(out=hist2[:, l], in_=h2[:, :, LB:LB + HC]),
                     r=[b_h2T[cur]], w=[b_hist2[l]])
            last_p = blk["kind"] == "p" and blk["pb"] == n_pblocks - 1

            def up_mm(i, half, bk, s):
                j = i % 2
                for kt in range(NDT):
                    outp = banks[bk][:, 0:ncols]
                    rhs = h2[:, kt, 0:ncols]
                    P.op("pe", lambda e, kt=kt, half=half, s=s, outp=outp, rhs=rhs, j=j: e.matmul(
                        outp, wus[:, s, kt, half, j * 128:(j + 1) * 128], rhs, start=(kt == 0), stop=(kt == NDT - 1)),
                        r=[b_ws[s], b_h2T[cur]], w=[b_bank[bk]])

            def elementwise(i, bkg, bkv):
                ps_ = i % 2
                for half, bk, ct, bct in ((0, bkg, cg, b_cg), (1, bkv, cv, b_cv)):
                    ft = i + half * NGT
                    upv = seg3(banks[bk][:, 0:ncols], nseg)
                    if blk["kind"] == "s":
                        P.op("act", lambda e, upv=upv, ft=ft: e.activation(out=upv[:, :, 0:HC],
                                                                           in_=seg3(stT[:, l, ft, :], nseg), func=AF.Copy),
                             r=[b_stT, b_bank[bk]], w=[b_bank[bk]])
                    cvw = seg3(ct[:, ps_, 0:nseg * L], nseg)
                    wc = [fmp[:, FM_CW + (l * 3 + k) * NFT + ft:FM_CW + (l * 3 + k) * NFT + ft + 1] for k in range(3)]
                    bcol = fmp[:, FM_CB + l * NFT + ft:FM_CB + l * NFT + ft + 1]
                    P.op("act", lambda e, upv=upv, cvw=cvw, wc=wc, bcol=bcol: e.activation(
                        out=cvw, in_=upv[:, :, 2:W], func=AF.Identity, scale=wc[2], bias=bcol),
                        r=[b_bank[bk], b_fmp], w=[bct[ps_]])
                    P.op("dve", lambda e, upv=upv, cvw=cvw, wc=wc: e.scalar_tensor_tensor(
                        out=cvw, in0=upv[:, :, 1:W - 1], scalar=wc[1], in1=cvw, op0=ALU.mult, op1=ALU.add),
                        r=[b_bank[bk], b_fmp, bct[ps_]], w=[bct[ps_]])
                    P.op("dve", lambda e, upv=upv, cvw=cvw, wc=wc: e.scalar_tensor_tensor(
                        out=cvw, in0=upv[:, :, 0:W - 2], scalar=wc[0], in1=cvw, op0=ALU.mult, op1=ALU.add),
                        r=[b_bank[bk], b_fmp, bct[ps_]], w=[bct[ps_]])
                    if blk["kind"] == "s" or last_p:
                        dstc = seg3(cstage[:, 0, ft, 0:nseg * HC], nseg)
                        P.op("act", lambda e, upv=upv, dstc=dstc: e.activation(out=dstc, in_=upv[:, :, W - HC:W], func=AF.Copy),
                             r=[b_bank[bk]], w=[b_cstage])
                P.op("act", lambda e, ps_=ps_: e.activation(out=gl[:, ps_, 0:nseg * L], in_=cg[:, ps_, 0:nseg * L], func=AF.Gelu),
                     r=[b_cg[ps_]], w=[b_gl[ps_]])
                g3 = i % 3
                P.op("dve", lambda e, ps_=ps_, g3=g3: e.tensor_tensor(out=gv[:, g3, 0:nseg * L], in0=gl[:, ps_, 0:nseg * L],
                                                                       in1=cv[:, ps_, 0:nseg * L], op=ALU.mult),
                     r=[b_gl[ps_], b_cv[ps_]], w=[b_gv[g3]])

            def down_mm(i, s):
                g3 = i % 3
                j = i % 2
                for q in range(nt):
                    for half in range(2):
                        bk = 2 * q + half
                        P.op("pe", lambda e, q=q, half=half, bk=bk, s=s, g3=g3, j=j: e.matmul(
                            banks[bk][:], gv[:, g3, q * 128:(q + 1) * 128], wds[:, s, j, half * 512:(half + 1) * 512],
                            start=(i == 0), stop=(i == NGT - 1)), r=[b_gv[g3], b_ws[s]], w=[b_bank[bk]])

            base = wstate["used"]
            issue_weights(base + NS - 1)
            pend = None
            for i in range(NGT):
                n = base + i // 2
                s = n % NS
                bkg, bkv = (4, 5) if i % 2 == 0 else (6, 7)
                up_mm(i, 0, bkg, s)
                up_mm(i, 1, bkv, s)
                elementwise(i, bkg, bkv)
                if pend is not None:
                    down_mm(*pend)
                    issue_weights(base + i // 2 + NS - 1)
                pend = (i, s)
            down_mm(*pend)
            wstate["used"] = base + NSS
            issue_weights(wstate["used"] + NS - 1)
            for q in range(nt):
                resid_update(blk, slot, q, (2 * q, 2 * q + 1), l, 1, final_out=(l == nlayers - 1))

        def pool_mixer(blk, slot, l):
            nseg, L, nt = blk["nseg"], blk["L"], blk["nt"]
            W = HP + L
            ncols = nseg * W
            cur = 0
            hT = hT0[:, cur]
            if blk["kind"] == "p":
                if blk["pb"] == 0:
                    P.op("pool", lambda e, hT=hT: e.memset(hT[:, :, 0:HP], 0.0), w=[b_hT0[cur]])
                else:
                    P.op("pool", lambda e, hT=hT: e.tensor_copy(out=hT[:, :, 0:HP], in_=hist0[:]),
                         r=[b_hist0], w=[b_hT0[cur]])
            else:
                for hb in range(2):
                    P.dma("sp", lambda e, hb=hb: e.dma_start(out=tmp_tok[0:120, hb, :],
                                                             in_=spool[hb * 8:(hb + 1) * 8].rearrange("b k d -> (b k) d")),
                          w=[b_tmp[hb]], key=f"tmp{hb}")
                    P.op("dve", lambda e, hb=hb: e.tensor_copy(out=xn[0:120, hb, :], in_=tmp_tok[0:120, hb, :]),
                         r=[b_tmp[hb]], w=[b_xn[hb]])
                    bk = 6 + hb
                    for dt in range(NDT):
                        P.op("pe", lambda e, hb=hb, dt=dt, bk=bk: e.transpose(
                            out=bank_bf(bk)[:, dt * 128:dt * 128 + 120], in_=xn[0:120, hb, dt * 128:(dt + 1) * 128],
                            identity=identb[0:120, 0:120]), r=[b_xn[hb], b_ident], w=[b_bank[bk]])
                    dstv = seg4(hT[:, :, 0:ncols], nseg)[:, :, hb * 8:(hb + 1) * 8, 0:HP]
                    P.op("act", lambda e, bk=bk, dstv=dstv: e.activation(
                        out=dstv, in_=bank_bf(bk).rearrange("p (k c) -> p k c", k=NDT)[:, :, 0:120].rearrange(
                            "p k (b c) -> p k b c", b=8), func=AF.Copy), r=[b_bank[bk]], w=[b_hT0[cur]])
            for q in range(nt):
                i_ss = prenorm(blk, slot, q, l, 0, hT, b_hT0[cur], HP, W)
                if blk["kind"] == "s" or (blk["pb"] == n_pblocks - 1 and q == nt - 1):
                    new_pool_rows(blk, slot, q, l, i_ss)
            if blk["kind"] == "p":
                P.op("pool", lambda e, hT=hT: e.tensor_copy(out=hist0[:], in_=hT[:, :, LB:LB + HP]),
                     r=[b_hT0[cur]], w=[b_hist0])
            for gi, w_ in enumerate((2, 4, 8, 16)):
                src = seg4(hT[:, 2 * gi:2 * gi + 2, 0:ncols], nseg)
                ta = seg4(pt[:, 0, :, 0:ncols], nseg)
                tb = seg4(pt[:, 1, :, 0:ncols], nseg)
                P.op("dve", lambda e, src=src, ta=ta: e.tensor_tensor(out=ta[:, :, :, 1:W], in0=src[:, :, :, 1:W],
                                                                       in1=src[:, :, :, 0:W - 1], op=ALU.add),
                     r=[b_hT0[cur]], w=[b_pt[0]])
                curt, curb, oth, othb = ta, b_pt[0], tb, b_pt[1]
                lo = 1
                sh = 2
                while sh < w_:
                    P.op("dve", lambda e, curt=curt, oth=oth, lo=lo, sh=sh: e.tensor_tensor(
                        out=oth[:, :, :, lo + sh:W], in0=curt[:, :, :, lo + sh:W], in1=curt[:, :, :, lo:W - sh], op=ALU.add),
                        r=[curb], w=[othb])
                    curt, curb, oth, othb = oth, othb, curt, curb
                    lo += sh
                    sh *= 2
                pv = seg4(pooled[:, 2 * gi:2 * gi + 2, 0:nseg * L], nseg)
                P.op("dve", lambda e, curt=curt, src=src, pv=pv, w_=w_: e.scalar_tensor_tensor(
                    out=pv, in0=curt[:, :, :, HP:W], scalar=1.0 / w_, in1=src[:, :, :, HP:W], op0=ALU.mult, op1=ALU.subtract),
                    r=[curb, b_hT0[cur]], w=[b_pooled])
                if blk["kind"] == "p" and blk["pb"] == 0:
                    for d2 in range(2):
                        P.op("dve", lambda e, curt=curt, gi=gi, d2=d2: e.tensor_tensor(
                            out=pt[:, 0, d2, 0:HP], in0=curt[:, d2, 0, HP:2 * HP], in1=inv15[:, gi, :], op=ALU.mult),
                            r=[curb, b_inv15], w=[b_pt[0]] if curb is not b_pt[0] else [b_pt[0]])
                        P.op("dve", lambda e, gi=gi, d2=d2, src=src: e.tensor_tensor(
                            out=pooled[:, 2 * gi + d2, 0:HP], in0=pt[:, 0, d2, 0:HP], in1=src[:, d2, 0, HP:2 * HP], op=ALU.subtract),
                            r=[b_pt[0], b_hT0[cur]], w=[b_pooled])
            for q in range(nt):
                for gi in range(4):
                    bk = 2 * q + gi // 2
                    for kt in range(2):
                        pv = seg3(pooled[:, 2 * gi + kt, 0:nseg * L], nseg)
                        lhsT = tok_cols(pv, nseg, q, 0, L)
                        P.op("pe", lambda e, gi=gi, kt=kt, bk=bk, lhsT=lhsT: e.matmul(
                            banks[bk][:, (gi % 2) * 256:(gi % 2 + 1) * 256], lhsT, Wp[:, gi, kt, :],
                            start=(kt == 0), stop=(kt == 1)), r=[b_pooled, b_Wp], w=[b_bank[bk]])
                resid_update(blk, slot, q, (2 * q, 2 * q + 1), l, 0)


        def new_pool_rows(blk, slot, q, l, i):
            isp = blk["kind"] == "p"
            xr = xres[:, slot, q, :]
            P.op("dve", lambda e, xr=xr, i=i: e.tensor_scalar(out=tmp_tok[:, 0, :], in0=xr, scalar1=ssb[:, i:i + 1], scalar2=None, op0=ALU.mult),
                 r=[b_xres[slot][q], b_ss[i]], w=[b_tmp[0]])
            for dt in range(NDT):
                bk = 4 + dt // 4
                P.op("pe", lambda e, dt=dt, bk=bk: e.transpose(out=banks[bk][:, (dt % 4) * 128:(dt % 4 + 1) * 128],
                                                               in_=tmp_tok[:, 0, dt * 128:(dt + 1) * 128], identity=identf[:]),
                     r=[b_tmp[0], b_ident], w=[b_bank[bk]])
            hTf = tmp_tok[:, 1, :].rearrange("p (k c) -> p k c", k=NDT)
            if isp:
                for dt in range(NDT):
                    bk = 4 + dt // 4
                    P.op("act", lambda e, dt=dt, bk=bk: e.activation(
                        out=hTf[:, dt, :], in_=banks[bk][:, (dt % 4) * 128:(dt % 4 + 1) * 128], func=AF.Identity,
                        scale=modp[:, l, 0, dt:dt + 1], bias=modp[:, l, 1, dt:dt + 1]), r=[b_bank[bk], b_modp], w=[b_tmp[1]])
            else:
                am = mods[:, l, 0, :, :].unsqueeze(3).to_broadcast([128, NDT, NSB, LS])
                bm = mods[:, l, 1, :, :].unsqueeze(3).to_broadcast([128, NDT, NSB, LS])
                for h4 in range(2):
                    bk = 4 + h4
                    hv = tmp_tok[:, 1, h4 * 512:(h4 + 1) * 512].rearrange("p (k s c) -> p k s c", k=4, s=NSB)
                    P.op("dve", lambda e, bk=bk, hv=hv, am=am, h4=h4: e.tensor_tensor(
                        out=hv, in0=banks[bk][:].rearrange("p (k s c) -> p k s c", k=4, s=NSB), in1=am[:, h4 * 4:(h4 + 1) * 4], op=ALU.mult),
                        r=[b_bank[bk], b_mods], w=[b_tmp[1]])
                    P.op("dve", lambda e, hv=hv, bm=bm, h4=h4: e.tensor_tensor(out=hv, in0=hv, in1=bm[:, h4 * 4:(h4 + 1) * 4], op=ALU.add),
                         r=[b_tmp[1], b_mods], w=[b_tmp[1]])
            for dt in range(NDT):
                bk = 6 + dt // 4
                P.op("pe", lambda e, dt=dt, bk=bk: e.transpose(out=banks[bk][:, (dt % 4) * 128:(dt % 4 + 1) * 128],
                                                               in_=hTf[:, dt, :], identity=identf[:]),
                     r=[b_tmp[1], b_ident], w=[b_bank[bk]])
            for h4 in range(2):
                P.op("act", lambda e, h4=h4: e.activation(out=tmp_tok[:, 0, h4 * 512:(h4 + 1) * 512], in_=banks[6 + h4][:], func=AF.Copy),
                     r=[b_bank[6 + h4]], w=[b_tmp[0]])
            if isp:
                P.dma("sp", lambda e: e.dma_start(out=npp[:, :], in_=tmp_tok[128 - HP:128, 0, :]), r=[b_tmp[0]], key="tmp0", store=True)
            else:
                for b_ in range(NSB):
                    P.dma("sp", lambda e, b_=b_: e.dma_start(out=nps[b_, HP - LS:HP, :], in_=tmp_tok[b_ * LS:(b_ + 1) * LS, 0, :]),
                          r=[b_tmp[0]], key="tmp0", store=True)
                P.dma("sp", lambda e: e.dma_start(out=nps[:, 0:HP - LS, :], in_=spool[:, LS:HP, :]), key="npsd", store=True)

        def conv_out(blk, l):
            isp = blk["kind"] == "p"
            ncol = 2 if isp else NSB * HC
            dst = ncp[l] if isp else ncs[l].rearrange("b k f -> (b k) f")
            for f0 in range(0, NFT, 4):
                for j in range(4):
                    P.op("pe", lambda e, f0=f0, j=j: e.transpose(out=banks[4][0:ncol, j * 128:(j + 1) * 128],
                                                                 in_=cstage[:, 0, f0 + j, 0:ncol], identity=identf[:]),
                         r=[b_cstage, b_ident], w=[b_bank[4]])
                P.op("act", lambda e: e.activation(out=cso[0:ncol, :], in_=banks[4][0:ncol, :], func=AF.Copy), r=[b_bank[4]], w=[b_cso])
                P.dma("sp", lambda e, f0=f0, dst=dst: e.dma_start(out=dst[:, f0 * 128:(f0 + 4) * 128], in_=cso[0:ncol, :]),
                      r=[b_cso], key="cso", store=True)

        if not skip_s5 and nlayers > 1:
            hT1 = sb("hT1", [128, NDT, LB], BF16); b_hT1 = P.buf("hT1")
            opsb = sb("opsb", [128, 2, 4, 8, 128], BF16); tabs = sb("tabs", [128, 2, 3 * 8 * 33]); b_opsb = P.bufs(2, "opsb")
            h_cm = sb("h_cm", [32, 8, 128], BF16); b_hcm = P.buf("hcm")
            Ub = sb("Ub", [128, 2, 8, 32], BF16); b_U = P.bufs(2, "U")
            s5t = sb("s5t", [128, 4, 264]); b_s5t = P.bufs(4, "s5t")
            Xb = sb("Xb", [128, 264], BF16); b_Xb = P.buf("Xb")
            carry = sb("carry", [128, 64]); b_carry = P.buf("carry")
            x0b = sb("x0b", [128, 8, NSB]); b_x0b = P.buf("x0b")
            x0tok = sb("x0tok", [NSB, 8, 128]); b_x0tok = P.buf("x0tok")
            xfin = sb("xfin", [128, 64, NSB]); b_xfin = P.buf("xfin")
            gy = sb("gy", [128, 8 * 32], BF16); b_gy = P.buf("gy")
            gy_cm = sb("gy_cm", [32, 8, 128], BF16); b_gycm = P.buf("gycm")
            gyT = sb("gyT", [128, NDT, LB], BF16); b_gyT = P.buf("gyT")
            perm = sb("perm", [128, 128]); b_perm = P.buf("perm")
            mglu = sb("mglu", [128, 2, D]); b_mglu = P.bufs(2, "mglu")
            b_sg = b_pt
            finsb = mglu[0:64].rearrange("p q (b c) -> p (q b) c", c=128)
            P.op("dve", lambda e: e.tensor_copy(out=perm[:, 0:64], in_=identf[:, 64:128]), r=[b_ident], w=[b_perm])
            P.op("dve", lambda e: e.tensor_scalar(out=perm[:, 64:128], in0=identf[:, 0:64], scalar1=-1.0, scalar2=None, op0=ALU.mult),
                 r=[b_ident], w=[b_perm])
            P.op("pool", lambda e: e.memset(carry[:], 0.0), w=[b_carry])
            stTf = stT[:].rearrange("p l f c -> p (l f c)")
            xff = xfin[:].rearrange("p g b -> p (g b)")
            s5_opsv = [opsb[:, 0], opsb[:, 1], stTf[:, 0:2048].bitcast(BF16).rearrange("p (k g c) -> p k g c", k=4, g=8)]
            s5_tabv = [tabs[:, 0], tabs[:, 1], xff[:, 0:792]]
            s5_bops = [b_opsb[0], b_opsb[1], P.buf("opsb2")]
            s5_Uv = [Ub[:, 0], Ub[:, 1], xff[:, 792:920].bitcast(BF16).rearrange("p (g j) -> p g j", g=8)]
            s5_bU = [b_U[0], b_U[1], P.buf("U2")]
            s5_Xbv = [Xb[:], stTf[:, 2048:2180].bitcast(BF16)]
            s5_bXb = [b_Xb, P.buf("Xb1")]
            s5_bglu = P.bufs(2, "glu")

        def s5_alias_guard():
            P.op("act", lambda e: e.activation(out=epsc[:, 0:1], in_=epsc[:, 0:1], func=AF.Copy),
                 r=[b_xfin, b_stT], w=[b_xfin, s5_bops[2], s5_bU[2], s5_bXb[1]])
            P.op("dve", lambda e: e.tensor_copy(out=ssb[:, 15:16], in_=ssb[:, 15:16]), r=[b_ss[15]], w=[b_ss[15], s5_bglu[0], s5_bglu[1]])

        s5_pref = {"ops": set(), "glu": False}

        def s5_prefetch():
            n_ = 8 * 33
            for bt in (0, 1):
                ov, tv, bo = s5_opsv[bt % 3], s5_tabv[bt % 3], s5_bops[bt % 3]
                key = f"opsb{bt % 3}"
                P.dma("sp", lambda e, bt=bt, ov=ov: e.dma_start(out=ov.rearrange("p k g c -> p (k g c)"), in_=s5ops[bt]),
                      r=[b_wscr], w=[bo], key=key)
                tsrc = s5tab_p[:, :, bt * n_:(bt + 1) * n_].rearrange("k p c -> p k c")
                P.dma("sp", lambda e, tv=tv, tsrc=tsrc: e.dma_start(out=tv[:, 0:3 * n_].rearrange("p (k c) -> p k c", k=3), in_=tsrc),
                      r=[b_wscr], w=[bo], key=key)
                s5_pref["ops"].add(bt)
            for ab in range(2):
                gv_ = Gs[:].rearrange("p a b d -> p (a b d)")[:, ab * 2048:(ab + 1) * 2048].bitcast(BF16).rearrange("p (k c) -> p k c", k=NDT)
                P.dma("sp", lambda e, ab=ab, gv_=gv_: e.dma_start(
                    out=gv_, in_=wg_scr[ab].rearrange("p (kt n) -> p kt n", kt=NDT)[:, :, 0:512]),
                    r=[b_wscr], w=[s5_bglu[ab]], key="glu%d" % ab)
            s5_pref["glu"] = True

        def s5_mixer(blk, slot, l):
            nseg, L, nt = blk["nseg"], blk["L"], blk["nt"]
            isp = blk["kind"] == "p"
            ntok = nseg * L
            NCH = ntok // 8
            CW = 33 if isp else 32
            n = 8 * CW
            NR = 3 if isp else 2
            NX = 2 if isp else 1
            for q in range(nt):
                prenorm(blk, slot, q, l, 0, hT1, b_hT1, 0, L)
            if isp and blk["pb"] == 0:
                P.op("pool", lambda e: e.memset(carry[:], 0.0), r=[b_carry], w=[b_carry])

            def v3(ap):
                if isp:
                    return ap.rearrange("p (g c) -> p g c", g=8)
                return ap.rearrange("p (g b c) -> p g b c", g=8, b=NSB)

            def ops_v(bt):
                return s5_opsv[bt % NR], s5_tabv[bt % NR], s5_bops[bt % NR]

            def U_v(bt):
                return s5_Uv[bt % NR], s5_bU[bt % NR]

            def Xb_v(bt):
                return s5_Xbv[bt % NX], s5_bXb[bt % NX]

            def load_ops(bt):
                ov, tv, bo = ops_v(bt)
                key = f"opsb{bt % NR}"
                P.dma("sp", lambda e, bt=bt, ov=ov: e.dma_start(out=ov.rearrange("p k g c -> p (k g c)"), in_=s5ops[bt]),
                      r=[b_wscr], w=[bo], key=key)
                tsrc = (s5tab_p if isp else s5tab_s)[:, :, bt * n:(bt + 1) * n].rearrange("k p c -> p k c")
                P.dma("sp", lambda e, tv=tv, tsrc=tsrc: e.dma_start(out=tv[:, 0:3 * n].rearrange("p (k c) -> p k c", k=3), in_=tsrc),
                      r=[b_wscr], w=[bo], key=key)

            def s1a(bt):
                if isp and bt in s5_pref["ops"]:
                    s5_pref["ops"].discard(bt)
                else:
                    load_ops(bt)
                for r_ in range(8):
                    P.op("pe", lambda e, r_=r_, bt=bt: e.transpose(out=bank_bf(0)[0:NCH, r_ * 128:(r_ + 1) * 128],
                                                                   in_=hT1[:, bt, r_:ntok:8], identity=identb[:]),
                         r=[b_hT1, b_ident], w=[b_bank[0]])
                P.op("act", lambda e: e.activation(
                    out=h_cm[0:NCH].rearrange("p g (r c) -> p r g c", r=8),
                    in_=bank_bf(0)[0:NCH, :].rearrange("p (r g c) -> p r g c", r=8, g=8), func=AF.Copy),
                     r=[b_bank[0]], w=[b_hcm])
                for g8 in range(8):
                    P.op("pe", lambda e, g8=g8: e.transpose(out=bank_bf(1)[:, g8 * NCH:(g8 + 1) * NCH],
                                                            in_=h_cm[0:NCH, g8, :], identity=identb[0:NCH, 0:NCH]),
                         r=[b_hcm, b_ident], w=[b_bank[1]])
                Uv, bU = U_v(bt)
                P.op("act", lambda e, Uv=Uv: e.activation(out=Uv[:, :, 0:NCH], in_=bank_bf(1)[:, 0:8 * NCH].rearrange("p (g j) -> p g j", g=8),
                                                          func=AF.Copy), r=[b_bank[1]], w=[bU])

            def s1b(bt):
                ov, tv, bo = ops_v(bt)
                Uv, bU = U_v(bt)
                for kind, bk in ((0, 2), (1, 3)):
                    for g8 in range(8):
                        pv = v3(banks[bk][:, 0:n])
                        outp = pv[:, g8, 1:33] if isp else pv[:, g8, :, 1]
                        P.op("pe", lambda e, kind=kind, g8=g8, outp=outp, ov=ov, Uv=Uv: e.matmul(
                            outp, ov[:, kind, g8, :], Uv[:, g8, 0:NCH], start=True, stop=True),
                            r=[bo, bU], w=[b_bank[bk]])

            def s2(bt):
                ov, tv, bo = ops_v(bt)
                Xbv, bXb = Xb_v(bt)
                if not isp:
                    for hh, src in enumerate((sre, sim)):
                        P.dma("sp", lambda e, hh=hh, src=src, bt=bt: e.dma_start(out=x0tok[:, :, hh * 64:(hh + 1) * 64],
                                                                                 in_=src[:, bt * 8:(bt + 1) * 8, :]),
                              w=[b_x0tok], key="x0tok")
                    for g8 in range(8):
                        P.op("pe", lambda e, g8=g8: e.transpose(out=banks[7][:, g8 * NSB:(g8 + 1) * NSB], in_=x0tok[:, g8, :],
                                                                identity=identf[0:NSB, 0:NSB]), r=[b_x0tok, b_ident], w=[b_bank[7]])
                    P.op("act", lambda e: e.activation(out=x0b[:].rearrange("p g b -> p (g b)"), in_=banks[7][:, 0:8 * NSB], func=AF.Copy),
                         r=[b_bank[7]], w=[b_x0b])
                COS = tv[:, 0:n]; SIN = tv[:, n:2 * n]; M2 = tv[:, 2 * n:3 * n]
                t1 = s5t[:, 0, 0:n]; Sp = s5t[:, 1, 0:n]; V = s5t[:, 2, 0:n]; X = s5t[:, 3, 0:n]
                def nc_(ap):
                    return v3(ap)[:, :, 1:33] if isp else v3(ap)[:, :, :, 1]
                P.op("dve", lambda e, COS=COS, t1=t1: e.tensor_tensor(out=nc_(t1), in0=nc_(banks[2][:, 0:n]), in1=nc_(COS), op=ALU.mult),
                     r=[b_bank[2], bo], w=[b_s5t[0]])
                P.op("dve", lambda e, SIN=SIN, Sp=Sp: e.tensor_tensor(out=nc_(Sp), in0=nc_(banks[3][:, 0:n]), in1=nc_(SIN), op=ALU.mult),
                     r=[b_bank[3], bo], w=[b_s5t[1]])
                P.op("dve", lambda e, Sp=Sp, t1=t1: e.tensor_tensor(out=nc_(Sp), in0=nc_(Sp), in1=nc_(t1), op=ALU.add),
                     r=[b_s5t[0], b_s5t[1]], w=[b_s5t[1]])
                if isp:
                    P.op("dve", lambda e, Sp=Sp, bt=bt: e.tensor_copy(out=v3(Sp)[:, :, 0], in_=carry[:, bt * 8:(bt + 1) * 8]),
                         r=[b_carry, b_s5t[1]], w=[b_s5t[1]])
                else:
                    P.op("dve", lambda e, Sp=Sp: e.tensor_copy(out=v3(Sp)[:, :, :, 0], in_=x0b[:]),
                         r=[b_x0b, b_s5t[1]], w=[b_s5t[1]])
                P.op("dve", lambda e, M2=M2, Sp=Sp, V=V: e.tensor_tensor_scan(out=V, data0=M2, data1=Sp, initial=0.0,
                                                                             op0=ALU.mult, op1=ALU.add),
                     r=[b_s5t[1], bo], w=[b_s5t[2]])
                P.op("pe", lambda e, V=V: e.matmul(banks[4][:, 0:n], perm[:], V, start=True, stop=True),
                     r=[b_perm, b_s5t[2]], w=[b_bank[4]])
                P.op("dve", lambda e, COS=COS, t1=t1, V=V: e.tensor_tensor(out=t1, in0=V, in1=COS, op=ALU.mult),
                     r=[b_s5t[2], bo, b_s5t[0]], w=[b_s5t[0]])
                P.op("dve", lambda e, SIN=SIN, X=X: e.tensor_tensor(out=X, in0=banks[4][:, 0:n], in1=SIN, op=ALU.mult),
                     r=[b_bank[4], bo], w=[b_s5t[3]])
                P.op("dve", lambda e, X=X, t1=t1: e.tensor_tensor(out=X, in0=t1, in1=X, op=ALU.subtract),
                     r=[b_s5t[0], b_s5t[3]], w=[b_s5t[3]])
                P.op("act", lambda e, X=X, Xbv=Xbv: e.activation(out=Xbv[:, 0:n], in_=X, func=AF.Copy), r=[b_s5t[3]], w=[bXb])
                if isp:
                    P.op("pool", lambda e, X=X, bt=bt: e.tensor_copy(out=carry[:, bt * 8:(bt + 1) * 8], in_=v3(X)[:, :, 32]),
                         r=[b_s5t[3], b_carry], w=[b_carry])
                else:
                    P.op("pool", lambda e, X=X, bt=bt: e.tensor_copy(out=xfin[:, bt * 8:(bt + 1) * 8, :], in_=v3(X)[:, :, :, 1]),
                         r=[b_s5t[3], b_xfin], w=[b_xfin])

            def s3(bt):
                ov, tv, bo = ops_v(bt)
                Uv, bU = U_v(bt)
                Xbv, bXb = Xb_v(bt)
                for g8 in range(8):
                    xv = v3(Xbv[:, 0:n])
                    xprev = xv[:, g8, 0:32] if isp else xv[:, g8, :, 0]
                    P.op("pe", lambda e, g8=g8, ov=ov, Uv=Uv: e.matmul(banks[5][:, g8 * NCH:(g8 + 1) * NCH], ov[:, 2, g8, :],
                                                                       Uv[:, g8, 0:NCH], start=True, stop=False),
                         r=[bo, bU], w=[b_bank[5]])
                    P.op("pe", lambda e, g8=g8, ov=ov, xprev=xprev: e.matmul(banks[5][:, g8 * NCH:(g8 + 1) * NCH], ov[:, 3, g8, :],
                                                                             xprev, start=False, stop=True),
                         r=[bo, bXb], w=[b_bank[5]])
                P.op("act", lambda e: e.activation(out=gy[:, 0:8 * NCH], in_=banks[5][:, 0:8 * NCH], func=AF.Gelu),
                     r=[b_bank[5]], w=[b_gy])
                for g8 in range(8):
                    P.op("pe", lambda e, g8=g8: e.transpose(out=bank_bf(6)[0:NCH, g8 * 128:(g8 + 1) * 128],
                                                            in_=gy[:, g8 * NCH:(g8 + 1) * NCH], identity=identb[:]),
                         r=[b_gy, b_ident], w=[b_bank[6]])
                P.op("act", lambda e: e.activation(
                    out=gy_cm[0:NCH].rearrange("p r (g c) -> p r g c", g=8),
                    in_=bank_bf(6)[0:NCH, :].rearrange("p (g r c) -> p r g c", g=8, r=8), func=AF.Copy),
                    r=[b_bank[6]], w=[b_gycm])
                for r_ in range(8):
                    P.op("pe", lambda e, r_=r_: e.transpose(out=bank_bf(7)[:, r_ * NCH:(r_ + 1) * NCH], in_=gy_cm[0:NCH, r_, :],
                                                            identity=identb[0:NCH, 0:NCH]), r=[b_gycm, b_ident], w=[b_bank[7]])
                P.op("dve", lambda e, bt=bt: e.tensor_copy(
                    out=gyT[:, bt, 0:ntok].rearrange("p (j r) -> p r j", r=8),
                    in_=bank_bf(7)[:, 0:8 * NCH].rearrange("p (r j) -> p r j", r=8)), r=[b_bank[7]], w=[b_gyT])

            if isp:
                if not s5_pref["glu"]:
                    for ab in range(2):
                        gv_ = Gs[:].rearrange("p a b d -> p (a b d)")[:, ab * 2048:(ab + 1) * 2048].bitcast(BF16).rearrange("p (k c) -> p k c", k=NDT)
                        P.dma("sp", lambda e, ab=ab, gv_=gv_: e.dma_start(
                            out=gv_, in_=wg_scr[ab].rearrange("p (kt n) -> p kt n", kt=NDT)[:, :, 0:512]),
                            r=[b_wscr], w=[s5_bglu[ab]], key="glu%d" % ab)
                s5_pref["glu"] = False
                s1a(0)
                s1b(0)
                for k in range(8):
                    if k + 1 < 8:
                        s1a(k + 1)
                    s2(k)
                    if k + 1 < 8:
                        s1b(k + 1)
                    if k >= 1:
                        s3(k - 1)
                s3(7)
            else:
                s1a(0)
                s1b(0)
                for k in range(8):
                    if k + 1 < 8:
                        s1a(k + 1)
                    s2(k)
                    if k + 1 < 8:
                        s1b(k + 1)
                    s3(k)
            wgsrc = [wg_scr[ab].rearrange("p (kt n) -> p kt n", kt=NDT) for ab in range(2)]
            if isp:
                gviews = [Gs[:].rearrange("p a b d -> p (a b d)")[:, sl_ * 2048:(sl_ + 1) * 2048].bitcast(BF16).rearrange(
                    "p (k c) -> p k c", k=NDT) for sl_ in range(2)]
                gbufs = s5_bglu
            else:
                gsl = (wstate["used"] + NS - 1) % NS
                gviews = [wus[:, gsl].rearrange("p k h c -> p k (h c)")] * 2
                gbufs = [b_ws[gsl], b_ws[gsl]]
            for hf in range(2):
                for ab in range(2):
                    if not (isp and hf == 0):
                        P.dma("sp", lambda e, ab=ab, hf=hf: e.dma_start(out=gviews[ab], in_=wgsrc[ab][:, :, hf * 512:(hf + 1) * 512]),
                              r=[b_wscr], w=[gbufs[ab]], key=("glu%d" % ab) if isp else f"ws{gsl}")
                    for q in range(nt):
                        bk = 2 * q + ab
                        for kt in range(NDT):
                            P.op("pe", lambda e, q=q, kt=kt, bk=bk, ab=ab: e.matmul(
                                banks[bk][:], gyT[:, kt, q * 128:(q + 1) * 128], gviews[ab][:, kt, :],
                                start=(kt == 0), stop=(kt == NDT - 1)), r=[b_gyT, gbufs[ab]], w=[b_bank[bk]])
                for q in range(nt):
                    P.op("act", lambda e, q=q: e.activation(out=pt[:, q].rearrange("p a c -> p (a c)")[:, 0:512], in_=banks[2 * q + 1][:], func=AF.Sigmoid),
                         r=[b_bank[2 * q + 1]], w=[b_sg[q]])
                    P.op("dve", lambda e, q=q, hf=hf: e.tensor_tensor(out=mglu[:, q, hf * 512:(hf + 1) * 512], in0=banks[2 * q][:],
                                                                     in1=pt[:, q].rearrange("p a c -> p (a c)")[:, 0:512], op=ALU.mult),
                         r=[b_bank[2 * q], b_sg[q]], w=[b_mglu[q]])
            for q in range(nt):
                resid_update(blk, slot, q, None, l, 0, srcs="mglu")

        def s5_outputs_sample():
            for b4 in range(0, NSB, 4):
                for bb in range(4):
                    b_ = b4 + bb
                    P.op("pe", lambda e, b_=b_, bb=bb: e.transpose(out=banks[4][0:64, bb * 128:(bb + 1) * 128], in_=xfin[:, :, b_],
                                                                   identity=identf[:]), r=[b_xfin, b_ident], w=[b_bank[4]])
                P.op("act", lambda e, b4=b4: e.activation(out=finsb[:, b4:b4 + 4, :].rearrange("p b c -> p (b c)"), in_=banks[4][0:64, :],
                                                          func=AF.Copy), r=[b_bank[4]], w=[b_mglu[0], b_mglu[1]])
            P.dma("sp", lambda e: e.dma_start(out=nrs.rearrange("b g p -> g b p"), in_=finsb[:, :, 0:64]), r=[b_mglu[0], b_mglu[1]], key="finsb", store=True)
            P.dma("sp", lambda e: e.dma_start(out=nis.rearrange("b g p -> g b p"), in_=finsb[:, :, 64:128]), r=[b_mglu[0], b_mglu[1]], key="finsb", store=True)

        def s5_outputs_prompt():
            P.op("pe", lambda e: e.transpose(out=banks[4][0:64, 0:128], in_=carry[:], identity=identf[:]),
                 r=[b_carry, b_ident], w=[b_bank[4]])
            P.op("act", lambda e: e.activation(out=finsb[:, 0, :], in_=banks[4][0:64, 0:128], func=AF.Copy), r=[b_bank[4], b_mglu[0], b_mglu[1]], w=[b_mglu[0], b_mglu[1]])
            P.dma("sp", lambda e: e.dma_start(out=nrp[:, :], in_=finsb[:, 0, 0:64]), r=[b_mglu[0], b_mglu[1]], key="finsb", store=True)
            P.dma("sp", lambda e: e.dma_start(out=nip[:, :], in_=finsb[:, 0, 64:128]), r=[b_mglu[0], b_mglu[1]], key="finsb", store=True)

        slot = 0
        for bi, blk in enumerate(blocks):
            nseg, L, nt = blk["nseg"], blk["L"], blk["nt"]
            for q in range(nt):
                src = xs[:, :] if blk["kind"] == "s" else xp[blk["pb"] * LB + q * 128: blk["pb"] * LB + (q + 1) * 128, :]
                P.dma("sp", lambda e, q=q, src=src, slot=slot: e.dma_start(out=xres[:, slot, q, :], in_=src),
                      w=[b_xres[slot][q]], key=f"x{slot}{q}")
            if blk["kind"] == "p" and not skip_s5 and nlayers > 1:
                s5_prefetch()
            for l in range(nlayers):
                if l % 2 == 0:
                    pool_mixer(blk, slot, l)
                else:
                    if not skip_s5:
                        s5_mixer(blk, slot, l)
                        if blk["kind"] == "s":
                            s5_outputs_sample()
                        elif blk["pb"] == n_pblocks - 1:
                            s5_outputs_prompt()
                ffn(blk, slot, l)
                if blk["kind"] == "s" or blk["pb"] == n_pblocks - 1:
                    conv_out(blk, l)
            for q in range(nt):
                dst = ys[:, :] if blk["kind"] == "s" else yp[blk["pb"] * LB + q * 128: blk["pb"] * LB + (q + 1) * 128, :]
                P.dma("act", lambda e, q=q, dst=dst: e.dma_start(out=dst, in_=tmp_tok[:, q % 2, :]),
                      r=[b_tmp[q % 2]], key=f"tmp{q % 2}", store=True)
            if blk["kind"] == "s" and not skip_s5 and nlayers > 1:
                s5_alias_guard()
            slot = 0

        P.enabled = True
        P.barrier_wait("sp", P.stores)
        P.full_barrier()
        P.emit(nc, st)
    return nc


_NC_CACHE = {}


def kernel(**inputs):
    f32 = lambda a: np.ascontiguousarray(np.asarray(a, dtype=np.float32))
    inp = {k: f32(v) for k, v in inputs.items()}
    if "nc" not in _NC_CACHE:
        _NC_CACHE["nc"] = build_nc()
    nc = _NC_CACHE["nc"]
    shared = ["ada_w", "ada_b", "mix_pre_g", "mix_post_g", "ffn_pre_g", "ffn_post_g", "pool_w", "pool_scale",
              "ssm_A_re", "ssm_A_im", "ssm_log_dt", "ssm_B_re", "ssm_B_im", "ssm_C_re", "ssm_C_im", "ssm_D",
              "ssm_glu_a", "ssm_glu_b", "ffn_w_up", "ffn_conv_w", "ffn_conv_b", "ffn_w_down"]
    in_maps = []
    for c in range(NCORES):
        sl = slice(c * NSB, (c + 1) * NSB)
        m = {k: inp[k] for k in shared}
        m["xp"] = inp["x_prompt"][c]
        m["xs"] = inp["x_sample"][sl].reshape(128, D)
        m["cp"] = np.ascontiguousarray(np.broadcast_to(inp["c_prompt"][c][None, :], (128, D)))
        m["cs"] = np.ascontiguousarray(np.repeat(inp["c_sample"][sl], LS, axis=0))
        m["spool"] = inp["state_pool"][0, sl]
        m["sre"] = inp["state_ssm_re"][0, sl]
        m["sim"] = inp["state_ssm_im"][0, sl]
        m["sconv"] = np.ascontiguousarray(inp["state_ffn_conv"][:, sl])
        in_maps.append(m)
    res = run_bass_kernel_spmd(nc, in_maps, core_ids=list(range(NCORES)))
    R = res.results
    y_prompt = np.stack([R[c]["yp"] for c in range(NCORES)], 0)
    y_sample = np.concatenate([R[c]["ys"].reshape(NSB, LS, D) for c in range(NCORES)], 0)
    npp = np.stack([R[c]["npp"] for c in range(NCORES)], 0)[None]
    nps = np.concatenate([R[c]["nps"] for c in range(NCORES)], 0)[None]
    nrp = np.stack([R[c]["nrp"] for c in range(NCORES)], 0)[None]
    nip = np.stack([R[c]["nip"] for c in range(NCORES)], 0)[None]
    nrs = np.concatenate([R[c]["nrs"] for c in range(NCORES)], 0)[None]
    nis = np.concatenate([R[c]["nis"] for c in range(NCORES)], 0)[None]
    ncp = np.stack([R[c]["ncp"] for c in range(NCORES)], 1)
    ncs = np.concatenate([R[c]["ncs"] for c in range(NCORES)], 1)
    return (y_prompt, y_sample, npp, nps, nrp, nip, nrs, nis, ncp, ncs)
```
